# Optimizing a Trainium2 kernel written in Bass

```python
import math
import jax
import jax.numpy as jnp
from jax import lax
import numpy as np


D_MODEL = 1024
BATCH = 4
SEQ = 8192
DEPTH = 4

GRID_W = 64
CTX_LEN = 256

N_MIXERS = 2
D_FF = 4 * D_MODEL
N_MOD = 6
NORM_EPS = 1e-6

MLA_HEADS = 8
MLA_NOPE = 128
MLA_ROPE = 64
MLA_V = 128
MLA_Q_RANK = 384
MLA_KV_RANK = 256
ROPE_BASE = 10000.0
Q_BLOCK = 128

GLA_HEADS = 4
GLA_DK = D_MODEL // 2 // GLA_HEADS
GLA_DV = D_MODEL // GLA_HEADS
GLA_GATE_RANK = 16
GLA_TAU = 16.0
GLA_CHUNK = 64

kernel_name = 'hybrid_mla_gla_diffusion_trunk'


def rms_norm(x, g):
    xf = x.astype(jnp.float32)
    y = xf * lax.rsqrt(jnp.mean(xf * xf, axis=-1, keepdims=True) + NORM_EPS)
    return (y * g.astype(jnp.float32)).astype(x.dtype)


def modulate(h, shift, scale):
    return h * (1.0 + scale) + shift


def sqrelu_mlp(h, w1, w2):
    return jnp.square(jax.nn.relu(h @ w1)) @ w2


def axial_rope_tables(n_tokens):
    rows = n_tokens // GRID_W
    row = jnp.repeat(jnp.arange(rows, dtype=jnp.float32), GRID_W)
    col = jnp.tile(jnp.arange(GRID_W, dtype=jnp.float32), rows)
    half = MLA_ROPE // 2
    inv_freq = ROPE_BASE ** (-jnp.arange(0, half, 2, dtype=jnp.float32) / half)
    ang_r = row[:, None] * inv_freq
    ang_c = col[:, None] * inv_freq
    return (jnp.cos(ang_r), jnp.sin(ang_r), jnp.cos(ang_c), jnp.sin(ang_c))


def _rotate_half(x, cos, sin):
    x1, x2 = jnp.split(x, 2, axis=-1)
    return jnp.concatenate([x1 * cos - x2 * sin, x1 * sin + x2 * cos], axis=-1)


def apply_axial_rope(x, rope):
    cos_r, sin_r, cos_c, sin_c = rope
    if x.ndim == 4:
        cos_r, sin_r, cos_c, sin_c = (t[:, None, :] for t in rope)
    x_r, x_c = jnp.split(x, 2, axis=-1)
    out = jnp.concatenate([_rotate_half(x_r, cos_r, sin_r), _rotate_half(x_c, cos_c, sin_c)], axis=-1)
    return out.astype(x.dtype)


def mla_attend(q_nope, q_rope, k_nope, k_rope, v):
    scale = (MLA_NOPE + MLA_ROPE) ** -0.5
    s = (jnp.einsum('bqhd,bkhd->bhqk', q_nope, k_nope, preferred_element_type=jnp.float32)
         + jnp.einsum('bqhr,bkr->bhqk', q_rope, k_rope, preferred_element_type=jnp.float32))
    p = jax.nn.softmax(s * scale, axis=-1)
    return jnp.einsum('bhqk,bkhd->bqhd', p.astype(v.dtype), v)


def mla_mixer(h_ctx, h_lat, rope, need_ctx, w_dq, g_q, w_uq, w_dkv, g_kv, w_ukv, w_o):
    B, C, _ = h_ctx.shape
    S = h_lat.shape[1]
    T = C + S
    h_all = jnp.concatenate([h_ctx, h_lat], axis=1)
    cq = rms_norm(h_all @ w_dq, g_q)
    q = (cq @ w_uq).reshape(B, T, MLA_HEADS, MLA_NOPE + MLA_ROPE)
    ckv = h_all @ w_dkv
    c_kv = rms_norm(ckv[..., :MLA_KV_RANK], g_kv)
    kv = (c_kv @ w_ukv).reshape(B, T, MLA_HEADS, MLA_NOPE + MLA_V)
    q_nope, q_rope = q[..., :MLA_NOPE], q[..., MLA_NOPE:]
    k_nope, v = kv[..., :MLA_NOPE], kv[..., MLA_NOPE:]
    k_rope = ckv[..., MLA_KV_RANK:]
    q_rope = jnp.concatenate([q_rope[:, :C], apply_axial_rope(q_rope[:, C:], rope)], axis=1)
    k_rope = jnp.concatenate([k_rope[:, :C], apply_axial_rope(k_rope[:, C:], rope)], axis=1)

    nb = S // Q_BLOCK
    def to_blocks(a):
        return jnp.moveaxis(a.reshape(B, nb, Q_BLOCK, *a.shape[2:]), 1, 0)
    qn_blk = to_blocks(q_nope[:, C:])
    qr_blk = to_blocks(q_rope[:, C:])
    o_lat = lax.map(lambda qb: mla_attend(qb[0], qb[1], k_nope, k_rope, v), (qn_blk, qr_blk))
    o_lat = jnp.moveaxis(o_lat, 0, 1).reshape(B, S, MLA_HEADS * MLA_V)
    y_lat = o_lat @ w_o

    y_ctx = None
    if need_ctx:
        o_ctx = mla_attend(q_nope[:, :C], q_rope[:, :C], k_nope[:, :C], k_rope[:, :C], v[:, :C])
        y_ctx = o_ctx.reshape(B, C, MLA_HEADS * MLA_V) @ w_o
    return y_ctx, y_lat


def gla_log_gate(h, w1, w2, b):
    return jax.nn.log_sigmoid(((h @ w1) @ w2 + b).astype(jnp.float32)) / GLA_TAU


def gla_chunk_scan(q, k, v, g, s0):
    B, T, H, _ = q.shape
    n = T // GLA_CHUNK
    def to_chunks(a):
        return a.reshape(B, n, GLA_CHUNK, H, a.shape[-1]).transpose(1, 0, 3, 2, 4).astype(jnp.float32)
    mask = jnp.tril(jnp.ones((GLA_CHUNK, GLA_CHUNK), dtype=bool))
    def step(s, inp):
        qc, kc, vc, gc = inp
        G = jnp.cumsum(gc, axis=2)
        G_last = G[:, :, -1:, :]
        q_t = qc * jnp.exp(G)
        k_t = kc * jnp.exp(-G)
        a = jnp.where(mask, jnp.einsum('bhid,bhjd->bhij', q_t, k_t), 0.0)
        o = jnp.einsum('bhij,bhjv->bhiv', a, vc) + jnp.einsum('bhid,bhdv->bhiv', q_t, s)
        k_end = kc * jnp.exp(G_last - G)
        s_new = jnp.exp(G_last[:, :, 0, :])[..., None] * s + jnp.einsum('bhjd,bhjv->bhdv', k_end, vc)
        return s_new, o
    s_fin, o = lax.scan(step, s0, (to_chunks(q), to_chunks(k), to_chunks(v), to_chunks(g)))
    o = o.transpose(1, 0, 3, 2, 4).reshape(B, T, H, v.shape[-1])
    return o, s_fin


def gla_mixer(h_ctx, h_lat, need_ctx, w_q, w_k, w_v, w_r, w_gate1, w_gate2, b_gate, g_o, w_o):
    B, C, _ = h_ctx.shape
    S = h_lat.shape[1]
    T = C + S
    h_all = jnp.concatenate([h_ctx, h_lat], axis=1)
    def heads(a, d):
        return a.reshape(B, T, GLA_HEADS, d)
    q = heads(h_all @ w_q, GLA_DK) * (GLA_DK ** -0.5)
    k = heads(h_all @ w_k, GLA_DK)
    v = heads(h_all @ w_v, GLA_DV)
    g_fwd = heads(gla_log_gate(h_all, w_gate1[0], w_gate2[0], b_gate[0]), GLA_DK)
    g_bwd = heads(gla_log_gate(h_all, w_gate1[1], w_gate2[1], b_gate[1]), GLA_DK)
    flip = lambda a: jnp.flip(a, axis=1)
    zeros = jnp.zeros((B, GLA_HEADS, GLA_DK, GLA_DV), jnp.float32)
    qc, kc, vc = q[:, :C], k[:, :C], v[:, :C]
    ql, kl, vl = q[:, C:], k[:, C:], v[:, C:]
    o_cf, s_cf = gla_chunk_scan(qc, kc, vc, g_fwd[:, :C], zeros)
    o_cb, s_cb = gla_chunk_scan(flip(qc), flip(kc), flip(vc), flip(g_bwd[:, :C]), zeros)
    o_lf, _ = gla_chunk_scan(ql, kl, vl, g_fwd[:, C:], s_cf)
    o_lb, _ = gla_chunk_scan(flip(ql), flip(kl), flip(vl), flip(g_bwd[:, C:]), s_cb)
    def out(o, h):
        y = rms_norm(o, g_o).astype(h.dtype).reshape(B, -1, GLA_HEADS * GLA_DV)
        return (y * jax.nn.silu(h @ w_r)) @ w_o
    y_lat = out(o_lf + flip(o_lb), h_lat)
    y_ctx = out(o_cf + flip(o_cb), h_ctx) if need_ctx else None
    return y_ctx, y_lat


def setup_inputs(seed: int = 0) -> dict:
    key = jax.random.key(seed)
    ks = jax.random.split(key, 32)
    D = D_MODEL
    n_a = (DEPTH + N_MIXERS - 1) // N_MIXERS
    n_b = DEPTH // N_MIXERS
    def nrm(k, shape, scale):
        return jax.random.normal(k, shape, jnp.float32) * scale
    return {
        'x': nrm(ks[0], (BATCH, SEQ, D), 1.0),
        'c': nrm(ks[1], (BATCH, D), 1.0),
        'ctx': nrm(ks[2], (BATCH, CTX_LEN, D), 1.0),
        'c_ctx': nrm(ks[3], (D,), 1.0),
        'ada_w': nrm(ks[4], (DEPTH, D, N_MOD * D), 0.5 * D ** -0.5),
        'ada_b': nrm(ks[5], (DEPTH, N_MOD * D), 0.02),
        'norm_g': 1.0 + nrm(ks[6], (DEPTH, 4, D), 0.02),
        'mlp_w1': nrm(ks[7], (DEPTH, D, D_FF), D ** -0.5),
        'mlp_w2': nrm(ks[8], (DEPTH, D_FF, D), D_FF ** -0.5),
        'mla_w_dq': nrm(ks[9], (n_a, D, MLA_Q_RANK), D ** -0.5),
        'mla_g_q': 1.0 + nrm(ks[10], (n_a, MLA_Q_RANK), 0.02),
        'mla_w_uq': nrm(ks[11], (n_a, MLA_Q_RANK, MLA_HEADS * (MLA_NOPE + MLA_ROPE)), MLA_Q_RANK ** -0.5),
        'mla_w_dkv': nrm(ks[12], (n_a, D, MLA_KV_RANK + MLA_ROPE), D ** -0.5),
        'mla_g_kv': 1.0 + nrm(ks[13], (n_a, MLA_KV_RANK), 0.02),
        'mla_w_ukv': nrm(ks[14], (n_a, MLA_KV_RANK, MLA_HEADS * (MLA_NOPE + MLA_V)), MLA_KV_RANK ** -0.5),
        'mla_w_o': nrm(ks[15], (n_a, MLA_HEADS * MLA_V, D), (MLA_HEADS * MLA_V) ** -0.5),
        'gla_w_q': nrm(ks[16], (n_b, D, GLA_HEADS * GLA_DK), D ** -0.5),
        'gla_w_k': nrm(ks[17], (n_b, D, GLA_HEADS * GLA_DK), D ** -0.5),
        'gla_w_v': nrm(ks[18], (n_b, D, GLA_HEADS * GLA_DV), D ** -0.5),
        'gla_w_r': nrm(ks[19], (n_b, D, GLA_HEADS * GLA_DV), D ** -0.5),
        'gla_w_gate1': nrm(ks[20], (n_b, 2, D, GLA_GATE_RANK), D ** -0.5),
        'gla_w_gate2': nrm(ks[21], (n_b, 2, GLA_GATE_RANK, GLA_HEADS * GLA_DK), GLA_GATE_RANK ** -0.5),
        'gla_b_gate': nrm(ks[22], (n_b, 2, GLA_HEADS * GLA_DK), 0.1),
        'gla_g_o': 1.0 + nrm(ks[23], (n_b, GLA_DV), 0.02),
        'gla_w_o': nrm(ks[24], (n_b, GLA_HEADS * GLA_DV, D), (GLA_HEADS * GLA_DV) ** -0.5),
    }


def reference(x, c, ctx, c_ctx, ada_w, ada_b, norm_g, mlp_w1, mlp_w2,
              mla_w_dq, mla_g_q, mla_w_uq, mla_w_dkv, mla_g_kv, mla_w_ukv, mla_w_o,
              gla_w_q, gla_w_k, gla_w_v, gla_w_r, gla_w_gate1, gla_w_gate2, gla_b_gate,
              gla_g_o, gla_w_o):
    B, S, D = x.shape
    rope = axial_rope_tables(S)
    silu_c = jax.nn.silu(c)
    silu_cc = jax.nn.silu(c_ctx)
    x_lat, x_ctx = x, ctx
    for i in range(DEPTH):
        need_ctx = i < DEPTH - 1
        j = i // N_MIXERS
        m_lat = (silu_c @ ada_w[i] + ada_b[i]).reshape(B, N_MOD, 1, D)
        m_ctx = (silu_cc @ ada_w[i] + ada_b[i]).reshape(N_MOD, D)
        h_lat = modulate(rms_norm(x_lat, norm_g[i, 0]), m_lat[:, 0], m_lat[:, 1])
        h_ctx = modulate(rms_norm(x_ctx, norm_g[i, 0]), m_ctx[0], m_ctx[1])
        if i % N_MIXERS == 0:
            y_ctx, y_lat = mla_mixer(h_ctx, h_lat, rope, need_ctx, mla_w_dq[j], mla_g_q[j], mla_w_uq[j],
                                     mla_w_dkv[j], mla_g_kv[j], mla_w_ukv[j], mla_w_o[j])
        else:
            y_ctx, y_lat = gla_mixer(h_ctx, h_lat, need_ctx, gla_w_q[j], gla_w_k[j], gla_w_v[j], gla_w_r[j],
                                     gla_w_gate1[j], gla_w_gate2[j], gla_b_gate[j], gla_g_o[j], gla_w_o[j])
        x_lat = x_lat + m_lat[:, 2] * rms_norm(y_lat, norm_g[i, 1])
        h = modulate(rms_norm(x_lat, norm_g[i, 2]), m_lat[:, 3], m_lat[:, 4])
        x_lat = x_lat + m_lat[:, 5] * rms_norm(sqrelu_mlp(h, mlp_w1[i], mlp_w2[i]), norm_g[i, 3])
        if need_ctx:
            x_ctx = x_ctx + m_ctx[2] * rms_norm(y_ctx, norm_g[i, 1])
            h = modulate(rms_norm(x_ctx, norm_g[i, 2]), m_ctx[3], m_ctx[4])
            x_ctx = x_ctx + m_ctx[5] * rms_norm(sqrelu_mlp(h, mlp_w1[i], mlp_w2[i]), norm_g[i, 3])
    return x_lat
```

```python
import contextlib
import numpy as np
import ml_dtypes
import concourse.bass as bass
import concourse.mybir as mybir
from concourse.bass_utils import run_bass_kernel_spmd

F32 = mybir.dt.float32
BF16 = mybir.dt.bfloat16
AF = mybir.ActivationFunctionType
ALU = mybir.AluOpType

NDMA = 20
P = 128
D = 1024
DC = 8
DFF = 4096
CTX = 256
EPS = 1e-6
GRID_W = 64
MLA_H = 8
GLA_H = 4
ROPE_BASE = 10000.0


class Buf:
    __slots__ = ("name", "w", "r")

    def __init__(self, name):
        self.name = name
        self.w = None
        self.r = []


class FW:
    def __init__(self, nc, stack):
        self.nc = nc
        self.eng = {"pe": nc.tensor, "act": nc.scalar, "dve": nc.vector,
                    "pool": nc.gpsimd, "sp": nc.sync}
        self.sem, self.cnt, self.semobj = {}, {}, {}
        for e in self.eng:
            self.sem[e] = stack.enter_context(nc.semaphore("s_" + e))
            self.cnt[e] = 0
            self.semobj["s_" + e] = self.sem[e]
        self.dsem, self.dcnt, self.dnext = {}, {}, {}
        for q in ("sp", "pool"):
            self.dsem[q] = [stack.enter_context(nc.semaphore(f"d_{q}{i}")) for i in range(NDMA)]
            self.dcnt[q] = [0] * NDMA
            self.dnext[q] = 0
            for i, s in enumerate(self.dsem[q]):
                self.semobj[f"d_{q}{i}"] = s
        self.bar = stack.enter_context(nc.semaphore("s_bar"))
        self.semobj["s_bar"] = self.bar
        self.barcnt = 0
        self.cc = stack.enter_context(nc.semaphore("s_cc"))
        self.semobj["s_cc"] = self.cc
        self.cccnt = 0
        self.known = {e: {} for e in self.eng}
        self.nwaits = 0
        self.nins = 0

    def _wait(self, e, tok):
        if tok is None:
            return
        name, val = tok
        if self.known[e].get(name, 0) >= val:
            return
        self.eng[e].wait_ge(self.semobj[name], val)
        self.known[e][name] = val
        self.nwaits += 1

    def _deps(self, e, reads, writes):
        own = "s_" + e
        for b in reads:
            if b.w is not None and not (b.w[0] == own and e == "pe"):
                self._wait(e, b.w)
        for b in writes:
            if b.w is not None and not (b.w[0] == own and e == "pe"):
                self._wait(e, b.w)
            for t in b.r:
                if t[0] != own:
                    self._wait(e, t)

    def _record(self, tok, reads, writes):
        for b in reads:
            b.r.append(tok)
        for b in writes:
            b.w = tok
            b.r = []

    def op(self, e, fn, reads=(), writes=()):
        self._deps(e, reads, writes)
        ins = fn(self.eng[e])
        self.cnt[e] += 1
        ins.then_inc(self.sem[e], 1)
        tok = ("s_" + e, self.cnt[e])
        self._record(tok, reads, writes)
        self.nins += 1
        return tok

    def dma(self, q, out, in_, reads=(), writes=()):
        i = self.dnext[q]
        self.dnext[q] = (i + 1) % NDMA
        name = f"d_{q}{i}"
        if self.dcnt[q][i] > 0:
            self._wait(q, (name, 16 * self.dcnt[q][i]))
        self._deps(q, reads, writes)
        self.dcnt[q][i] += 1
        self.eng[q].dma_start(out=out, in_=in_).then_inc(self.dsem[q][i], 16)
        tok = (name, 16 * self.dcnt[q][i])
        self._record(tok, reads, writes)
        self.nins += 1
        return tok

    def barrier(self):
        for e in self.eng:
            if e != "sp" and self.cnt[e] > 0:
                self._wait("sp", ("s_" + e, self.cnt[e]))
        for q in self.dsem:
            for i in range(NDMA):
                if self.dcnt[q][i] > 0:
                    self._wait("sp", (f"d_{q}{i}", 16 * self.dcnt[q][i]))
        if self.cccnt > 0:
            self._wait("sp", ("s_cc", self.cccnt))
        self.barcnt += 1
        self.eng["sp"].sem_inc(self.bar, 1)
        for e in self.eng:
            if e != "sp":
                self._wait(e, ("s_bar", self.barcnt))
            for e2 in self.eng:
                self.known[e]["s_" + e2] = self.cnt[e2]
            for q in self.dsem:
                for i in range(NDMA):
                    self.known[e][f"d_{q}{i}"] = 16 * self.dcnt[q][i]
            self.known[e]["s_cc"] = self.cccnt

    def allgather_many(self, pairs):
        self.barrier()
        for (in_ap, out_ap) in pairs:
            self.cccnt += 1
            self.eng["pool"].collective_compute(
                "AllGather", ALU.bypass,
                replica_groups=[[0, 1], [2, 3], [4, 5], [6, 7]],
                ins=[in_ap], outs=[out_ap]).then_inc(self.cc, 1)
        self.barrier()


class T:
    def __init__(self, h, name):
        self.h = h
        self.b = Buf(name)

    def __getitem__(self, k):
        return self.h[k]


class Cfg:
    def __init__(self, NL=4096, DEPTH=4):
        self.NL = NL
        self.DEPTH = DEPTH
        self.NT = CTX + NL
        self.NKEY = CTX + 2 * NL


def tok_tiles(cfg, TT):
    out = []
    t = 0
    while t < CTX:
        n = min(TT, CTX - t)
        out.append((t, n, 1))
        t += n
    while t < cfg.NT:
        n = min(TT, cfg.NT - t)
        out.append((t, n, 0))
        t += n
    return out


class Prog:
    def __init__(self, cfg):
        self.cfg = cfg
        self.nc = bass.Bass("TRN2", target_bir_lowering=False)
        self.stack = contextlib.ExitStack()

    def sb(self, st, name, shape, dt):
        self._uid = getattr(self, "_uid", 0) + 1
        name = f"{name}_{self._uid}"
        return T(st.enter_context(self.nc.sbuf_tensor(name, shape, dt)), name)

    def dram_in(self, name, shape, dt=F32):
        return self.nc.dram_tensor(name, list(shape), dt, kind="ExternalInput").ap()

    def dram(self, name, shape, dt):
        return self.nc.dram_tensor(name, list(shape), dt).ap()

    def load_w_bf16(self, dst, src_ap, nsplit):
        fw = self.fw
        a = src_ap.shape[1]
        step = (a + nsplit - 1) // nsplit
        for i in range(0, a, step):
            j = min(a, i + step)
            fw.dma("pool", dst[:, i:j, :], src_ap[:, i:j, :], writes=[dst.b])

    def rstd_from_sq(self, ps, n, F, rstd):
        fw = self.fw
        fw.op("act", lambda e: e.activation(rstd[:, :n], ps[:, :n], AF.Sqrt, bias=EPS, scale=1.0 / F),
              writes=[ps.b, rstd.b])
        fw.op("dve", lambda e: e.reciprocal(rstd[:, :n], rstd[:, :n]), reads=[rstd.b], writes=[rstd.b])

    def prenorm(self, x, n, A, Bv, h, ps, sq, rstd, tmp):
        fw = self.fw
        for c in range(DC):
            s = sq[c % 2]
            fw.op("act", lambda e: e.activation(s[:, :n], x[:, c, :n], AF.Square), reads=[x.b], writes=[s.b])
            fw.op("pe", lambda e: e.matmul(ps[:, :n], self.ones_bf[:, :], s[:, :n], start=(c == 0), stop=(c == DC - 1)),
                  reads=[s.b, self.ones_bf.b], writes=[ps.b])
        self.rstd_from_sq(ps, n, D, rstd)
        for c in range(DC):
            t = tmp[c % 2]
            fw.op("dve", lambda e: e.scalar_tensor_tensor(t[:, :n], x[:, c, :n], A[:, c:c + 1], rstd[:, :n], ALU.mult, ALU.mult),
                  reads=[x.b, rstd.b, self.sc.b], writes=[t.b])
            fw.op("act", lambda e: e.activation(h[:, c, :n], t[:, :n], AF.Identity, bias=Bv[:, c:c + 1], scale=1.0),
                  reads=[t.b, self.sc.b], writes=[h.b])

    def postnorm_residual(self, n, Cg, x, y, ps_stat, sq, rstd, tmp, mm_chunk, pbanks):
        fw = self.fw
        pending = None
        for c in range(DC):
            pb = pbanks[c % 2]
            mm_chunk(c, pb)
            fw.op("dve", lambda e: e.tensor_copy(y[:, c, :n], pb[:, :n]), writes=[pb.b, y.b])
            s = sq[c % 2]
            fw.op("act", lambda e: e.activation(s[:, :n], y[:, c, :n], AF.Square), reads=[y.b], writes=[s.b])
            if pending is not None:
                pc, psq = pending
                fw.op("pe", lambda e: e.matmul(ps_stat[:, :n], self.ones_bf[:, :], psq[:, :n], start=(pc == 0), stop=False),
                      reads=[psq.b], writes=[ps_stat.b])
            pending = (c, s)
        pc, psq = pending
        fw.op("pe", lambda e: e.matmul(ps_stat[:, :n], self.ones_bf[:, :], psq[:, :n], start=False, stop=True),
              reads=[psq.b], writes=[ps_stat.b])
        self.rstd_from_sq(ps_stat, n, D, rstd)
        for c in range(DC):
            t = tmp[c % 2]
            fw.op("dve", lambda e: e.scalar_tensor_tensor(t[:, :n], y[:, c, :n], Cg[:, c:c + 1], rstd[:, :n], ALU.mult, ALU.mult),
                  reads=[y.b, rstd.b, self.sc.b], writes=[t.b])
            fw.op("pool", lambda e: e.tensor_tensor(x[:, c, :n], x[:, c, :n], t[:, :n], ALU.add),
                  reads=[x.b, t.b], writes=[x.b])

    def scv(self, l, slot, s):
        return self.sc[:, l, slot, :, s]

    def phase_ada(self):
        fw, nc, cfg = self.fw, self.nc, self.cfg
        with contextlib.ExitStack() as st:
            csb = self.sb(st, "csb", [P, DC, 2], F32)
            sil = self.sb(st, "sil", [P, DC, 2], F32)
            adab = self.sb(st, "adab", [P, cfg.DEPTH, 48], F32)
            ng = self.sb(st, "ng", [P, cfg.DEPTH, 4, DC], F32)
            mod = self.sb(st, "mod", [P, cfg.DEPTH, 48, 2], F32)
            wbuf = [self.sb(st, f"adaw{i}", [P, DC, 1024], F32) for i in range(2)]
            fw.dma("sp", csb[:, :, :], self.in_cT, writes=[csb.b])
            fw.dma("sp", adab[:, :, :], self.in_adab, writes=[adab.b])
            fw.dma("sp", ng[:, :, :, :], self.in_ng, writes=[ng.b])
            fw.op("act", lambda e: e.activation(sil[:, :, :], csb[:, :, :], AF.Silu), reads=[csb.b], writes=[sil.b])
            ps = self.PS[0]
            it = 0
            for l in range(cfg.DEPTH):
                wv = self.in_adaw[l].rearrange("(c p) m -> p c m", p=P)
                for g6 in range(6):
                    wb = wbuf[it % 2]
                    it += 1
                    for hh in range(2):
                        fw.dma("sp", wb[:, hh * 4:(hh + 1) * 4, :], wv[:, hh * 4:(hh + 1) * 4, g6 * 1024:(g6 + 1) * 1024], writes=[wb.b])
                    for j in range(8):
                        jj = g6 * 8 + j
                        for k in range(DC):
                            last = (k == DC - 1)
                            f = lambda e: e.matmul(ps[:, jj * 2:jj * 2 + 2], wb[:, k, j * 128:(j + 1) * 128], sil[:, k, :],
                                                   start=(k == 0), stop=last)
                            if last:
                                fw.op("pe", f, reads=[wb.b, sil.b], writes=[ps.b])
                            else:
                                fw._deps("pe", [wb.b, sil.b], [ps.b])
                                f(fw.eng["pe"])
                psv = ps[:, 0:96].rearrange("p (j s) -> p j s", s=2)
                for s in range(2):
                    fw.op("dve", lambda e: e.tensor_tensor(mod[:, l, :, s], psv[:, :, s], adab[:, l, :], ALU.add),
                          reads=[adab.b], writes=[ps.b, mod.b])
                for s in range(2):
                    m = lambda j: mod[:, l, j * 8:(j + 1) * 8, s]
                    sc = self.sc
                    fw.op("dve", lambda e: e.scalar_tensor_tensor(sc[:, l, 0, :, s], m(1), 1.0, ng[:, l, 0, :], ALU.add, ALU.mult),
                          reads=[mod.b, ng.b], writes=[sc.b])
                    fw.op("dve", lambda e: e.tensor_copy(sc[:, l, 1, :, s], m(0)), reads=[mod.b], writes=[sc.b])
                    fw.op("dve", lambda e: e.tensor_tensor(sc[:, l, 2, :, s], m(2), ng[:, l, 1, :], ALU.mult),
                          reads=[mod.b, ng.b], writes=[sc.b])
                    fw.op("dve", lambda e: e.scalar_tensor_tensor(sc[:, l, 3, :, s], m(4), 1.0, ng[:, l, 2, :], ALU.add, ALU.mult),
                          reads=[mod.b, ng.b], writes=[sc.b])
                    fw.op("dve", lambda e: e.tensor_copy(sc[:, l, 4, :, s], m(3)), reads=[mod.b], writes=[sc.b])
                    fw.op("dve", lambda e: e.tensor_tensor(sc[:, l, 5, :, s], m(5), ng[:, l, 3, :], ALU.mult),
                          reads=[mod.b, ng.b], writes=[sc.b])
            fw.barrier()

    def phase_mlp(self, l, x_src, x_dst):
        fw, nc, cfg = self.fw, self.nc, self.cfg
        TT = 256
        with contextlib.ExitStack() as st:
            w1 = self.sb(st, "w1", [P, DC, DFF], BF16)
            w2 = self.sb(st, "w2", [P, DFF // P, D], BF16)
            xs = [self.sb(st, f"mx{i}", [P, DC, TT], F32) for i in range(2)]
            h = self.sb(st, "mh", [P, DC, TT], BF16)
            hid = self.sb(st, "mhid", [P, DFF // P, TT], BF16)
            y = self.sb(st, "my", [P, DC, TT], F32)
            rl = [self.sb(st, f"mrl{i}", [P, TT], F32) for i in range(2)]
            sq = [self.sb(st, f"msq{i}", [P, TT], BF16) for i in range(2)]
            tmp = [self.sb(st, f"mtmp{i}", [P, TT], F32) for i in range(2)]
            rstd = self.sb(st, "mrstd", [P, TT], F32)
            self.load_w_bf16(w1, self.in_w1[l].rearrange("(c p) m -> p c m", p=P), 4)
            self.load_w_bf16(w2, self.in_w2[l].rearrange("(c p) m -> p c m", p=P), 8)
            tiles = tok_tiles(cfg, TT)
            xv_src = x_src.rearrange("(c p) t -> p c t", p=P)
            xv_dst = x_dst.rearrange("(c p) t -> p c t", p=P)
            PS = self.PS

            def load(i):
                t0, n, s = tiles[i]
                xb = xs[i % 2]
                fw.dma("sp", xb[:, :, :n], xv_src[:, :, t0:t0 + n], writes=[xb.b])

            load(0)
            for i, (t0, n, s) in enumerate(tiles):
                if i + 1 < len(tiles):
                    load(i + 1)
                x = xs[i % 2]
                self.prenorm(x, n, self.scv(l, 3, s), self.scv(l, 4, s), h, PS[4], sq, rstd, tmp)
                for m in range(DFF // P):
                    pb = PS[m % 2]
                    for k in range(DC):
                        f = lambda e: e.matmul(pb[:, :n], w1[:, k, m * P:(m + 1) * P], h[:, k, :n], start=(k == 0), stop=(k == DC - 1))
                        if k == DC - 1:
                            fw.op("pe", f, reads=[w1.b, h.b], writes=[pb.b])
                        else:
                            fw._deps("pe", [w1.b, h.b], [pb.b])
                            f(fw.eng["pe"])
                    r = rl[m % 2]
                    fw.op("act", lambda e: e.activation(r[:, :n], pb[:, :n], AF.Relu), writes=[pb.b, r.b])
                    fw.op("pool", lambda e: e.tensor_tensor(hid[:, m, :n], r[:, :n], r[:, :n], ALU.mult), reads=[r.b], writes=[hid.b])

                def mm_chunk(c, pb):
                    KC = DFF // P
                    for k in range(KC):
                        f = lambda e: e.matmul(pb[:, :n], w2[:, k, c * P:(c + 1) * P], hid[:, k, :n], start=(k == 0), stop=(k == KC - 1))
                        if k == KC - 1:
                            fw.op("pe", f, reads=[w2.b, hid.b], writes=[pb.b])
                        else:
                            fw._deps("pe", [w2.b, hid.b], [pb.b])
                            f(fw.eng["pe"])

                self.postnorm_residual(n, self.scv(l, 5, s), x, y, PS[5], sq, rstd, tmp, mm_chunk, [PS[2], PS[3]])
                fw.dma("pool", xv_dst[:, :, t0:t0 + n], x[:, :, :n], reads=[x.b])
            fw.barrier()

    def mm(self, pb, out_ap, pairs, reads):
        fw = self.fw
        last = len(pairs) - 1
        for i, (l_, r_) in enumerate(pairs):
            f = lambda e: e.matmul(out_ap, l_, r_, start=(i == 0), stop=(i == last))
            if i == last:
                fw.op("pe", f, reads=reads, writes=[pb.b])
            else:
                fw._deps("pe", reads, [pb.b])
                f(fw.eng["pe"])

    def phase_mla1(self, l, j, x_src):
        fw, nc, cfg = self.fw, self.nc, self.cfg
        TT = 256
        PS = self.PS
        with contextlib.ExitStack() as st:
            wdq = self.sb(st, "wdq", [P, DC, 384], BF16)
            wuqn = self.sb(st, "wuqn", [P, 3, 8, 128], BF16)
            wuqr = self.sb(st, "wuqr", [P, 3, 8, 128], BF16)
            wdkc = self.sb(st, "wdkc", [P, DC, 256], BF16)
            wdkr = self.sb(st, "wdkr", [P, DC, 128], BF16)
            wukk = self.sb(st, "wukk", [P, 2, 8, 128], BF16)
            wukv = self.sb(st, "wukv", [P, 2, 8, 128], BF16)
            foldb = self.sb(st, "foldb", [P, P], BF16)
            gsc = self.sb(st, "gsc", [P, 5], F32)
            xs = [self.sb(st, f"ax{i}", [P, DC, TT], F32) for i in range(2)]
            cs = [self.sb(st, f"acs{i}", [P, TT], F32) for i in range(2)]
            h = self.sb(st, "ah", [P, DC, TT], BF16)
            sq = [self.sb(st, f"asq{i}", [P, TT], BF16) for i in range(2)]
            tmp = [self.sb(st, f"atmp{i}", [P, TT], F32) for i in range(2)]
            rstd = self.sb(st, "arstd", [P, TT], F32)
            cqf = self.sb(st, "acqf", [P, 3, TT], F32)
            cqb = self.sb(st, "acqb", [P, 3, TT], BF16)
            ckf = self.sb(st, "ackf", [P, 2, TT], F32)
            ckb = self.sb(st, "ackb", [P, 2, TT], BF16)
            kru = self.sb(st, "akru", [P, TT], BF16)
            ob = [self.sb(st, f"aob{i}", [P, TT], BF16) for i in range(4)]
            vsb = [self.sb(st, f"avsb{i}", [P, 1024], BF16) for i in range(2)]

            fw.dma("pool", wdq[:, :, :], self.in_wdq[j].rearrange("(c p) m -> p c m", p=P), writes=[wdq.b])
            uqv = self.in_wuq[j].rearrange("(c p) (h e) -> p c h e", p=P, e=192)
            for c in range(3):
                fw.dma("pool", wuqn[:, c, :, :], uqv[:, c, :, 0:128], writes=[wuqn.b])
                fw.dma("pool", wuqr[:, c, :, 0:64], uqv[:, c, :, 128:192], writes=[wuqr.b])
            dkv = self.in_wdkv[j].rearrange("(c p) m -> p c m", p=P)
            fw.dma("pool", wdkc[:, :, :], dkv[:, :, 0:256], writes=[wdkc.b])
            fw.dma("pool", wdkr[:, :, 0:64], dkv[:, :, 256:320], writes=[wdkr.b])
            ukv = self.in_wukv[j].rearrange("(c p) (h e) -> p c h e", p=P, e=256)
            for c in range(2):
                fw.dma("pool", wukk[:, c, :, :], ukv[:, c, :, 0:128], writes=[wukk.b])
                fw.dma("pool", wukv[:, c, :, :], ukv[:, c, :, 128:256], writes=[wukv.b])
            fw.dma("sp", gsc[:, :], self.in_mlag[:, j, :], writes=[gsc.b])
            fw.op("dve", lambda e: e.tensor_copy(foldb[:, :], self.cst[:, 0, :]), reads=[self.cst.b], writes=[foldb.b])
            for (wt, nd) in ((wuqr, 4), (wdkr, 3)):
                for (d0, s0, sg) in ((64, 16, -1.0), (80, 0, 1.0), (96, 48, -1.0), (112, 32, 1.0)):
                    if nd == 4:
                        o_, i_ = wt[:, :, :, d0:d0 + 16], wt[:, :, :, s0:s0 + 16]
                    else:
                        o_, i_ = wt[:, :, d0:d0 + 16], wt[:, :, s0:s0 + 16]
                    fw.op("dve", lambda e: e.tensor_scalar_mul(o_, i_, sg), reads=[wt.b], writes=[wt.b])

            tiles = tok_tiles(cfg, TT)
            xv = x_src.rearrange("(c p) t -> p c t", p=P)

            def load(i):
                t0, n, s = tiles[i]
                fw.dma("sp", xs[i % 2][:, :, :n], xv[:, :, t0:t0 + n], writes=[xs[i % 2].b])
                fw.dma("sp", cs[i % 2][:, :n], self.in_rope[:, t0:t0 + n], writes=[cs[i % 2].b])

            load(0)
            nob = 0
            for i, (t0, n, s) in enumerate(tiles):
                if i + 1 < len(tiles):
                    load(i + 1)
                x, csb = xs[i % 2], cs[i % 2]
                self.prenorm(x, n, self.scv(l, 0, s), self.scv(l, 1, s), h, PS[4], sq, rstd, tmp)
                for m in range(3):
                    pb = PS[m % 2]
                    self.mm(pb, pb[:, :n], [(wdq[:, k, m * P:(m + 1) * P], h[:, k, :n]) for k in range(DC)], [wdq.b, h.b])
                    fw.op("dve", lambda e: e.tensor_copy(cqf[:, m, :n], pb[:, :n]), writes=[pb.b, cqf.b])
                    sqq = sq[m % 2]
                    fw.op("act", lambda e: e.activation(sqq[:, :n], cqf[:, m, :n], AF.Square), reads=[cqf.b], writes=[sqq.b])
                    fw.op("pe", lambda e: e.matmul(PS[5][:, :n], self.ones_bf[:, :], sqq[:, :n], start=(m == 0), stop=(m == 2)),
                          reads=[sqq.b], writes=[PS[5].b])
                self.rstd_from_sq(PS[5], n, 384, rstd)
                for m in range(3):
                    fw.op("dve", lambda e: e.scalar_tensor_tensor(cqb[:, m, :n], cqf[:, m, :n], gsc[:, m:m + 1], rstd[:, :n], ALU.mult, ALU.mult),
                          reads=[cqf.b, gsc.b, rstd.b], writes=[cqb.b])
                for m in range(2):
                    pb = PS[m % 2]
                    self.mm(pb, pb[:, :n], [(wdkc[:, k, m * P:(m + 1) * P], h[:, k, :n]) for k in range(DC)], [wdkc.b, h.b])
                    fw.op("dve", lambda e: e.tensor_copy(ckf[:, m, :n], pb[:, :n]), writes=[pb.b, ckf.b])
                    sqq = sq[m % 2]
                    fw.op("act", lambda e: e.activation(sqq[:, :n], ckf[:, m, :n], AF.Square), reads=[ckf.b], writes=[sqq.b])
                    fw.op("pe", lambda e: e.matmul(PS[5][:, :n], self.ones_bf[:, :], sqq[:, :n], start=(m == 0), stop=(m == 1)),
                          reads=[sqq.b], writes=[PS[5].b])
                self.rstd_from_sq(PS[5], n, 256, rstd)
                for m in range(2):
                    fw.op("dve", lambda e: e.scalar_tensor_tensor(ckb[:, m, :n], ckf[:, m, :n], gsc[:, 3 + m:4 + m], rstd[:, :n], ALU.mult, ALU.mult),
                          reads=[ckf.b, gsc.b, rstd.b], writes=[ckb.b])
                pb = PS[2]
                self.mm(pb, pb[:, :n], [(wdkr[:, k, :], h[:, k, :n]) for k in range(DC)], [wdkr.b, h.b])
                fw.op("dve", lambda e: e.tensor_tensor(kru[:, :n], pb[:, :n], csb[:, :n], ALU.mult), reads=[csb.b], writes=[pb.b, kru.b])
                pb = PS[3]
                self.mm(pb, pb[:, :n], [(foldb[:, :], kru[:, :n])], [foldb.b, kru.b])
                o_ = ob[nob % 4]; nob += 1
                fw.op("act", lambda e: e.activation(o_[:, :n], pb[:, :n], AF.Copy), writes=[pb.b, o_.b])
                fw.dma("pool", self.KrT_own[:, t0:t0 + n], o_[:, :n], reads=[o_.b])
                for hh in range(MLA_H):
                    pb = PS[0]
                    self.mm(pb, pb[:, :n], [(wuqn[:, k, hh, :], cqb[:, k, :n]) for k in range(3)], [wuqn.b, cqb.b])
                    o_ = ob[nob % 4]; nob += 1
                    fw.op("act", lambda e: e.activation(o_[:, :n], pb[:, :n], AF.Copy), writes=[pb.b, o_.b])
                    fw.dma("pool", self.qnT[hh * P:(hh + 1) * P, t0:t0 + n], o_[:, :n], reads=[o_.b])
                    pb = PS[1]
                    self.mm(pb, pb[:, :n], [(wuqr[:, k, hh, :], cqb[:, k, :n]) for k in range(3)], [wuqr.b, cqb.b])
                    o_ = ob[nob % 4]; nob += 1
                    fw.op("dve", lambda e: e.tensor_tensor(o_[:, :n], pb[:, :n], csb[:, :n], ALU.mult), reads=[csb.b], writes=[pb.b, o_.b])
                    fw.dma("pool", self.qrT[hh * P:(hh + 1) * P, t0:t0 + n], o_[:, :n], reads=[o_.b])
                    pb = PS[2]
                    self.mm(pb, pb[:, :n], [(wukk[:, k, hh, :], ckb[:, k, :n]) for k in range(2)], [wukk.b, ckb.b])
                    o_ = ob[nob % 4]; nob += 1
                    fw.op("act", lambda e: e.activation(o_[:, :n], pb[:, :n], AF.Copy), writes=[pb.b, o_.b])
                    fw.dma("pool", self.KnT_own[hh * P:(hh + 1) * P, t0:t0 + n], o_[:, :n], reads=[o_.b])
                for sidx in range(n // P):
                    vs_ = vsb[sidx % 2]
                    for half in range(2):
                        pb = PS[6 + half]
                        self.mm(pb, pb[:, :], [(ckb[:, k, sidx * P:(sidx + 1) * P], wukv[:, k, half * 4:(half + 1) * 4, :]) for k in range(2)],
                                [wukv.b, ckb.b])
                        if half == 0:
                            fw.op("act", lambda e: e.activation(vs_[:, 0:512], pb[:, :], AF.Copy), writes=[pb.b, vs_.b])
                        else:
                            fw.op("dve", lambda e: e.tensor_copy(vs_[:, 512:1024], pb[:, :]), writes=[pb.b, vs_.b])
                    fw.dma("pool", self.V_own[t0 + sidx * P:t0 + (sidx + 1) * P, :], vs_[:, :], reads=[vs_.b])
            fw.barrier()

    def phase_mla2(self):
        fw, nc, cfg = self.fw, self.nc, self.cfg
        PS = self.PS
        NT, NL, NKEY = cfg.NT, cfg.NL, cfg.NKEY
        NKT = NKEY // P
        SCALE = 192.0 ** -0.5
        TQ = 512
        with contextlib.ExitStack() as st:
            Kn = [self.sb(st, f"bKn{i}", [P, NKEY], BF16) for i in range(2)]
            Vh = [self.sb(st, f"bVh{i}", [P, NKT, P], BF16) for i in range(2)]
            Kr = self.sb(st, "bKr", [P, NKEY], BF16)
            qn = [self.sb(st, f"bqn{i}", [P, TQ], BF16) for i in range(2)]
            qr = [self.sb(st, f"bqr{i}", [P, TQ], BF16) for i in range(2)]
            Pt = [self.sb(st, f"bPt{i}", [P, 2, TQ], BF16) for i in range(2)]
            acc = self.sb(st, "bacc", [P, 2, TQ], F32)
            accs = self.sb(st, "baccs", [P, TQ], F32)
            rec = self.sb(st, "brec", [P, TQ], F32)
            obf = [self.sb(st, f"bobf{i}", [P, TQ], BF16) for i in range(2)]
            for half in range(2):
                hp = slice(half * 64, (half + 1) * 64)
                fw.dma("sp", Kr[hp, 0:NT], self.Kr_all[half, 0:64, :], writes=[Kr.b])
                fw.dma("sp", Kr[hp, NT:NKEY], self.Kr_all[half, 64:128, CTX:NT], writes=[Kr.b])

            def loadhead(hh):
                kb, vb = Kn[hh % 2], Vh[hh % 2]
                for half in range(2):
                    hp = slice(half * 64, (half + 1) * 64)
                    c = 2 * hh + half
                    fw.dma("sp", kb[hp, 0:NT], self.Kn_all[c, 0:64, :], writes=[kb.b])
                    fw.dma("sp", kb[hp, NT:NKEY], self.Kn_all[c, 64:128, CTX:NT], writes=[kb.b])
                hc = slice(hh * P, (hh + 1) * P)
                for c in range(NT // 256):
                    v0 = self.V_all[c, 0:256, hc].rearrange("(t p) d -> p t d", p=P)
                    fw.dma("sp", vb[:, 2 * c:2 * c + 2, :], v0, writes=[vb.b])
                    if c >= 1:
                        v1 = self.V_all[c, 256:512, hc].rearrange("(t p) d -> p t d", p=P)
                        k0 = NT // P + 2 * (c - 1)
                        fw.dma("sp", vb[:, k0:k0 + 2, :], v1, writes=[vb.b])

            qtiles = tok_tiles(cfg, TQ)
            loadhead(0)
            it = 0
            for hh in range(MLA_H):
                if hh + 1 < MLA_H:
                    loadhead(hh + 1)
                kb, vb = Kn[hh % 2], Vh[hh % 2]
                for (t0, n, s) in qtiles:
                    qn_, qr_ = qn[it % 2], qr[it % 2]
                    o_ = obf[it % 2]
                    Ob = PS[4 + it % 2]
                    it += 1
                    fw.dma("sp", qn_[:, :n], self.qnT[hh * P:(hh + 1) * P, t0:t0 + n], writes=[qn_.b])
                    fw.dma("sp", qr_[:, :n], self.qrT[hh * P:(hh + 1) * P, t0:t0 + n], writes=[qr_.b])
                    nkt = (CTX // P) if s == 1 else NKT
                    npair = nkt // 2

                    def scores(pi):
                        for jj in range(2):
                            kt = 2 * pi + jj
                            pb = PS[2 * (pi % 2) + jj]
                            self.mm(pb, pb[:, :n], [(kb[:, kt * P:(kt + 1) * P], qn_[:, :n]), (Kr[:, kt * P:(kt + 1) * P], qr_[:, :n])],
                                    [kb.b, Kr.b, qn_.b, qr_.b])

                    scores(0)
                    for pi in range(npair):
                        if pi + 1 < npair:
                            scores(pi + 1)
                        S2 = self.PS2[pi % 2][:, :].rearrange("p (a b) -> p a b", a=2)
                        pt = Pt[pi % 2]
                        fw.op("act", lambda e: e.activation(pt[:, :, :n], S2[:, :, :n], AF.Exp, scale=SCALE),
                              writes=[PS[2 * (pi % 2)].b, PS[2 * (pi % 2) + 1].b, pt.b])
                        if pi == 0:
                            fw.op("dve", lambda e: e.tensor_copy(acc[:, :, :n], pt[:, :, :n]), reads=[pt.b], writes=[acc.b])
                        else:
                            fw.op("dve", lambda e: e.tensor_tensor(acc[:, :, :n], acc[:, :, :n], pt[:, :, :n], ALU.add), reads=[pt.b], writes=[acc.b])
                        for jj in range(2):
                            kt = 2 * pi + jj
                            f = lambda e: e.matmul(Ob[:, :n], vb[:, kt, :], pt[:, jj, :n], start=(kt == 0), stop=(kt == nkt - 1))
                            if jj == 1:
                                fw.op("pe", f, reads=[vb.b, pt.b], writes=[Ob.b])
                            else:
                                fw._deps("pe", [vb.b, pt.b], [Ob.b])
                                f(fw.eng["pe"])
                    fw.op("dve", lambda e: e.tensor_tensor(accs[:, :n], acc[:, 0, :n], acc[:, 1, :n], ALU.add), reads=[acc.b], writes=[accs.b])
                    fw.op("pe", lambda e: e.matmul(PS[6][:, :n], self.ones_f[:, :], accs[:, :n], start=True, stop=True),
                          reads=[accs.b, self.ones_f.b], writes=[PS[6].b])
                    fw.op("dve", lambda e: e.reciprocal(rec[:, :n], PS[6][:, :n]), writes=[PS[6].b, rec.b])
                    fw.op("dve", lambda e: e.tensor_tensor(o_[:, :n], Ob[:, :n], rec[:, :n], ALU.mult), reads=[rec.b], writes=[Ob.b, o_.b])
                    fw.dma("pool", self.oT[hh * P:(hh + 1) * P, t0:t0 + n], o_[:, :n], reads=[o_.b])
            fw.barrier()

    def phase_oproj(self, l, w_src, x_src, x_dst, gla_j=None):
        fw, nc, cfg = self.fw, self.nc, self.cfg
        PS = self.PS
        TT = 256
        with contextlib.ExitStack() as st:
            wo = self.sb(st, "wo", [P, DC, D], BF16)
            xs = [self.sb(st, f"ox{i}", [P, DC, TT], F32) for i in range(2)]
            os_ = [self.sb(st, f"oo{i}", [P, DC, TT], BF16) for i in range(2)]
            y = self.sb(st, "oy", [P, DC, TT], F32)
            sq = [self.sb(st, f"osq{i}", [P, TT], BF16) for i in range(2)]
            tmp = [self.sb(st, f"otmp{i}", [P, TT], F32) for i in range(2)]
            rstd = self.sb(st, "orstd", [P, TT], F32)
            if gla_j is not None:
                of = [self.sb(st, f"oof{i}", [P, DC, TT], F32) for i in range(2)]
                rs = [self.sb(st, f"ors{i}", [P, DC, TT], BF16) for i in range(2)]
                go = self.sb(st, "ogo", [P, 2], F32)
                fw.dma("sp", go[:, :], self.in_glago[:, gla_j, :], writes=[go.b])
            self.load_w_bf16(wo, w_src.rearrange("(c p) m -> p c m", p=P), 2)
            tiles = tok_tiles(cfg, TT)
            xv_src = x_src.rearrange("(c p) t -> p c t", p=P)
            xv_dst = x_dst.rearrange("(c p) t -> p c t", p=P)

            def load(i):
                t0, n, s = tiles[i]
                fw.dma("sp", xs[i % 2][:, :, :n], xv_src[:, :, t0:t0 + n], writes=[xs[i % 2].b])
                if gla_j is None:
                    fw.dma("sp", os_[i % 2][:, :, :n], self.oT.rearrange("(c p) t -> p c t", p=P)[:, :, t0:t0 + n], writes=[os_[i % 2].b])
                else:
                    fw.dma("sp", of[i % 2][:, :, :n], self.goT.rearrange("(c p) t -> p c t", p=P)[:, :, t0:t0 + n], writes=[of[i % 2].b])
                    fw.dma("sp", rs[i % 2][:, :, :n], self.grT.rearrange("(c p) t -> p c t", p=P)[:, :, t0:t0 + n], writes=[rs[i % 2].b])

            load(0)
            for i, (t0, n, s) in enumerate(tiles):
                if i + 1 < len(tiles):
                    load(i + 1)
                x, o_ = xs[i % 2], os_[i % 2]
                if gla_j is not None:
                    of_, rs_ = of[i % 2], rs[i % 2]
                    for hd in range(GLA_H):
                        for half in range(2):
                            c = hd * 2 + half
                            sqq = sq[half]
                            fw.op("act", lambda e: e.activation(sqq[:, :n], of_[:, c, :n], AF.Square), reads=[of_.b], writes=[sqq.b])
                            fw.op("pe", lambda e: e.matmul(PS[4][:, :n], self.ones_bf[:, :], sqq[:, :n], start=(half == 0), stop=(half == 1)),
                                  reads=[sqq.b], writes=[PS[4].b])
                        self.rstd_from_sq(PS[4], n, 256, rstd)
                        for half in range(2):
                            c = hd * 2 + half
                            t = tmp[half]
                            fw.op("dve", lambda e: e.scalar_tensor_tensor(t[:, :n], of_[:, c, :n], go[:, half:half + 1], rstd[:, :n], ALU.mult, ALU.mult),
                                  reads=[of_.b, go.b, rstd.b], writes=[t.b])
                            fw.op("pool", lambda e: e.tensor_tensor(o_[:, c, :n], t[:, :n], rs_[:, c, :n], ALU.mult), reads=[t.b, rs_.b], writes=[o_.b])

                def mm_chunk(c, pb):
                    self.mm(pb, pb[:, :n], [(wo[:, k, c * P:(c + 1) * P], o_[:, k, :n]) for k in range(DC)], [wo.b, o_.b])

                self.postnorm_residual(n, self.scv(l, 2, s), x, y, PS[5], sq, rstd, tmp, mm_chunk, [PS[2], PS[3]])
                fw.dma("pool", xv_dst[:, :, t0:t0 + n], x[:, :, :n], reads=[x.b])
            fw.barrier()

    def phase_gla1(self, l, j, x_src):
        fw, nc, cfg = self.fw, self.nc, self.cfg
        TT = 256
        PS = self.PS
        with contextlib.ExitStack() as st:
            wq = self.sb(st, "gwq", [P, DC, 512], BF16)
            wk = self.sb(st, "gwk", [P, DC, 512], BF16)
            wv = self.sb(st, "gwv", [P, DC, 1024], BF16)
            wr = self.sb(st, "gwr", [P, DC, 1024], BF16)
            wg1 = self.sb(st, "gwg1", [P, DC, 2, 16], BF16)
            w2a = self.sb(st, "gw2a", [32, 2, 512], BF16)
            xs = [self.sb(st, f"gx{i}", [P, DC, TT], F32) for i in range(2)]
            h = self.sb(st, "gh", [P, DC, TT], BF16)
            sq = [self.sb(st, f"gsq{i}", [P, TT], BF16) for i in range(2)]
            tmp = [self.sb(st, f"gtmp{i}", [P, TT], F32) for i in range(2)]
            rstd = self.sb(st, "grstd", [P, TT], F32)
            low = [self.sb(st, f"glow{i}", [32, TT], BF16) for i in range(2)]
            of32 = [self.sb(st, f"gof{i}", [P, 512], F32) for i in range(4)]
            obf = [self.sb(st, f"gob{i}", [P, 1024], BF16) for i in range(2)]
            robf = [self.sb(st, f"grob{i}", [P, DC, TT], BF16) for i in range(2)]
            ex = [self.sb(st, f"gex{i}", [P, 512], F32) for i in range(2)]
            for (w_, src) in ((wq, self.in_gwq), (wk, self.in_gwk), (wv, self.in_gwv), (wr, self.in_gwr)):
                self.load_w_bf16(w_, src[j].rearrange("(c p) m -> p c m", p=P), 2)
            for d_ in range(2):
                fw.dma("pool", wg1[:, :, d_, :], self.in_gw1[j, d_].rearrange("(c p) m -> p c m", p=P), writes=[wg1.b])
                fw.dma("pool", w2a[0:16, d_, :], self.in_gw2[j, d_], writes=[w2a.b])
                fw.dma("pool", w2a[16:17, d_, :], self.in_gbg[j, d_:d_ + 1, :], writes=[w2a.b])
                fw.op("dve", lambda e: e.memset(low[d_][:, :], 1.0), writes=[low[d_].b])
            tiles = tok_tiles(cfg, TT)
            xv = x_src.rearrange("(c p) t -> p c t", p=P)

            def load(i):
                t0, n, s = tiles[i]
                fw.dma("sp", xs[i % 2][:, :, :n], xv[:, :, t0:t0 + n], writes=[xs[i % 2].b])

            load(0)
            nf = 0
            nb = 0
            gout = [self.ggA, self.ggB]
            for i, (t0, n, s) in enumerate(tiles):
                if i + 1 < len(tiles):
                    load(i + 1)
                x = xs[i % 2]
                self.prenorm(x, n, self.scv(l, 0, s), self.scv(l, 1, s), h, PS[4], sq, rstd, tmp)
                for (w_, dst, scl) in ((wq, self.gqT, 128.0 ** -0.5), (wk, self.gkT, 1.0)):
                    for hd in range(GLA_H):
                        pb = PS[hd % 2]
                        self.mm(pb, pb[:, :n], [(w_[:, k, hd * P:(hd + 1) * P], h[:, k, :n]) for k in range(DC)], [w_.b, h.b])
                        o_ = of32[nf % 4]; nf += 1
                        fw.op("act", lambda e: e.activation(o_[:, :n], pb[:, :n], AF.Copy, scale=scl), writes=[pb.b, o_.b])
                        fw.dma("pool", dst[hd * P:(hd + 1) * P, t0:t0 + n], o_[:, :n], reads=[o_.b])
                ob_ = robf[i % 2]
                for c in range(DC):
                    pb = PS[c % 2]
                    self.mm(pb, pb[:, :n], [(wr[:, k, c * P:(c + 1) * P], h[:, k, :n]) for k in range(DC)], [wr.b, h.b])
                    fw.op("act", lambda e: e.activation(ob_[:, c, :n], pb[:, :n], AF.Silu), writes=[pb.b, ob_.b])
                fw.dma("pool", self.grT.rearrange("(c p) t -> p c t", p=P)[:, :, t0:t0 + n], ob_[:, :, :n], reads=[ob_.b])
                for d_ in range(2):
                    pb = PS[2 + d_]
                    self.mm(pb, pb[0:16, :n], [(wg1[:, k, d_, :], h[:, k, :n]) for k in range(DC)], [wg1.b, h.b])
                    fw.op("dve", lambda e: e.tensor_copy(low[d_][0:16, :n], pb[0:16, :n]), writes=[pb.b, low[d_].b])
                for sidx in range(n // P):
                    tsl = slice(sidx * P, (sidx + 1) * P)
                    r0 = t0 + sidx * P
                    pb = PS[6]
                    self.mm(pb, pb[:, :], [(h[:, k, tsl], wk[:, k, :]) for k in range(DC)], [wk.b, h.b])
                    o_ = of32[nf % 4]; nf += 1
                    fw.op("dve", lambda e: e.tensor_copy(o_[:, :], pb[:, :]), writes=[pb.b, o_.b])
                    fw.dma("pool", self.gk[r0:r0 + P, :], o_[:, :], reads=[o_.b])
                    ob_ = obf[nb % 2]; nb += 1
                    for half in range(2):
                        pb = PS[6 + half]
                        self.mm(pb, pb[:, :], [(h[:, k, tsl], wv[:, k, half * 512:(half + 1) * 512]) for k in range(DC)], [wv.b, h.b])
                        if half == 0:
                            fw.op("act", lambda e: e.activation(ob_[:, 0:512], pb[:, :], AF.Copy), writes=[pb.b, ob_.b])
                        else:
                            fw.op("dve", lambda e: e.tensor_copy(ob_[:, 512:1024], pb[:, :]), writes=[pb.b, ob_.b])
                    fw.dma("pool", self.gv[r0:r0 + P, :], ob_[:, :], reads=[ob_.b])
                    for d_ in range(2):
                        pb = PS[d_]
                        self.mm(pb, pb[:, :], [(low[d_][0:17, tsl], w2a[0:17, d_, :])], [low[d_].b, w2a.b])
                        e_ = ex[d_]
                        fw.op("act", lambda e: e.activation(e_[:, :], pb[:, :], AF.Exp, scale=-1.0), writes=[pb.b, e_.b])
                        o_ = of32[nf % 4]; nf += 1
                        fw.op("act", lambda e: e.activation(o_[:, :], e_[:, :], AF.Ln, bias=1.0, scale=1.0), reads=[e_.b], writes=[o_.b])
                        fw.dma("pool", gout[d_][r0:r0 + P, :], o_[:, :], reads=[o_.b])
            fw.barrier()

    def phase_gla_scan(self, which):
        fw, nc, cfg = self.fw, self.nc, self.cfg
        PS = self.PS
        NTL = cfg.NT // P
        NCT = CTX // P
        A = (which == "A")
        cM, cU, cMask = (1, 2, 3) if A else (4, 5, 6)
        gsrc = self.ggA if A else self.ggB
        with contextlib.ExitStack() as st:
            S = [self.sb(st, f"sS{hd}", [P, 256], F32) for hd in range(GLA_H)]
            Sbf = [[self.sb(st, f"sSbf{i}_{hd}", [P, 256], BF16) for hd in range(GLA_H)] for i in range(2)]
            qT = [self.sb(st, f"sq{i}", [P, GLA_H, P], F32) for i in range(2)]
            kT = [self.sb(st, f"sk{i}", [P, GLA_H, P], F32) for i in range(2)]
            kk = [self.sb(st, f"skk{i}", [P, 512], F32) for i in range(2)]
            vv = [self.sb(st, f"sv{i}", [P, 1024], BF16) for i in range(2)]
            gg = [self.sb(st, f"sg{i}", [P, 512], F32) for i in range(2)]
            oa = [self.sb(st, f"soa{i}", [P, DC, P], F32) for i in range(2)]
            eG = [self.sb(st, f"seG{i}", [P, P], F32) for i in range(4)]
            enG = [self.sb(st, f"senG{i}", [P, P], F32) for i in range(4)]
            eE = [self.sb(st, f"seE{i}", [P, P], F32) for i in range(4)]
            qt = [self.sb(st, f"sqt{i}", [P, P], BF16) for i in range(4)]
            kt = [self.sb(st, f"skt{i}", [P, P], BF16) for i in range(4)]
            ke = [self.sb(st, f"ske{i}", [P, P], BF16) for i in range(4)]
            ab = [self.sb(st, f"sab{i}", [P, P], BF16) for i in range(4)]
            osb = [self.sb(st, f"sos{i}", [P, 2, P], F32) for i in range(4)]
            cst = self.cst
            if A:
                for hd in range(GLA_H):
                    fw.op("dve", lambda e: e.memset(S[hd][:, :], 0.0), writes=[S[hd].b])
            else:
                s0 = self.sb(st, "ss0", [P, GLA_H, 256], F32)
                s1 = self.sb(st, "ss1", [P, GLA_H, 256], F32)
                sel = self.sb(st, "ssel", [P, 2], F32)
                fw.dma("sp", sel[:, :], self.in_sel, writes=[sel.b])
                ev = self.exch_all.rearrange("(r h p) v -> r p h v", r=2, p=P)
                fw.dma("sp", s0[:, :, :], ev[0], writes=[s0.b])
                fw.dma("sp", s1[:, :, :], ev[1], writes=[s1.b])
                for hd in range(GLA_H):
                    fw.op("dve", lambda e: e.tensor_scalar_mul(S[hd][:, :], s0[:, hd, :], sel[:, 0:1]), reads=[s0.b, sel.b], writes=[S[hd].b])
                    fw.op("dve", lambda e: e.scalar_tensor_tensor(S[hd][:, :], s1[:, hd, :], sel[:, 1:2], S[hd][:, :], ALU.mult, ALU.add),
                          reads=[s1.b, sel.b, S[hd].b], writes=[S[hd].b])
            for hd in range(GLA_H):
                fw.op("dve", lambda e: e.tensor_copy(Sbf[0][hd][:, :], S[hd][:, :]), reads=[S[hd].b], writes=[Sbf[0][hd].b])
            sp_ = 0
            if A:
                order = list(range(NTL))
            else:
                order = list(range(NTL - 1, NCT - 1, -1)) + list(range(NCT - 1, -1, -1))
            corder = (0, 1) if A else (1, 0)

            def load(ii):
                t = order[ii]
                r0 = t * P
                b_ = ii % 2
                fw.dma("sp", qT[b_][:, :, :], self.gqT.rearrange("(h p) t -> p h t", p=P)[:, :, r0:r0 + P], writes=[qT[b_].b])
                fw.dma("sp", kT[b_][:, :, :], self.gkT.rearrange("(h p) t -> p h t", p=P)[:, :, r0:r0 + P], writes=[kT[b_].b])
                fw.dma("sp", kk[b_][:, :], self.gk[r0:r0 + P, :], writes=[kk[b_].b])
                fw.dma("sp", vv[b_][:, :], self.gv[r0:r0 + P, :], writes=[vv[b_].b])
                fw.dma("sp", gg[b_][:, :], gsrc[r0:r0 + P, :], writes=[gg[b_].b])
                if not A:
                    fw.dma("sp", oa[b_][:, :, :], self.goT.rearrange("(c p) t -> p c t", p=P)[:, :, r0:r0 + P], writes=[oa[b_].b])

            load(0)
            nit = 0
            nos = 0
            for ii, t in enumerate(order):
                if ii + 1 < len(order):
                    load(ii + 1)
                b_ = ii % 2
                r0 = t * P
                if (not A) and t == NCT - 1:
                    sp_ ^= 1
                    for hd in range(GLA_H):
                        fw.op("dve", lambda e: e.memset(S[hd][:, :], 0.0), writes=[S[hd].b])
                        fw.op("dve", lambda e: e.tensor_copy(Sbf[sp_][hd][:, :], S[hd][:, :]), reads=[S[hd].b], writes=[Sbf[sp_][hd].b])
                cur = sp_
                for hd in range(GLA_H):
                    bX, bY = PS[2 * hd], PS[2 * hd + 1]
                    i2 = nit % 4
                    nit += 1
                    hs = slice(hd * P, (hd + 1) * P)
                    g_ = gg[b_]
                    fw.op("pe", lambda e: e.matmul(bX[:, 0:128], g_[:, hs], cst[:, cM, :], start=True, stop=True),
                          reads=[g_.b, cst.b], writes=[bX.b])
                    fw.op("pe", lambda e: e.matmul(bX[:, 128:256], cst[:, cU, :], g_[:, hs], start=True, stop=True),
                          reads=[g_.b, cst.b], writes=[bX.b])
                    fw.op("act", lambda e: e.activation(eG[i2][:, :], bX[:, 0:128], AF.Exp), writes=[bX.b, eG[i2].b])
                    fw.op("act", lambda e: e.activation(enG[i2][:, :], bX[:, 0:128], AF.Exp, scale=-1.0), writes=[bX.b, enG[i2].b])
                    fw.op("act", lambda e: e.activation(eE[i2][:, :], bX[:, 128:256], AF.Exp), writes=[bX.b, eE[i2].b])
                    fw.op("dve", lambda e: e.tensor_tensor(qt[i2][:, :], qT[b_][:, hd, :], eG[i2][:, :], ALU.mult),
                          reads=[qT[b_].b, eG[i2].b], writes=[qt[i2].b])
                    fw.op("dve", lambda e: e.tensor_tensor(kt[i2][:, :], kT[b_][:, hd, :], enG[i2][:, :], ALU.mult),
                          reads=[kT[b_].b, enG[i2].b], writes=[kt[i2].b])
                    fw.op("dve", lambda e: e.tensor_tensor(ke[i2][:, :], kk[b_][:, hs], eE[i2][:, :], ALU.mult),
                          reads=[kk[b_].b, eE[i2].b], writes=[ke[i2].b])
                    fw.op("pe", lambda e: e.matmul(bX[:, 256:384], kt[i2][:, :], qt[i2][:, :], start=True, stop=True),
                          reads=[kt[i2].b, qt[i2].b], writes=[bX.b])
                    fw.op("dve", lambda e: e.tensor_tensor(ab[i2][:, :], bX[:, 256:384], cst[:, cMask, :], ALU.mult),
                          reads=[cst.b], writes=[bX.b, ab[i2].b])
                    c0, c1 = corder
                    v_ = vv[b_]

                    def upd(c, src_par, dst_par):
                        cs_ = slice(c * 64, (c + 1) * 64)
                        fw.op("pe", lambda e: e.matmul(bY[:, 256:512], ke[i2][cs_, :], v_[cs_, hd * 256:(hd + 1) * 256], start=True, stop=True),
                              reads=[ke[i2].b, v_.b], writes=[bY.b])
                        col = c * 64 + (63 if A else 0)
                        fw.op("dve", lambda e: e.scalar_tensor_tensor(S[hd][:, :], S[hd][:, :], eG[i2][:, col:col + 1], bY[:, 256:512], ALU.mult, ALU.add),
                              reads=[eG[i2].b], writes=[S[hd].b, bY.b])
                        fw.op("act", lambda e: e.activation(Sbf[dst_par][hd][:, :], S[hd][:, :], AF.Copy), reads=[S[hd].b], writes=[Sbf[dst_par][hd].b])

                    upd(c0, cur, cur ^ 1)
                    for half in range(2):
                        po = bY[:, half * P:(half + 1) * P]
                        vs = slice(hd * 256 + half * P, hd * 256 + (half + 1) * P)
                        ss = slice(half * P, (half + 1) * P)
                        fw._deps("pe", [v_.b, ab[i2].b, Sbf[0][hd].b, Sbf[1][hd].b, qt[i2].b], [bY.b])
                        fw.eng["pe"].matmul(po, v_[:, vs], ab[i2][:, :], start=True, stop=False)
                        fw.eng["pe"].matmul(po[:, c0 * 64:(c0 + 1) * 64], Sbf[cur][hd][:, ss], qt[i2][:, c0 * 64:(c0 + 1) * 64], start=False, stop=False)
                        fw.op("pe", lambda e: e.matmul(po[:, c1 * 64:(c1 + 1) * 64], Sbf[cur ^ 1][hd][:, ss], qt[i2][:, c1 * 64:(c1 + 1) * 64], start=False, stop=True),
                              reads=[v_.b, ab[i2].b, Sbf[0][hd].b, Sbf[1][hd].b, qt[i2].b], writes=[bY.b])
                    o_ = osb[nos % 4]; nos += 1
                    pov = bY[:, 0:256].rearrange("p (a b) -> p a b", a=2)
                    if A:
                        fw.op("act", lambda e: e.activation(o_[:, :, :], pov, AF.Copy), writes=[bY.b, o_.b])
                    else:
                        fw.op("dve", lambda e: e.tensor_tensor(o_[:, :, :], pov, oa[b_][:, 2 * hd:2 * hd + 2, :], ALU.add),
                              reads=[oa[b_].b], writes=[bY.b, o_.b])
                    fw.dma("pool", self.goT[hd * 256:(hd + 1) * 256, r0:r0 + P].rearrange("(a p) t -> p a t", p=P), o_[:, :, :], reads=[o_.b])
                    upd(c1, cur ^ 1, cur)
            if A:
                for hd in range(GLA_H):
                    fw.dma("pool", self.exch_own[hd * P:(hd + 1) * P, :], S[hd][:, :], reads=[S[hd].b])
            fw.barrier()

    def declare_common(self):
        cfg, nc = self.cfg, self.nc
        L = cfg.DEPTH
        self.in_xT = self.dram_in("xT", [D, cfg.NT])
        self.in_cT = self.dram_in("cT", [P, DC, 2])
        self.in_adab = self.dram_in("adabT", [P, L, 48])
        self.in_ng = self.dram_in("ngT", [P, L, 4, DC])
        self.in_adaw = self.dram_in("ada_w", [L, D, 6 * D])
        self.in_w1 = self.dram_in("mlp_w1", [L, D, DFF])
        self.in_w2 = self.dram_in("mlp_w2", [L, DFF, D])
        self.in_consts = self.dram_in("consts", [P, 8, P])
        self.out_T = nc.dram_tensor("outT", [D, cfg.NT], F32, kind="ExternalOutput").ap()
        self.xA = self.dram("xA", [D, cfg.NT], F32)
        self.xB = self.dram("xB", [D, cfg.NT], F32)

    def declare_all(self):
        cfg, nc = self.cfg, self.nc
        self.declare_common()
        NT = cfg.NT
        na = (cfg.DEPTH + 1) // 2
        nb = max(cfg.DEPTH // 2, 1)
        self.in_rope = self.dram_in("ropeT", [P, NT])
        self.in_sel = self.dram_in("sel", [P, 2])
        self.in_wdq = self.dram_in("mla_w_dq", [na, D, 384])
        self.in_wuq = self.dram_in("mla_w_uq", [na, 384, 1536])
        self.in_wdkv = self.dram_in("mla_w_dkv", [na, D, 320])
        self.in_wukv = self.dram_in("mla_w_ukv", [na, 256, 2048])
        self.in_wo_mla = self.dram_in("mla_w_o", [na, D, D])
        self.in_mlag = self.dram_in("mla_gT", [P, na, 5])
        self.in_gwq = self.dram_in("gla_w_q", [nb, D, 512])
        self.in_gwk = self.dram_in("gla_w_k", [nb, D, 512])
        self.in_gwv = self.dram_in("gla_w_v", [nb, D, D])
        self.in_gwr = self.dram_in("gla_w_r", [nb, D, D])
        self.in_gw1 = self.dram_in("gla_w_gate1", [nb, 2, D, 16])
        self.in_gw2 = self.dram_in("gla_w_gate2", [nb, 2, 16, 512])
        self.in_gbg = self.dram_in("gla_b_gate", [nb, 2, 512])
        self.in_glago = self.dram_in("gla_goT", [P, nb, 2])
        self.in_wo_gla = self.dram_in("gla_w_o", [nb, D, D])
        self.KnT_own = self.dram("KnT_own", [1024, NT], BF16)
        self.KrT_own = self.dram("KrT_own", [P, NT], BF16)
        self.V_own = self.dram("V_own", [NT, 1024], BF16)
        self.Kn_all = self.dram("Kn_all", [16, P, NT], BF16)
        self.Kr_all = self.dram("Kr_all", [2, P, NT], BF16)
        self.V_all = self.dram("V_all", [NT // 256, 512, 1024], BF16)
        self.qnT = self.dram("qnT", [1024, NT], BF16)
        self.qrT = self.dram("qrT", [1024, NT], BF16)
        self.oT = self.dram("oT", [1024, NT], BF16)
        self.gqT = self.dram("gqT", [512, NT], F32)
        self.gkT = self.dram("gkT", [512, NT], F32)
        self.gk = self.dram("gk", [NT, 512], F32)
        self.gv = self.dram("gv", [NT, 1024], BF16)
        self.ggA = self.dram("ggA", [NT, 512], F32)
        self.ggB = self.dram("ggB", [NT, 512], F32)
        self.grT = self.dram("grT", [1024, NT], BF16)
        self.goT = self.dram("goT", [1024, NT], F32)
        self.exch_own = self.dram("exch_own", [512, 256], F32)
        self.exch_all = self.dram("exch_all", [1024, 256], F32)

    def build(self):
        cfg = self.cfg
        self.declare_all()
        self.setup_globals()
        fw = self.fw
        self.phase_ada()
        x_cur = self.in_xT
        for l in range(cfg.DEPTH):
            j = l // 2
            if l % 2 == 0:
                self.phase_mla1(l, j, x_cur)
                pairs = [(self.KnT_own[c * 64:(c + 1) * 64, :], self.Kn_all[c]) for c in range(16)]
                pairs += [(self.KrT_own[c * 64:(c + 1) * 64, :], self.Kr_all[c]) for c in range(2)]
                pairs += [(self.V_own[c * 256:(c + 1) * 256, :], self.V_all[c]) for c in range(cfg.NT // 256)]
                fw.allgather_many(pairs)
                self.phase_mla2()
                self.phase_oproj(l, self.in_wo_mla[j], x_cur, self.xA)
            else:
                self.phase_gla1(l, j, x_cur)
                self.phase_gla_scan("A")
                fw.allgather_many([(self.exch_own, self.exch_all)])
                self.phase_gla_scan("B")
                self.phase_oproj(l, self.in_wo_gla[j], x_cur, self.xA, gla_j=j)
            dst = self.out_T if l == cfg.DEPTH - 1 else self.xB
            self.phase_mlp(l, self.xA, dst)
            x_cur = self.xB
        fw.barrier()
        self.stack.close()
        return self.nc

    def setup_globals(self):
        st, nc, cfg = self.stack, self.nc, self.cfg
        self.fw = FW(nc, st)
        fw = self.fw
        self.PS2 = [st.enter_context(nc.psum_tensor(f"ps{i}", [P, 1024], F32)) for i in range(4)]
        self.PS = [T(self.PS2[i // 2][:, (i % 2) * 512:(i % 2 + 1) * 512], f"psb{i}") for i in range(8)]
        self.sc = self.sb(st, "sc", [P, cfg.DEPTH, 6, DC, 2], F32)
        self.cst = self.sb(st, "cst", [P, 8, P], F32)
        self.ones_bf = self.sb(st, "ones_bf", [P, P], BF16)
        self.ones_f = self.sb(st, "ones_f", [P, P], F32)
        fw.dma("sp", self.cst[:, :, :], self.in_consts, writes=[self.cst.b])
        fw.op("dve", lambda e: e.memset(self.ones_bf[:, :], 1.0), writes=[self.ones_bf.b])
        fw.op("dve", lambda e: e.memset(self.ones_f[:, :], 1.0), writes=[self.ones_f.b])

    def build_mlp_only(self):
        self.declare_common()
        self.setup_globals()
        self.phase_ada()
        self.phase_mlp(0, self.in_xT, self.out_T)
        self.fw.barrier()
        self.stack.close()
        return self.nc


def make_consts():
    c = np.zeros((P, 8, P), np.float32)
    j = np.arange(P)[:, None]
    i = np.arange(P)[None, :]
    same = (j // 64) == (i // 64)
    c[:, 0, :] = ((j % 64) == (i % 64)).astype(np.float32)
    c[:, 1, :] = np.where(same & (j <= i), -1.0 / 16, 0.0)
    c[:, 2, :] = np.where(same & (j > i), -1.0 / 16, 0.0)
    c[:, 3, :] = np.where(same & (j <= i), 1.0, 0.0)
    c[:, 4, :] = np.where(same & (j >= i), -1.0 / 16, 0.0)
    c[:, 5, :] = np.where(same & (j < i), -1.0 / 16, 0.0)
    c[:, 6, :] = np.where(same & (j >= i), 1.0, 0.0)
    return c


def rope_table(pos):
    half = 32
    inv = (ROPE_BASE ** (-np.arange(0, half, 2, dtype=np.float32) / np.float32(half))).astype(np.float32)
    row = (pos // GRID_W).astype(np.float32)
    col = (pos % GRID_W).astype(np.float32)
    ang_r = (row[:, None] * inv[None, :]).astype(np.float32)
    ang_c = (col[:, None] * inv[None, :]).astype(np.float32)
    ang = np.concatenate([ang_r, ang_r, ang_c, ang_c], axis=1)
    tab = np.zeros((P, CTX + len(pos)), np.float32)
    tab[0:64, :CTX] = 1.0
    tab[0:64, CTX:] = np.cos(ang).T
    tab[64:128, CTX:] = np.sin(ang).T
    return tab


_PROG_CACHE = {}


def make_in_maps(cfg, inputs, n_batch):
    NL, L = cfg.NL, cfg.DEPTH
    f32 = lambda a: np.ascontiguousarray(np.asarray(a, dtype=np.float32))
    x, c, ctx, c_ctx = (np.asarray(inputs[k]) for k in ("x", "c", "ctx", "c_ctx"))
    shared = {
        "adabT": f32(np.asarray(inputs["ada_b"])[:L].reshape(L, 48, P).transpose(2, 0, 1)),
        "ngT": f32(np.asarray(inputs["norm_g"])[:L].reshape(L, 4, DC, P).transpose(3, 0, 1, 2)),
        "ada_w": f32(np.asarray(inputs["ada_w"])[:L]),
        "mlp_w1": f32(np.asarray(inputs["mlp_w1"])[:L]),
        "mlp_w2": f32(np.asarray(inputs["mlp_w2"])[:L]),
        "consts": make_consts(),
    }
    na = (L + 1) // 2
    nb = max(L // 2, 1)
    for k in ("mla_w_dq", "mla_w_uq", "mla_w_dkv", "mla_w_ukv", "mla_w_o"):
        shared[k] = f32(np.asarray(inputs[k])[:na])
    gq = np.asarray(inputs["mla_g_q"])[:na].reshape(na, 3, P)
    gkv = np.asarray(inputs["mla_g_kv"])[:na].reshape(na, 2, P)
    shared["mla_gT"] = f32(np.concatenate([gq, gkv], axis=1).transpose(2, 0, 1))
    for k in ("gla_w_q", "gla_w_k", "gla_w_v", "gla_w_r", "gla_w_o"):
        shared[k] = f32(np.asarray(inputs[k])[:nb])
    shared["gla_goT"] = f32(np.asarray(inputs["gla_g_o"])[:nb].reshape(nb, 2, P).transpose(2, 0, 1))
    in_maps = []
    for b in range(n_batch):
        for r in range(2):
            if r == 0:
                pos = np.arange(0, NL)
                cx = ctx[b]
                dirs = [0, 1]
            else:
                pos = np.arange(2 * NL - 1, NL - 1, -1)
                cx = ctx[b][::-1]
                dirs = [1, 0]
            xt = np.concatenate([cx, x[b][pos]], axis=0)
            m = dict(shared)
            m["xT"] = f32(xt.T)
            m["cT"] = f32(np.stack([c[b].reshape(DC, P).T, c_ctx.reshape(DC, P).T], -1))
            m["ropeT"] = rope_table(pos)
            sel = np.zeros((P, 2), np.float32)
            sel[:, 1 - r] = 1.0
            m["sel"] = sel
            m["gla_w_gate1"] = f32(np.asarray(inputs["gla_w_gate1"])[:nb][:, dirs])
            m["gla_w_gate2"] = f32(np.asarray(inputs["gla_w_gate2"])[:nb][:, dirs])
            m["gla_b_gate"] = f32(np.asarray(inputs["gla_b_gate"])[:nb][:, dirs])
            in_maps.append(m)
    return in_maps


def assemble(cfg, outs, n_batch):
    NL = cfg.NL
    out = np.zeros((n_batch, 2 * NL, D), np.float32)
    for b in range(n_batch):
        out[b, :NL] = np.asarray(outs[2 * b])[:, CTX:].T
        out[b, NL:] = np.asarray(outs[2 * b + 1])[:, CTX:].T[::-1]
    return out


def run_cores(cfg, inputs, n_batch):
    in_maps = make_in_maps(cfg, inputs, n_batch)
    nc = Prog(cfg).build()
    res = run_bass_kernel_spmd(nc, in_maps, core_ids=list(range(2 * n_batch)))
    return assemble(cfg, [r["outT"] for r in res.results], n_batch)


def kernel(**inputs):
    cfg = Cfg(NL=4096, DEPTH=4)
    return run_cores(cfg, inputs, 4)
```

```python
import contextlib
import numpy as np
import ml_dtypes
import concourse.bass as bass
import concourse.mybir as mybir
from concourse.bass_utils import run_bass_kernel_spmd

F32 = mybir.dt.float32
BF16 = mybir.dt.bfloat16
AF = mybir.ActivationFunctionType
ALU = mybir.AluOpType

NDMA = 20
P = 128
D = 1024
DC = 8
DFF = 4096
CTX = 256
EPS = 1e-6
GRID_W = 64
MLA_H = 8
GLA_H = 4
ROPE_BASE = 10000.0


class Buf:
    __slots__ = ("name", "w", "r")

    def __init__(self, name):
        self.name = name
        self.w = None
        self.r = []


class FW:
    def __init__(self, nc, stack):
        self.nc = nc
        self.eng = {"pe": nc.tensor, "act": nc.scalar, "dve": nc.vector,
                    "pool": nc.gpsimd, "sp": nc.sync}
        self.sem, self.cnt, self.semobj = {}, {}, {}
        for e in self.eng:
            self.sem[e] = stack.enter_context(nc.semaphore("s_" + e))
            self.cnt[e] = 0
            self.semobj["s_" + e] = self.sem[e]
        self.dsem, self.dcnt, self.dnext = {}, {}, {}
        for q in ("sp", "pool"):
            self.dsem[q] = [stack.enter_context(nc.semaphore(f"d_{q}{i}")) for i in range(NDMA)]
            self.dcnt[q] = [0] * NDMA
            self.dnext[q] = 0
            for i, s in enumerate(self.dsem[q]):
                self.semobj[f"d_{q}{i}"] = s
        self.bar = stack.enter_context(nc.semaphore("s_bar"))
        self.semobj["s_bar"] = self.bar
        self.barcnt = 0
        self.cc = stack.enter_context(nc.semaphore("s_cc"))
        self.semobj["s_cc"] = self.cc
        self.cccnt = 0
        self.known = {e: {} for e in self.eng}
        self.nwaits = 0
        self.nins = 0

    def _wait(self, e, tok):
        if tok is None:
            return
        name, val = tok
        if self.known[e].get(name, 0) >= val:
            return
        self.eng[e].wait_ge(self.semobj[name], val)
        self.known[e][name] = val
        self.nwaits += 1

    def _deps(self, e, reads, writes):
        own = "s_" + e
        for b in reads:
            if b.w is not None and not (b.w[0] == own and e == "pe"):
                self._wait(e, b.w)
        for b in writes:
            if b.w is not None and not (b.w[0] == own and e == "pe"):
                self._wait(e, b.w)
            for t in b.r:
                if t[0] != own:
                    self._wait(e, t)

    def _record(self, tok, reads, writes):
        for b in reads:
            b.r.append(tok)
        for b in writes:
            b.w = tok
            b.r = []

    def op(self, e, fn, reads=(), writes=()):
        self._deps(e, reads, writes)
        ins = fn(self.eng[e])
        self.cnt[e] += 1
        ins.then_inc(self.sem[e], 1)
        tok = ("s_" + e, self.cnt[e])
        self._record(tok, reads, writes)
        self.nins += 1
        return tok

    def dma(self, q, out, in_, reads=(), writes=()):
        i = self.dnext[q]
        self.dnext[q] = (i + 1) % NDMA
        name = f"d_{q}{i}"
        if self.dcnt[q][i] > 0:
            self._wait(q, (name, 16 * self.dcnt[q][i]))
        self._deps(q, reads, writes)
        self.dcnt[q][i] += 1
        self.eng[q].dma_start(out=out, in_=in_).then_inc(self.dsem[q][i], 16)
        tok = (name, 16 * self.dcnt[q][i])
        self._record(tok, reads, writes)
        self.nins += 1
        return tok

    def barrier(self):
        for e in self.eng:
            if e != "sp" and self.cnt[e] > 0:
                self._wait("sp", ("s_" + e, self.cnt[e]))
        for q in self.dsem:
            for i in range(NDMA):
                if self.dcnt[q][i] > 0:
                    self._wait("sp", (f"d_{q}{i}", 16 * self.dcnt[q][i]))
        if self.cccnt > 0:
            self._wait("sp", ("s_cc", self.cccnt))
        self.barcnt += 1
        self.eng["sp"].sem_inc(self.bar, 1)
        for e in self.eng:
            if e != "sp":
                self._wait(e, ("s_bar", self.barcnt))
            for e2 in self.eng:
                self.known[e]["s_" + e2] = self.cnt[e2]
            for q in self.dsem:
                for i in range(NDMA):
                    self.known[e][f"d_{q}{i}"] = 16 * self.dcnt[q][i]
            self.known[e]["s_cc"] = self.cccnt

    def allgather_many(self, pairs):
        self.barrier()
        for (in_ap, out_ap) in pairs:
            self.cccnt += 1
            self.eng["pool"].collective_compute(
                "AllGather", ALU.bypass,
                replica_groups=[[0, 1], [2, 3], [4, 5], [6, 7]],
                ins=[in_ap], outs=[out_ap]).then_inc(self.cc, 1)
        self.barrier()


class T:
    def __init__(self, h, name):
        self.h = h
        self.b = Buf(name)

    def __getitem__(self, k):
        return self.h[k]


class Cfg:
    def __init__(self, NL=4096, DEPTH=4):
        self.NL = NL
        self.DEPTH = DEPTH
        self.NT = CTX + NL
        self.NKEY = CTX + 2 * NL


def tok_tiles(cfg, TT):
    out = []
    t = 0
    while t < CTX:
        n = min(TT, CTX - t)
        out.append((t, n, 1))
        t += n
    while t < cfg.NT:
        n = min(TT, cfg.NT - t)
        out.append((t, n, 0))
        t += n
    return out


class Prog:
    def __init__(self, cfg):
        self.cfg = cfg
        self.nc = bass.Bass("TRN2", target_bir_lowering=False)
        self.stack = contextlib.ExitStack()

    def sb(self, st, name, shape, dt):
        self._uid = getattr(self, "_uid", 0) + 1
        name = f"{name}_{self._uid}"
        return T(st.enter_context(self.nc.sbuf_tensor(name, shape, dt)), name)

    def dram_in(self, name, shape, dt=F32):
        return self.nc.dram_tensor(name, list(shape), dt, kind="ExternalInput").ap()

    def dram(self, name, shape, dt):
        return self.nc.dram_tensor(name, list(shape), dt).ap()

    def load_w_bf16(self, dst, src_ap, nsplit):
        fw = self.fw
        a = src_ap.shape[1]
        step = (a + nsplit - 1) // nsplit
        for i in range(0, a, step):
            j = min(a, i + step)
            fw.dma("pool", dst[:, i:j, :], src_ap[:, i:j, :], writes=[dst.b])

    def rstd_from_sq(self, ps, n, F, rstd):
        fw = self.fw
        fw.op("act", lambda e: e.activation(rstd[:, :n], ps[:, :n], AF.Sqrt, bias=EPS, scale=1.0 / F),
              writes=[ps.b, rstd.b])
        fw.op("dve", lambda e: e.reciprocal(rstd[:, :n], rstd[:, :n]), reads=[rstd.b], writes=[rstd.b])

    def prenorm(self, x, n, A, Bv, h, ps, sq, rstd, tmp):
        fw = self.fw
        for c in range(DC):
            s = sq[c % 2]
            fw.op("act", lambda e: e.activation(s[:, :n], x[:, c, :n], AF.Square), reads=[x.b], writes=[s.b])
            fw.op("pe", lambda e: e.matmul(ps[:, :n], self.ones_bf[:, :], s[:, :n], start=(c == 0), stop=(c == DC - 1)),
                  reads=[s.b, self.ones_bf.b], writes=[ps.b])
        self.rstd_from_sq(ps, n, D, rstd)
        for c in range(DC):
            t = tmp[c % 2]
            fw.op("dve", lambda e: e.scalar_tensor_tensor(t[:, :n], x[:, c, :n], A[:, c:c + 1], rstd[:, :n], ALU.mult, ALU.mult),
                  reads=[x.b, rstd.b, self.sc.b], writes=[t.b])
            fw.op("act", lambda e: e.activation(h[:, c, :n], t[:, :n], AF.Identity, bias=Bv[:, c:c + 1], scale=1.0),
                  reads=[t.b, self.sc.b], writes=[h.b])

    def postnorm_residual(self, n, Cg, x, y, ps_stat, sq, rstd, tmp, mm_chunk, pbanks):
        fw = self.fw
        pending = None
        for c in range(DC):
            pb = pbanks[c % 2]
            mm_chunk(c, pb)
            fw.op("dve", lambda e: e.tensor_copy(y[:, c, :n], pb[:, :n]), writes=[pb.b, y.b])
            s = sq[c % 2]
            fw.op("act", lambda e: e.activation(s[:, :n], y[:, c, :n], AF.Square), reads=[y.b], writes=[s.b])
            if pending is not None:
                pc, psq = pending
                fw.op("pe", lambda e: e.matmul(ps_stat[:, :n], self.ones_bf[:, :], psq[:, :n], start=(pc == 0), stop=False),
                      reads=[psq.b], writes=[ps_stat.b])
            pending = (c, s)
        pc, psq = pending
        fw.op("pe", lambda e: e.matmul(ps_stat[:, :n], self.ones_bf[:, :], psq[:, :n], start=False, stop=True),
              reads=[psq.b], writes=[ps_stat.b])
        self.rstd_from_sq(ps_stat, n, D, rstd)
        for c in range(DC):
            t = tmp[c % 2]
            fw.op("dve", lambda e: e.scalar_tensor_tensor(t[:, :n], y[:, c, :n], Cg[:, c:c + 1], rstd[:, :n], ALU.mult, ALU.mult),
                  reads=[y.b, rstd.b, self.sc.b], writes=[t.b])
            fw.op("pool", lambda e: e.tensor_tensor(x[:, c, :n], x[:, c, :n], t[:, :n], ALU.add),
                  reads=[x.b, t.b], writes=[x.b])

    def scv(self, l, slot, s):
        return self.sc[:, l, slot, :, s]

    def phase_ada(self):
        fw, nc, cfg = self.fw, self.nc, self.cfg
        with contextlib.ExitStack() as st:
            csb = self.sb(st, "csb", [P, DC, 2], F32)
            sil = self.sb(st, "sil", [P, DC, 2], F32)
            adab = self.sb(st, "adab", [P, cfg.DEPTH, 48], F32)
            ng = self.sb(st, "ng", [P, cfg.DEPTH, 4, DC], F32)
            mod = self.sb(st, "mod", [P, cfg.DEPTH, 48, 2], F32)
            wbuf = [self.sb(st, f"adaw{i}", [P, DC, 1024], F32) for i in range(2)]
            fw.dma("sp", csb[:, :, :], self.in_cT, writes=[csb.b])
            fw.dma("sp", adab[:, :, :], self.in_adab, writes=[adab.b])
            fw.dma("sp", ng[:, :, :, :], self.in_ng, writes=[ng.b])
            fw.op("act", lambda e: e.activation(sil[:, :, :], csb[:, :, :], AF.Silu), reads=[csb.b], writes=[sil.b])
            ps = self.PS[0]
            it = 0
            for l in range(cfg.DEPTH):
                wv = self.in_adaw[l].rearrange("(c p) m -> p c m", p=P)
                for g6 in range(6):
                    wb = wbuf[it % 2]
                    it += 1
                    for hh in range(2):
                        fw.dma("sp", wb[:, hh * 4:(hh + 1) * 4, :], wv[:, hh * 4:(hh + 1) * 4, g6 * 1024:(g6 + 1) * 1024], writes=[wb.b])
                    for j in range(8):
                        jj = g6 * 8 + j
                        for k in range(DC):
                            last = (k == DC - 1)
                            f = lambda e: e.matmul(ps[:, jj * 2:jj * 2 + 2], wb[:, k, j * 128:(j + 1) * 128], sil[:, k, :],
                                                   start=(k == 0), stop=last)
                            if last:
                                fw.op("pe", f, reads=[wb.b, sil.b], writes=[ps.b])
                            else:
                                fw._deps("pe", [wb.b, sil.b], [ps.b])
                                f(fw.eng["pe"])
                psv = ps[:, 0:96].rearrange("p (j s) -> p j s", s=2)
                for s in range(2):
                    fw.op("dve", lambda e: e.tensor_tensor(mod[:, l, :, s], psv[:, :, s], adab[:, l, :], ALU.add),
                          reads=[adab.b], writes=[ps.b, mod.b])
                for s in range(2):
                    m = lambda j: mod[:, l, j * 8:(j + 1) * 8, s]
                    sc = self.sc
                    fw.op("dve", lambda e: e.scalar_tensor_tensor(sc[:, l, 0, :, s], m(1), 1.0, ng[:, l, 0, :], ALU.add, ALU.mult),
                          reads=[mod.b, ng.b], writes=[sc.b])
                    fw.op("dve", lambda e: e.tensor_copy(sc[:, l, 1, :, s], m(0)), reads=[mod.b], writes=[sc.b])
                    fw.op("dve", lambda e: e.tensor_tensor(sc[:, l, 2, :, s], m(2), ng[:, l, 1, :], ALU.mult),
                          reads=[mod.b, ng.b], writes=[sc.b])
                    fw.op("dve", lambda e: e.scalar_tensor_tensor(sc[:, l, 3, :, s], m(4), 1.0, ng[:, l, 2, :], ALU.add, ALU.mult),
                          reads=[mod.b, ng.b], writes=[sc.b])
                    fw.op("dve", lambda e: e.tensor_copy(sc[:, l, 4, :, s], m(3)), reads=[mod.b], writes=[sc.b])
                    fw.op("dve", lambda e: e.tensor_tensor(sc[:, l, 5, :, s], m(5), ng[:, l, 3, :], ALU.mult),
                          reads=[mod.b, ng.b], writes=[sc.b])
            fw.barrier()

    def phase_mlp(self, l, x_src, x_dst):
        fw, nc, cfg = self.fw, self.nc, self.cfg
        TT = 256
        with contextlib.ExitStack() as st:
            w1 = self.sb(st, "w1", [P, DC, DFF], BF16)
            w2 = self.sb(st, "w2", [P, DFF // P, D], BF16)
            xs = [self.sb(st, f"mx{i}", [P, DC, TT], F32) for i in range(2)]
            h = self.sb(st, "mh", [P, DC, TT], BF16)
            hid = self.sb(st, "mhid", [P, DFF // P, TT], BF16)
            y = self.sb(st, "my", [P, DC, TT], F32)
            rl = [self.sb(st, f"mrl{i}", [P, TT], F32) for i in range(2)]
            sq = [self.sb(st, f"msq{i}", [P, TT], BF16) for i in range(2)]
            tmp = [self.sb(st, f"mtmp{i}", [P, TT], F32) for i in range(2)]
            rstd = self.sb(st, "mrstd", [P, TT], F32)
            self.load_w_bf16(w1, self.in_w1[l].rearrange("(c p) m -> p c m", p=P), 4)
            self.load_w_bf16(w2, self.in_w2[l].rearrange("(c p) m -> p c m", p=P), 8)
            tiles = tok_tiles(cfg, TT)
            xv_src = x_src.rearrange("(c p) t -> p c t", p=P)
            xv_dst = x_dst.rearrange("(c p) t -> p c t", p=P)
            PS = self.PS

            def load(i):
                t0, n, s = tiles[i]
                xb = xs[i % 2]
                fw.dma("sp", xb[:, :, :n], xv_src[:, :, t0:t0 + n], writes=[xb.b])

            load(0)
            for i, (t0, n, s) in enumerate(tiles):
                if i + 1 < len(tiles):
                    load(i + 1)
                x = xs[i % 2]
                self.prenorm(x, n, self.scv(l, 3, s), self.scv(l, 4, s), h, PS[4], sq, rstd, tmp)
                for m in range(DFF // P):
                    pb = PS[m % 2]
                    for k in range(DC):
                        f = lambda e: e.matmul(pb[:, :n], w1[:, k, m * P:(m + 1) * P], h[:, k, :n], start=(k == 0), stop=(k == DC - 1))
                        if k == DC - 1:
                            fw.op("pe", f, reads=[w1.b, h.b], writes=[pb.b])
                        else:
                            fw._deps("pe", [w1.b, h.b], [pb.b])
                            f(fw.eng["pe"])
                    r = rl[m % 2]
                    fw.op("act", lambda e: e.activation(r[:, :n], pb[:, :n], AF.Relu), writes=[pb.b, r.b])
                    fw.op("pool", lambda e: e.tensor_tensor(hid[:, m, :n], r[:, :n], r[:, :n], ALU.mult), reads=[r.b], writes=[hid.b])

                def mm_chunk(c, pb):
                    KC = DFF // P
                    for k in range(KC):
                        f = lambda e: e.matmul(pb[:, :n], w2[:, k, c * P:(c + 1) * P], hid[:, k, :n], start=(k == 0), stop=(k == KC - 1))
                        if k == KC - 1:
                            fw.op("pe", f, reads=[w2.b, hid.b], writes=[pb.b])
                        else:
                            fw._deps("pe", [w2.b, hid.b], [pb.b])
                            f(fw.eng["pe"])

                self.postnorm_residual(n, self.scv(l, 5, s), x, y, PS[5], sq, rstd, tmp, mm_chunk, [PS[2], PS[3]])
                fw.dma("pool", xv_dst[:, :, t0:t0 + n], x[:, :, :n], reads=[x.b])
            fw.barrier()

    def mm(self, pb, out_ap, pairs, reads):
        fw = self.fw
        last = len(pairs) - 1
        for i, (l_, r_) in enumerate(pairs):
            f = lambda e: e.matmul(out_ap, l_, r_, start=(i == 0), stop=(i == last))
            if i == last:
                fw.op("pe", f, reads=reads, writes=[pb.b])
            else:
                fw._deps("pe", reads, [pb.b])
                f(fw.eng["pe"])

    def phase_mla1(self, l, j, x_src):
        fw, nc, cfg = self.fw, self.nc, self.cfg
        TT = 512
        PS = self.PS
        with contextlib.ExitStack() as st:
            wdq = self.sb(st, "wdq", [P, DC, 384], BF16)
            wuqn = self.sb(st, "wuqn", [P, 3, 8, 128], BF16)
            wuqr = self.sb(st, "wuqr", [P, 3, 8, 128], BF16)
            wdkc = self.sb(st, "wdkc", [P, DC, 256], BF16)
            wdkr = self.sb(st, "wdkr", [P, DC, 128], BF16)
            wukk = self.sb(st, "wukk", [P, 2, 8, 128], BF16)
            wukv = self.sb(st, "wukv", [P, 2, 8, 128], BF16)
            foldb = self.sb(st, "foldb", [P, P], BF16)
            gsc = self.sb(st, "gsc", [P, 5], F32)
            xs = [self.sb(st, f"ax{i}", [P, DC, TT], F32) for i in range(2)]
            cs = [self.sb(st, f"acs{i}", [P, TT], F32) for i in range(2)]
            h = self.sb(st, "ah", [P, DC, TT], BF16)
            sq = [self.sb(st, f"asq{i}", [P, TT], BF16) for i in range(2)]
            tmp = [self.sb(st, f"atmp{i}", [P, TT], F32) for i in range(2)]
            rstd = self.sb(st, "arstd", [P, TT], F32)
            cqf = self.sb(st, "acqf", [P, 3, TT], F32)
            cqb = self.sb(st, "acqb", [P, 3, TT], BF16)
            ckf = self.sb(st, "ackf", [P, 2, TT], F32)
            ckb = self.sb(st, "ackb", [P, 2, TT], BF16)
            kru = self.sb(st, "akru", [P, TT], BF16)
            ob = [self.sb(st, f"aob{i}", [P, TT], BF16) for i in range(4)]
            vsb = [self.sb(st, f"avsb{i}", [P, 1024], BF16) for i in range(2)]

            fw.dma("pool", wdq[:, :, :], self.in_wdq[j].rearrange("(c p) m -> p c m", p=P), writes=[wdq.b])
            uqv = self.in_wuq[j].rearrange("(c p) (h e) -> p c h e", p=P, e=192)
            for c in range(3):
                fw.dma("pool", wuqn[:, c, :, :], uqv[:, c, :, 0:128], writes=[wuqn.b])
                fw.dma("pool", wuqr[:, c, :, 0:64], uqv[:, c, :, 128:192], writes=[wuqr.b])
            dkv = self.in_wdkv[j].rearrange("(c p) m -> p c m", p=P)
            fw.dma("pool", wdkc[:, :, :], dkv[:, :, 0:256], writes=[wdkc.b])
            fw.dma("pool", wdkr[:, :, 0:64], dkv[:, :, 256:320], writes=[wdkr.b])
            ukv = self.in_wukv[j].rearrange("(c p) (h e) -> p c h e", p=P, e=256)
            for c in range(2):
                fw.dma("pool", wukk[:, c, :, :], ukv[:, c, :, 0:128], writes=[wukk.b])
                fw.dma("pool", wukv[:, c, :, :], ukv[:, c, :, 128:256], writes=[wukv.b])
            fw.dma("sp", gsc[:, :], self.in_mlag[:, j, :], writes=[gsc.b])
            fw.op("dve", lambda e: e.tensor_copy(foldb[:, :], self.cst[:, 0, :]), reads=[self.cst.b], writes=[foldb.b])
            for (wt, nd) in ((wuqr, 4), (wdkr, 3)):
                for (d0, s0, sg) in ((64, 16, -1.0), (80, 0, 1.0), (96, 48, -1.0), (112, 32, 1.0)):
                    if nd == 4:
                        o_, i_ = wt[:, :, :, d0:d0 + 16], wt[:, :, :, s0:s0 + 16]
                    else:
                        o_, i_ = wt[:, :, d0:d0 + 16], wt[:, :, s0:s0 + 16]
                    fw.op("dve", lambda e: e.tensor_scalar_mul(o_, i_, sg), reads=[wt.b], writes=[wt.b])

            tiles = tok_tiles(cfg, TT)
            xv = x_src.rearrange("(c p) t -> p c t", p=P)

            def load(i):
                t0, n, s = tiles[i]
                fw.dma("sp", xs[i % 2][:, :, :n], xv[:, :, t0:t0 + n], writes=[xs[i % 2].b])
                fw.dma("sp", cs[i % 2][:, :n], self.in_rope[:, t0:t0 + n], writes=[cs[i % 2].b])

            load(0)
            nob = 0
            for i, (t0, n, s) in enumerate(tiles):
                if i + 1 < len(tiles):
                    load(i + 1)
                x, csb = xs[i % 2], cs[i % 2]
                self.prenorm(x, n, self.scv(l, 0, s), self.scv(l, 1, s), h, PS[4], sq, rstd, tmp)
                for m in range(3):
                    pb = PS[m % 2]
                    self.mm(pb, pb[:, :n], [(wdq[:, k, m * P:(m + 1) * P], h[:, k, :n]) for k in range(DC)], [wdq.b, h.b])
                    fw.op("dve", lambda e: e.tensor_copy(cqf[:, m, :n], pb[:, :n]), writes=[pb.b, cqf.b])
                    sqq = sq[m % 2]
                    fw.op("act", lambda e: e.activation(sqq[:, :n], cqf[:, m, :n], AF.Square), reads=[cqf.b], writes=[sqq.b])
                    fw.op("pe", lambda e: e.matmul(PS[5][:, :n], self.ones_bf[:, :], sqq[:, :n], start=(m == 0), stop=(m == 2)),
                          reads=[sqq.b], writes=[PS[5].b])
                self.rstd_from_sq(PS[5], n, 384, rstd)
                for m in range(3):
                    fw.op("dve", lambda e: e.scalar_tensor_tensor(cqb[:, m, :n], cqf[:, m, :n], gsc[:, m:m + 1], rstd[:, :n], ALU.mult, ALU.mult),
                          reads=[cqf.b, gsc.b, rstd.b], writes=[cqb.b])
                for m in range(2):
                    pb = PS[m % 2]
                    self.mm(pb, pb[:, :n], [(wdkc[:, k, m * P:(m + 1) * P], h[:, k, :n]) for k in range(DC)], [wdkc.b, h.b])
                    fw.op("dve", lambda e: e.tensor_copy(ckf[:, m, :n], pb[:, :n]), writes=[pb.b, ckf.b])
                    sqq = sq[m % 2]
                    fw.op("act", lambda e: e.activation(sqq[:, :n], ckf[:, m, :n], AF.Square), reads=[ckf.b], writes=[sqq.b])
                    fw.op("pe", lambda e: e.matmul(PS[5][:, :n], self.ones_bf[:, :], sqq[:, :n], start=(m == 0), stop=(m == 1)),
                          reads=[sqq.b], writes=[PS[5].b])
                self.rstd_from_sq(PS[5], n, 256, rstd)
                for m in range(2):
                    fw.op("dve", lambda e: e.scalar_tensor_tensor(ckb[:, m, :n], ckf[:, m, :n], gsc[:, 3 + m:4 + m], rstd[:, :n], ALU.mult, ALU.mult),
                          reads=[ckf.b, gsc.b, rstd.b], writes=[ckb.b])
                pb = PS[2]
                self.mm(pb, pb[:, :n], [(wdkr[:, k, :], h[:, k, :n]) for k in range(DC)], [wdkr.b, h.b])
                fw.op("dve", lambda e: e.tensor_tensor(kru[:, :n], pb[:, :n], csb[:, :n], ALU.mult), reads=[csb.b], writes=[pb.b, kru.b])
                pb = PS[3]
                self.mm(pb, pb[:, :n], [(foldb[:, :], kru[:, :n])], [foldb.b, kru.b])
                o_ = ob[nob % 4]; nob += 1
                fw.op("act", lambda e: e.activation(o_[:, :n], pb[:, :n], AF.Copy), writes=[pb.b, o_.b])
                fw.dma("pool", self.KrT_own[:, t0:t0 + n], o_[:, :n], reads=[o_.b])
                for hh in range(MLA_H):
                    pb = PS[0]
                    self.mm(pb, pb[:, :n], [(wuqn[:, k, hh, :], cqb[:, k, :n]) for k in range(3)], [wuqn.b, cqb.b])
                    o_ = ob[nob % 4]; nob += 1
                    fw.op("act", lambda e: e.activation(o_[:, :n], pb[:, :n], AF.Copy), writes=[pb.b, o_.b])
                    fw.dma("pool", self.qnT[hh * P:(hh + 1) * P, t0:t0 + n], o_[:, :n], reads=[o_.b])
                    pb = PS[1]
                    self.mm(pb, pb[:, :n], [(wuqr[:, k, hh, :], cqb[:, k, :n]) for k in range(3)], [wuqr.b, cqb.b])
                    o_ = ob[nob % 4]; nob += 1
                    fw.op("dve", lambda e: e.tensor_tensor(o_[:, :n], pb[:, :n], csb[:, :n], ALU.mult), reads=[csb.b], writes=[pb.b, o_.b])
                    fw.dma("pool", self.qrT[hh * P:(hh + 1) * P, t0:t0 + n], o_[:, :n], reads=[o_.b])
                    pb = PS[2]
                    self.mm(pb, pb[:, :n], [(wukk[:, k, hh, :], ckb[:, k, :n]) for k in range(2)], [wukk.b, ckb.b])
                    o_ = ob[nob % 4]; nob += 1
                    fw.op("act", lambda e: e.activation(o_[:, :n], pb[:, :n], AF.Copy), writes=[pb.b, o_.b])
                    fw.dma("pool", self.KnT_own[hh * P:(hh + 1) * P, t0:t0 + n], o_[:, :n], reads=[o_.b])
                for sidx in range(n // P):
                    vs_ = vsb[sidx % 2]
                    for half in range(2):
                        pb = PS[6 + half]
                        self.mm(pb, pb[:, :], [(ckb[:, k, sidx * P:(sidx + 1) * P], wukv[:, k, half * 4:(half + 1) * 4, :]) for k in range(2)],
                                [wukv.b, ckb.b])
                        if half == 0:
                            fw.op("act", lambda e: e.activation(vs_[:, 0:512], pb[:, :], AF.Copy), writes=[pb.b, vs_.b])
                        else:
                            fw.op("dve", lambda e: e.tensor_copy(vs_[:, 512:1024], pb[:, :]), writes=[pb.b, vs_.b])
                    fw.dma("pool", self.V_own[t0 + sidx * P:t0 + (sidx + 1) * P, :], vs_[:, :], reads=[vs_.b])
            fw.barrier()

    def phase_mla2(self):
        fw, nc, cfg = self.fw, self.nc, self.cfg
        PS = self.PS
        NT, NL, NKEY = cfg.NT, cfg.NL, cfg.NKEY
        NKT = NKEY // P
        SCALE = 192.0 ** -0.5
        TQ = 512
        with contextlib.ExitStack() as st:
            Kn = [self.sb(st, f"bKn{i}", [P, NKEY], BF16) for i in range(2)]
            Vh = [self.sb(st, f"bVh{i}", [P, NKT, P], BF16) for i in range(2)]
            Kr = self.sb(st, "bKr", [P, NKEY], BF16)
            qn = [self.sb(st, f"bqn{i}", [P, TQ], BF16) for i in range(2)]
            qr = [self.sb(st, f"bqr{i}", [P, TQ], BF16) for i in range(2)]
            Pt = [self.sb(st, f"bPt{i}", [P, 2, TQ], BF16) for i in range(2)]
            acc = self.sb(st, "bacc", [P, 2, TQ], F32)
            accs = self.sb(st, "baccs", [P, TQ], F32)
            rec = self.sb(st, "brec", [P, TQ], F32)
            obf = [self.sb(st, f"bobf{i}", [P, TQ], BF16) for i in range(2)]
            for half in range(2):
                hp = slice(half * 64, (half + 1) * 64)
                fw.dma("sp", Kr[hp, 0:NT], self.Kr_all[half, 0:64, :], writes=[Kr.b])
                fw.dma("sp", Kr[hp, NT:NKEY], self.Kr_all[half, 64:128, CTX:NT], writes=[Kr.b])

            def loadhead(hh):
                kb, vb = Kn[hh % 2], Vh[hh % 2]
                for half in range(2):
                    hp = slice(half * 64, (half + 1) * 64)
                    c = 2 * hh + half
                    fw.dma("sp", kb[hp, 0:NT], self.Kn_all[c, 0:64, :], writes=[kb.b])
                    fw.dma("sp", kb[hp, NT:NKEY], self.Kn_all[c, 64:128, CTX:NT], writes=[kb.b])
                hc = slice(hh * P, (hh + 1) * P)
                for c in range(NT // 256):
                    v0 = self.V_all[c, 0:256, hc].rearrange("(t p) d -> p t d", p=P)
                    fw.dma("sp", vb[:, 2 * c:2 * c + 2, :], v0, writes=[vb.b])
                    if c >= 1:
                        v1 = self.V_all[c, 256:512, hc].rearrange("(t p) d -> p t d", p=P)
                        k0 = NT // P + 2 * (c - 1)
                        fw.dma("sp", vb[:, k0:k0 + 2, :], v1, writes=[vb.b])

            qtiles = tok_tiles(cfg, TQ)
            loadhead(0)
            it = 0
            for hh in range(MLA_H):
                if hh + 1 < MLA_H:
                    loadhead(hh + 1)
                kb, vb = Kn[hh % 2], Vh[hh % 2]
                for (t0, n, s) in qtiles:
                    qn_, qr_ = qn[it % 2], qr[it % 2]
                    o_ = obf[it % 2]
                    Ob = PS[4 + it % 2]
                    it += 1
                    fw.dma("sp", qn_[:, :n], self.qnT[hh * P:(hh + 1) * P, t0:t0 + n], writes=[qn_.b])
                    fw.dma("sp", qr_[:, :n], self.qrT[hh * P:(hh + 1) * P, t0:t0 + n], writes=[qr_.b])
                    nkt = (CTX // P) if s == 1 else NKT
                    npair = nkt // 2

                    def scores(pi):
                        for jj in range(2):
                            kt = 2 * pi + jj
                            pb = PS[2 * (pi % 2) + jj]
                            self.mm(pb, pb[:, :n], [(kb[:, kt * P:(kt + 1) * P], qn_[:, :n]), (Kr[:, kt * P:(kt + 1) * P], qr_[:, :n])],
                                    [kb.b, Kr.b, qn_.b, qr_.b])

                    scores(0)
                    for pi in range(npair):
                        if pi + 1 < npair:
                            scores(pi + 1)
                        S2 = self.PS2[pi % 2][:, :].rearrange("p (a b) -> p a b", a=2)
                        pt = Pt[pi % 2]
                        fw.op("act", lambda e: e.activation(pt[:, :, :n], S2[:, :, :n], AF.Exp, scale=SCALE),
                              writes=[PS[2 * (pi % 2)].b, PS[2 * (pi % 2) + 1].b, pt.b])
                        if pi == 0:
                            fw.op("dve", lambda e: e.tensor_copy(acc[:, :, :n], pt[:, :, :n]), reads=[pt.b], writes=[acc.b])
                        else:
                            fw.op("dve", lambda e: e.tensor_tensor(acc[:, :, :n], acc[:, :, :n], pt[:, :, :n], ALU.add), reads=[pt.b], writes=[acc.b])
                        for jj in range(2):
                            kt = 2 * pi + jj
                            f = lambda e: e.matmul(Ob[:, :n], vb[:, kt, :], pt[:, jj, :n], start=(kt == 0), stop=(kt == nkt - 1))
                            if jj == 1:
                                fw.op("pe", f, reads=[vb.b, pt.b], writes=[Ob.b])
                            else:
                                fw._deps("pe", [vb.b, pt.b], [Ob.b])
                                f(fw.eng["pe"])
                    fw.op("dve", lambda e: e.tensor_tensor(accs[:, :n], acc[:, 0, :n], acc[:, 1, :n], ALU.add), reads=[acc.b], writes=[accs.b])
                    fw.op("pe", lambda e: e.matmul(PS[6][:, :n], self.ones_f[:, :], accs[:, :n], start=True, stop=True),
                          reads=[accs.b, self.ones_f.b], writes=[PS[6].b])
                    fw.op("dve", lambda e: e.reciprocal(rec[:, :n], PS[6][:, :n]), writes=[PS[6].b, rec.b])
                    fw.op("dve", lambda e: e.tensor_tensor(o_[:, :n], Ob[:, :n], rec[:, :n], ALU.mult), reads=[rec.b], writes=[Ob.b, o_.b])
                    fw.dma("pool", self.oT[hh * P:(hh + 1) * P, t0:t0 + n], o_[:, :n], reads=[o_.b])
            fw.barrier()

    def phase_oproj(self, l, w_src, x_src, x_dst, gla_j=None):
        fw, nc, cfg = self.fw, self.nc, self.cfg
        PS = self.PS
        TT = 512
        with contextlib.ExitStack() as st:
            wo = self.sb(st, "wo", [P, DC, D], BF16)
            xs = [self.sb(st, f"ox{i}", [P, DC, TT], F32) for i in range(2)]
            os_ = [self.sb(st, f"oo{i}", [P, DC, TT], BF16) for i in range(2)]
            y = self.sb(st, "oy", [P, DC, TT], F32)
            sq = [self.sb(st, f"osq{i}", [P, TT], BF16) for i in range(2)]
            tmp = [self.sb(st, f"otmp{i}", [P, TT], F32) for i in range(2)]
            rstd = self.sb(st, "orstd", [P, TT], F32)
            if gla_j is not None:
                of = [self.sb(st, f"oof{i}", [P, DC, TT], F32) for i in range(2)]
                rs = [self.sb(st, f"ors{i}", [P, DC, TT], BF16) for i in range(2)]
                go = self.sb(st, "ogo", [P, 2], F32)
                fw.dma("sp", go[:, :], self.in_glago[:, gla_j, :], writes=[go.b])
            self.load_w_bf16(wo, w_src.rearrange("(c p) m -> p c m", p=P), 2)
            tiles = tok_tiles(cfg, TT)
            xv_src = x_src.rearrange("(c p) t -> p c t", p=P)
            xv_dst = x_dst.rearrange("(c p) t -> p c t", p=P)

            def load(i):
                t0, n, s = tiles[i]
                fw.dma("sp", xs[i % 2][:, :, :n], xv_src[:, :, t0:t0 + n], writes=[xs[i % 2].b])
                if gla_j is None:
                    fw.dma("sp", os_[i % 2][:, :, :n], self.oT.rearrange("(c p) t -> p c t", p=P)[:, :, t0:t0 + n], writes=[os_[i % 2].b])
                else:
                    fw.dma("sp", of[i % 2][:, :, :n], self.goT.rearrange("(c p) t -> p c t", p=P)[:, :, t0:t0 + n], writes=[of[i % 2].b])
                    fw.dma("sp", rs[i % 2][:, :, :n], self.grT.rearrange("(c p) t -> p c t", p=P)[:, :, t0:t0 + n], writes=[rs[i % 2].b])

            load(0)
            for i, (t0, n, s) in enumerate(tiles):
                if i + 1 < len(tiles):
                    load(i + 1)
                x, o_ = xs[i % 2], os_[i % 2]
                if gla_j is not None:
                    of_, rs_ = of[i % 2], rs[i % 2]
                    for hd in range(GLA_H):
                        for half in range(2):
                            c = hd * 2 + half
                            sqq = sq[half]
                            fw.op("act", lambda e: e.activation(sqq[:, :n], of_[:, c, :n], AF.Square), reads=[of_.b], writes=[sqq.b])
                            fw.op("pe", lambda e: e.matmul(PS[4][:, :n], self.ones_bf[:, :], sqq[:, :n], start=(half == 0), stop=(half == 1)),
                                  reads=[sqq.b], writes=[PS[4].b])
                        self.rstd_from_sq(PS[4], n, 256, rstd)
                        for half in range(2):
                            c = hd * 2 + half
                            t = tmp[half]
                            fw.op("dve", lambda e: e.scalar_tensor_tensor(t[:, :n], of_[:, c, :n], go[:, half:half + 1], rstd[:, :n], ALU.mult, ALU.mult),
                                  reads=[of_.b, go.b, rstd.b], writes=[t.b])
                            fw.op("pool", lambda e: e.tensor_tensor(o_[:, c, :n], t[:, :n], rs_[:, c, :n], ALU.mult), reads=[t.b, rs_.b], writes=[o_.b])

                def mm_chunk(c, pb):
                    self.mm(pb, pb[:, :n], [(wo[:, k, c * P:(c + 1) * P], o_[:, k, :n]) for k in range(DC)], [wo.b, o_.b])

                self.postnorm_residual(n, self.scv(l, 2, s), x, y, PS[5], sq, rstd, tmp, mm_chunk, [PS[2], PS[3]])
                fw.dma("pool", xv_dst[:, :, t0:t0 + n], x[:, :, :n], reads=[x.b])
            fw.barrier()

    def phase_gla1(self, l, j, x_src):
        fw, nc, cfg = self.fw, self.nc, self.cfg
        TT = 512
        PS = self.PS
        with contextlib.ExitStack() as st:
            wq = self.sb(st, "gwq", [P, DC, 512], BF16)
            wk = self.sb(st, "gwk", [P, DC, 512], BF16)
            wv = self.sb(st, "gwv", [P, DC, 1024], BF16)
            wr = self.sb(st, "gwr", [P, DC, 1024], BF16)
            wg1 = self.sb(st, "gwg1", [P, DC, 2, 16], BF16)
            w2a = self.sb(st, "gw2a", [32, 2, 512], BF16)
            xs = [self.sb(st, f"gx{i}", [P, DC, TT], F32) for i in range(2)]
            h = self.sb(st, "gh", [P, DC, TT], BF16)
            sq = [self.sb(st, f"gsq{i}", [P, TT], BF16) for i in range(2)]
            tmp = [self.sb(st, f"gtmp{i}", [P, TT], F32) for i in range(2)]
            rstd = self.sb(st, "grstd", [P, TT], F32)
            low = [self.sb(st, f"glow{i}", [32, TT], BF16) for i in range(2)]
            of32 = [self.sb(st, f"gof{i}", [P, 512], F32) for i in range(4)]
            obf = [self.sb(st, f"gob{i}", [P, 1024], BF16) for i in range(2)]
            robf = [self.sb(st, f"grob{i}", [P, DC, TT], BF16) for i in range(2)]
            ex = [self.sb(st, f"gex{i}", [P, 512], F32) for i in range(2)]
            for (w_, src) in ((wq, self.in_gwq), (wk, self.in_gwk), (wv, self.in_gwv), (wr, self.in_gwr)):
                self.load_w_bf16(w_, src[j].rearrange("(c p) m -> p c m", p=P), 2)
            for d_ in range(2):
                fw.dma("pool", wg1[:, :, d_, :], self.in_gw1[j, d_].rearrange("(c p) m -> p c m", p=P), writes=[wg1.b])
                fw.dma("pool", w2a[0:16, d_, :], self.in_gw2[j, d_], writes=[w2a.b])
                fw.dma("pool", w2a[16:17, d_, :], self.in_gbg[j, d_:d_ + 1, :], writes=[w2a.b])
                fw.op("dve", lambda e: e.memset(low[d_][:, :], 1.0), writes=[low[d_].b])
            tiles = tok_tiles(cfg, TT)
            xv = x_src.rearrange("(c p) t -> p c t", p=P)

            def load(i):
                t0, n, s = tiles[i]
                fw.dma("sp", xs[i % 2][:, :, :n], xv[:, :, t0:t0 + n], writes=[xs[i % 2].b])

            load(0)
            nf = 0
            nb = 0
            gout = [self.ggA, self.ggB]
            for i, (t0, n, s) in enumerate(tiles):
                if i + 1 < len(tiles):
                    load(i + 1)
                x = xs[i % 2]
                self.prenorm(x, n, self.scv(l, 0, s), self.scv(l, 1, s), h, PS[4], sq, rstd, tmp)
                for (w_, dst, scl) in ((wq, self.gqT, 128.0 ** -0.5), (wk, self.gkT, 1.0)):
                    for hd in range(GLA_H):
                        pb = PS[hd % 2]
                        self.mm(pb, pb[:, :n], [(w_[:, k, hd * P:(hd + 1) * P], h[:, k, :n]) for k in range(DC)], [w_.b, h.b])
                        o_ = of32[nf % 4]; nf += 1
                        fw.op("act", lambda e: e.activation(o_[:, :n], pb[:, :n], AF.Copy, scale=scl), writes=[pb.b, o_.b])
                        fw.dma("pool", dst[hd * P:(hd + 1) * P, t0:t0 + n], o_[:, :n], reads=[o_.b])
                ob_ = robf[i % 2]
                for c in range(DC):
                    pb = PS[c % 2]
                    self.mm(pb, pb[:, :n], [(wr[:, k, c * P:(c + 1) * P], h[:, k, :n]) for k in range(DC)], [wr.b, h.b])
                    fw.op("act", lambda e: e.activation(ob_[:, c, :n], pb[:, :n], AF.Silu), writes=[pb.b, ob_.b])
                fw.dma("pool", self.grT.rearrange("(c p) t -> p c t", p=P)[:, :, t0:t0 + n], ob_[:, :, :n], reads=[ob_.b])
                for d_ in range(2):
                    pb = PS[2 + d_]
                    self.mm(pb, pb[0:16, :n], [(wg1[:, k, d_, :], h[:, k, :n]) for k in range(DC)], [wg1.b, h.b])
                    fw.op("dve", lambda e: e.tensor_copy(low[d_][0:16, :n], pb[0:16, :n]), writes=[pb.b, low[d_].b])
                for sidx in range(n // P):
                    tsl = slice(sidx * P, (sidx + 1) * P)
                    r0 = t0 + sidx * P
                    pb = PS[6]
                    self.mm(pb, pb[:, :], [(h[:, k, tsl], wk[:, k, :]) for k in range(DC)], [wk.b, h.b])
                    o_ = of32[nf % 4]; nf += 1
                    fw.op("dve", lambda e: e.tensor_copy(o_[:, :], pb[:, :]), writes=[pb.b, o_.b])
                    fw.dma("pool", self.gk[r0:r0 + P, :], o_[:, :], reads=[o_.b])
                    ob_ = obf[nb % 2]; nb += 1
                    for half in range(2):
                        pb = PS[6 + half]
                        self.mm(pb, pb[:, :], [(h[:, k, tsl], wv[:, k, half * 512:(half + 1) * 512]) for k in range(DC)], [wv.b, h.b])
                        if half == 0:
                            fw.op("act", lambda e: e.activation(ob_[:, 0:512], pb[:, :], AF.Copy), writes=[pb.b, ob_.b])
                        else:
                            fw.op("dve", lambda e: e.tensor_copy(ob_[:, 512:1024], pb[:, :]), writes=[pb.b, ob_.b])
                    fw.dma("pool", self.gv[r0:r0 + P, :], ob_[:, :], reads=[ob_.b])
                    for d_ in range(2):
                        pb = PS[d_]
                        self.mm(pb, pb[:, :], [(low[d_][0:17, tsl], w2a[0:17, d_, :])], [low[d_].b, w2a.b])
                        e_ = ex[d_]
                        fw.op("act", lambda e: e.activation(e_[:, :], pb[:, :], AF.Exp, scale=-1.0), writes=[pb.b, e_.b])
                        o_ = of32[nf % 4]; nf += 1
                        fw.op("act", lambda e: e.activation(o_[:, :], e_[:, :], AF.Ln, bias=1.0, scale=1.0), reads=[e_.b], writes=[o_.b])
                        fw.dma("pool", gout[d_][r0:r0 + P, :], o_[:, :], reads=[o_.b])
            fw.barrier()

    def phase_gla_scan(self, which):
        fw, nc, cfg = self.fw, self.nc, self.cfg
        PS = self.PS
        NTL = cfg.NT // P
        NCT = CTX // P
        A = (which == "A")
        cM, cU, cMask = (1, 2, 3) if A else (4, 5, 6)
        gsrc = self.ggA if A else self.ggB
        with contextlib.ExitStack() as st:
            S = [self.sb(st, f"sS{hd}", [P, 256], F32) for hd in range(GLA_H)]
            Sbf = [[self.sb(st, f"sSbf{i}_{hd}", [P, 256], BF16) for hd in range(GLA_H)] for i in range(2)]
            qT = [self.sb(st, f"sq{i}", [P, GLA_H, P], F32) for i in range(2)]
            kT = [self.sb(st, f"sk{i}", [P, GLA_H, P], F32) for i in range(2)]
            kk = [self.sb(st, f"skk{i}", [P, 512], F32) for i in range(2)]
            vv = [self.sb(st, f"sv{i}", [P, 1024], BF16) for i in range(2)]
            gg = [self.sb(st, f"sg{i}", [P, 512], F32) for i in range(2)]
            oa = [self.sb(st, f"soa{i}", [P, DC, P], F32) for i in range(2)]
            eG = [self.sb(st, f"seG{i}", [P, P], F32) for i in range(4)]
            enG = [self.sb(st, f"senG{i}", [P, P], F32) for i in range(4)]
            eE = [self.sb(st, f"seE{i}", [P, P], F32) for i in range(4)]
            qt = [self.sb(st, f"sqt{i}", [P, P], BF16) for i in range(4)]
            kt = [self.sb(st, f"skt{i}", [P, P], BF16) for i in range(4)]
            ke = [self.sb(st, f"ske{i}", [P, P], BF16) for i in range(4)]
            ab = [self.sb(st, f"sab{i}", [P, P], BF16) for i in range(4)]
            osb = [self.sb(st, f"sos{i}", [P, 2, P], F32) for i in range(4)]
            cst = self.cst
            if A:
                for hd in range(GLA_H):
                    fw.op("dve", lambda e: e.memset(S[hd][:, :], 0.0), writes=[S[hd].b])
            else:
                s0 = self.sb(st, "ss0", [P, GLA_H, 256], F32)
                s1 = self.sb(st, "ss1", [P, GLA_H, 256], F32)
                sel = self.sb(st, "ssel", [P, 2], F32)
                fw.dma("sp", sel[:, :], self.in_sel, writes=[sel.b])
                ev = self.exch_all.rearrange("(r h p) v -> r p h v", r=2, p=P)
                fw.dma("sp", s0[:, :, :], ev[0], writes=[s0.b])
                fw.dma("sp", s1[:, :, :], ev[1], writes=[s1.b])
                for hd in range(GLA_H):
                    fw.op("dve", lambda e: e.tensor_scalar_mul(S[hd][:, :], s0[:, hd, :], sel[:, 0:1]), reads=[s0.b, sel.b], writes=[S[hd].b])
                    fw.op("dve", lambda e: e.scalar_tensor_tensor(S[hd][:, :], s1[:, hd, :], sel[:, 1:2], S[hd][:, :], ALU.mult, ALU.add),
                          reads=[s1.b, sel.b, S[hd].b], writes=[S[hd].b])
            for hd in range(GLA_H):
                fw.op("dve", lambda e: e.tensor_copy(Sbf[0][hd][:, :], S[hd][:, :]), reads=[S[hd].b], writes=[Sbf[0][hd].b])
            sp_ = 0
            if A:
                order = list(range(NTL))
            else:
                order = list(range(NTL - 1, NCT - 1, -1)) + list(range(NCT - 1, -1, -1))
            corder = (0, 1) if A else (1, 0)

            def load(ii):
                t = order[ii]
                r0 = t * P
                b_ = ii % 2
                fw.dma("sp", qT[b_][:, :, :], self.gqT.rearrange("(h p) t -> p h t", p=P)[:, :, r0:r0 + P], writes=[qT[b_].b])
                fw.dma("sp", kT[b_][:, :, :], self.gkT.rearrange("(h p) t -> p h t", p=P)[:, :, r0:r0 + P], writes=[kT[b_].b])
                fw.dma("sp", kk[b_][:, :], self.gk[r0:r0 + P, :], writes=[kk[b_].b])
                fw.dma("sp", vv[b_][:, :], self.gv[r0:r0 + P, :], writes=[vv[b_].b])
                fw.dma("sp", gg[b_][:, :], gsrc[r0:r0 + P, :], writes=[gg[b_].b])
                if not A:
                    fw.dma("sp", oa[b_][:, :, :], self.goT.rearrange("(c p) t -> p c t", p=P)[:, :, r0:r0 + P], writes=[oa[b_].b])

            load(0)
            nit = 0
            nos = 0
            for ii, t in enumerate(order):
                if ii + 1 < len(order):
                    load(ii + 1)
                b_ = ii % 2
                r0 = t * P
                if (not A) and t == NCT - 1:
                    sp_ ^= 1
                    for hd in range(GLA_H):
                        fw.op("dve", lambda e: e.memset(S[hd][:, :], 0.0), writes=[S[hd].b])
                        fw.op("dve", lambda e: e.tensor_copy(Sbf[sp_][hd][:, :], S[hd][:, :]), reads=[S[hd].b], writes=[Sbf[sp_][hd].b])
                cur = sp_
                for hd in range(GLA_H):
                    bX, bY = PS[2 * hd], PS[2 * hd + 1]
                    i2 = nit % 4
                    nit += 1
                    hs = slice(hd * P, (hd + 1) * P)
                    g_ = gg[b_]
                    fw.op("pe", lambda e: e.matmul(bX[:, 0:128], g_[:, hs], cst[:, cM, :], start=True, stop=True),
                          reads=[g_.b, cst.b], writes=[bX.b])
                    fw.op("pe", lambda e: e.matmul(bX[:, 128:256], cst[:, cU, :], g_[:, hs], start=True, stop=True),
                          reads=[g_.b, cst.b], writes=[bX.b])
                    fw.op("act", lambda e: e.activation(eG[i2][:, :], bX[:, 0:128], AF.Exp), writes=[bX.b, eG[i2].b])
                    fw.op("act", lambda e: e.activation(enG[i2][:, :], bX[:, 0:128], AF.Exp, scale=-1.0), writes=[bX.b, enG[i2].b])
                    fw.op("act", lambda e: e.activation(eE[i2][:, :], bX[:, 128:256], AF.Exp), writes=[bX.b, eE[i2].b])
                    fw.op("dve", lambda e: e.tensor_tensor(qt[i2][:, :], qT[b_][:, hd, :], eG[i2][:, :], ALU.mult),
                          reads=[qT[b_].b, eG[i2].b], writes=[qt[i2].b])
                    fw.op("dve", lambda e: e.tensor_tensor(kt[i2][:, :], kT[b_][:, hd, :], enG[i2][:, :], ALU.mult),
                          reads=[kT[b_].b, enG[i2].b], writes=[kt[i2].b])
                    fw.op("dve", lambda e: e.tensor_tensor(ke[i2][:, :], kk[b_][:, hs], eE[i2][:, :], ALU.mult),
                          reads=[kk[b_].b, eE[i2].b], writes=[ke[i2].b])
                    fw.op("pe", lambda e: e.matmul(bX[:, 256:384], kt[i2][:, :], qt[i2][:, :], start=True, stop=True),
                          reads=[kt[i2].b, qt[i2].b], writes=[bX.b])
                    fw.op("dve", lambda e: e.tensor_tensor(ab[i2][:, :], bX[:, 256:384], cst[:, cMask, :], ALU.mult),
                          reads=[cst.b], writes=[bX.b, ab[i2].b])
                    c0, c1 = corder
                    v_ = vv[b_]

                    def upd(c, src_par, dst_par):
                        cs_ = slice(c * 64, (c + 1) * 64)
                        fw.op("pe", lambda e: e.matmul(bY[:, 256:512], ke[i2][cs_, :], v_[cs_, hd * 256:(hd + 1) * 256], start=True, stop=True),
                              reads=[ke[i2].b, v_.b], writes=[bY.b])
                        col = c * 64 + (63 if A else 0)
                        fw.op("dve", lambda e: e.scalar_tensor_tensor(S[hd][:, :], S[hd][:, :], eG[i2][:, col:col + 1], bY[:, 256:512], ALU.mult, ALU.add),
                              reads=[eG[i2].b], writes=[S[hd].b, bY.b])
                        fw.op("act", lambda e: e.activation(Sbf[dst_par][hd][:, :], S[hd][:, :], AF.Copy), reads=[S[hd].b], writes=[Sbf[dst_par][hd].b])

                    upd(c0, cur, cur ^ 1)
                    for half in range(2):
                        po = bY[:, half * P:(half + 1) * P]
                        vs = slice(hd * 256 + half * P, hd * 256 + (half + 1) * P)
                        ss = slice(half * P, (half + 1) * P)
                        fw._deps("pe", [v_.b, ab[i2].b, Sbf[0][hd].b, Sbf[1][hd].b, qt[i2].b], [bY.b])
                        fw.eng["pe"].matmul(po, v_[:, vs], ab[i2][:, :], start=True, stop=False)
                        fw.eng["pe"].matmul(po[:, c0 * 64:(c0 + 1) * 64], Sbf[cur][hd][:, ss], qt[i2][:, c0 * 64:(c0 + 1) * 64], start=False, stop=False)
                        fw.op("pe", lambda e: e.matmul(po[:, c1 * 64:(c1 + 1) * 64], Sbf[cur ^ 1][hd][:, ss], qt[i2][:, c1 * 64:(c1 + 1) * 64], start=False, stop=True),
                              reads=[v_.b, ab[i2].b, Sbf[0][hd].b, Sbf[1][hd].b, qt[i2].b], writes=[bY.b])
                    o_ = osb[nos % 4]; nos += 1
                    pov = bY[:, 0:256].rearrange("p (a b) -> p a b", a=2)
                    if A:
                        fw.op("act", lambda e: e.activation(o_[:, :, :], pov, AF.Copy), writes=[bY.b, o_.b])
                    else:
                        fw.op("dve", lambda e: e.tensor_tensor(o_[:, :, :], pov, oa[b_][:, 2 * hd:2 * hd + 2, :], ALU.add),
                              reads=[oa[b_].b], writes=[bY.b, o_.b])
                    fw.dma("pool", self.goT[hd * 256:(hd + 1) * 256, r0:r0 + P].rearrange("(a p) t -> p a t", p=P), o_[:, :, :], reads=[o_.b])
                    upd(c1, cur ^ 1, cur)
            if A:
                for hd in range(GLA_H):
                    fw.dma("pool", self.exch_own[hd * P:(hd + 1) * P, :], S[hd][:, :], reads=[S[hd].b])
            fw.barrier()

    def declare_common(self):
        cfg, nc = self.cfg, self.nc
        L = cfg.DEPTH
        self.in_xT = self.dram_in("xT", [D, cfg.NT])
        self.in_cT = self.dram_in("cT", [P, DC, 2])
        self.in_adab = self.dram_in("adabT", [P, L, 48])
        self.in_ng = self.dram_in("ngT", [P, L, 4, DC])
        self.in_adaw = self.dram_in("ada_w", [L, D, 6 * D])
        self.in_w1 = self.dram_in("mlp_w1", [L, D, DFF])
        self.in_w2 = self.dram_in("mlp_w2", [L, DFF, D])
        self.in_consts = self.dram_in("consts", [P, 8, P])
        self.out_T = nc.dram_tensor("outT", [D, cfg.NT], F32, kind="ExternalOutput").ap()
        self.xA = self.dram("xA", [D, cfg.NT], F32)
        self.xB = self.dram("xB", [D, cfg.NT], F32)

    def declare_all(self):
        cfg, nc = self.cfg, self.nc
        self.declare_common()
        NT = cfg.NT
        na = (cfg.DEPTH + 1) // 2
        nb = max(cfg.DEPTH // 2, 1)
        self.in_rope = self.dram_in("ropeT", [P, NT])
        self.in_sel = self.dram_in("sel", [P, 2])
        self.in_wdq = self.dram_in("mla_w_dq", [na, D, 384])
        self.in_wuq = self.dram_in("mla_w_uq", [na, 384, 1536])
        self.in_wdkv = self.dram_in("mla_w_dkv", [na, D, 320])
        self.in_wukv = self.dram_in("mla_w_ukv", [na, 256, 2048])
        self.in_wo_mla = self.dram_in("mla_w_o", [na, D, D])
        self.in_mlag = self.dram_in("mla_gT", [P, na, 5])
        self.in_gwq = self.dram_in("gla_w_q", [nb, D, 512])
        self.in_gwk = self.dram_in("gla_w_k", [nb, D, 512])
        self.in_gwv = self.dram_in("gla_w_v", [nb, D, D])
        self.in_gwr = self.dram_in("gla_w_r", [nb, D, D])
        self.in_gw1 = self.dram_in("gla_w_gate1", [nb, 2, D, 16])
        self.in_gw2 = self.dram_in("gla_w_gate2", [nb, 2, 16, 512])
        self.in_gbg = self.dram_in("gla_b_gate", [nb, 2, 512])
        self.in_glago = self.dram_in("gla_goT", [P, nb, 2])
        self.in_wo_gla = self.dram_in("gla_w_o", [nb, D, D])
        self.KnT_own = self.dram("KnT_own", [1024, NT], BF16)
        self.KrT_own = self.dram("KrT_own", [P, NT], BF16)
        self.V_own = self.dram("V_own", [NT, 1024], BF16)
        self.Kn_all = self.dram("Kn_all", [16, P, NT], BF16)
        self.Kr_all = self.dram("Kr_all", [2, P, NT], BF16)
        self.V_all = self.dram("V_all", [NT // 256, 512, 1024], BF16)
        self.qnT = self.dram("qnT", [1024, NT], BF16)
        self.qrT = self.dram("qrT", [1024, NT], BF16)
        self.oT = self.dram("oT", [1024, NT], BF16)
        self.gqT = self.dram("gqT", [512, NT], F32)
        self.gkT = self.dram("gkT", [512, NT], F32)
        self.gk = self.dram("gk", [NT, 512], F32)
        self.gv = self.dram("gv", [NT, 1024], BF16)
        self.ggA = self.dram("ggA", [NT, 512], F32)
        self.ggB = self.dram("ggB", [NT, 512], F32)
        self.grT = self.dram("grT", [1024, NT], BF16)
        self.goT = self.dram("goT", [1024, NT], F32)
        self.exch_own = self.dram("exch_own", [512, 256], F32)
        self.exch_all = self.dram("exch_all", [1024, 256], F32)

    def build(self):
        cfg = self.cfg
        self.declare_all()
        self.setup_globals()
        fw = self.fw
        self.phase_ada()
        x_cur = self.in_xT
        for l in range(cfg.DEPTH):
            j = l // 2
            if l % 2 == 0:
                self.phase_mla1(l, j, x_cur)
                pairs = [(self.KnT_own[c * 64:(c + 1) * 64, :], self.Kn_all[c]) for c in range(16)]
                pairs += [(self.KrT_own[c * 64:(c + 1) * 64, :], self.Kr_all[c]) for c in range(2)]
                pairs += [(self.V_own[c * 256:(c + 1) * 256, :], self.V_all[c]) for c in range(cfg.NT // 256)]
                fw.allgather_many(pairs)
                self.phase_mla2()
                self.phase_oproj(l, self.in_wo_mla[j], x_cur, self.xA)
            else:
                self.phase_gla1(l, j, x_cur)
                self.phase_gla_scan("A")
                fw.allgather_many([(self.exch_own, self.exch_all)])
                self.phase_gla_scan("B")
                self.phase_oproj(l, self.in_wo_gla[j], x_cur, self.xA, gla_j=j)
            dst = self.out_T if l == cfg.DEPTH - 1 else self.xB
            self.phase_mlp(l, self.xA, dst)
            x_cur = self.xB
        fw.barrier()
        self.stack.close()
        return self.nc

    def setup_globals(self):
        st, nc, cfg = self.stack, self.nc, self.cfg
        self.fw = FW(nc, st)
        fw = self.fw
        self.PS2 = [st.enter_context(nc.psum_tensor(f"ps{i}", [P, 1024], F32)) for i in range(4)]
        self.PS = [T(self.PS2[i // 2][:, (i % 2) * 512:(i % 2 + 1) * 512], f"psb{i}") for i in range(8)]
        self.sc = self.sb(st, "sc", [P, cfg.DEPTH, 6, DC, 2], F32)
        self.cst = self.sb(st, "cst", [P, 8, P], F32)
        self.ones_bf = self.sb(st, "ones_bf", [P, P], BF16)
        self.ones_f = self.sb(st, "ones_f", [P, P], F32)
        fw.dma("sp", self.cst[:, :, :], self.in_consts, writes=[self.cst.b])
        fw.op("dve", lambda e: e.memset(self.ones_bf[:, :], 1.0), writes=[self.ones_bf.b])
        fw.op("dve", lambda e: e.memset(self.ones_f[:, :], 1.0), writes=[self.ones_f.b])

    def build_mlp_only(self):
        self.declare_common()
        self.setup_globals()
        self.phase_ada()
        self.phase_mlp(0, self.in_xT, self.out_T)
        self.fw.barrier()
        self.stack.close()
        return self.nc


def make_consts():
    c = np.zeros((P, 8, P), np.float32)
    j = np.arange(P)[:, None]
    i = np.arange(P)[None, :]
    same = (j // 64) == (i // 64)
    c[:, 0, :] = ((j % 64) == (i % 64)).astype(np.float32)
    c[:, 1, :] = np.where(same & (j <= i), -1.0 / 16, 0.0)
    c[:, 2, :] = np.where(same & (j > i), -1.0 / 16, 0.0)
    c[:, 3, :] = np.where(same & (j <= i), 1.0, 0.0)
    c[:, 4, :] = np.where(same & (j >= i), -1.0 / 16, 0.0)
    c[:, 5, :] = np.where(same & (j < i), -1.0 / 16, 0.0)
    c[:, 6, :] = np.where(same & (j >= i), 1.0, 0.0)
    return c


def rope_table(pos):
    half = 32
    inv = (ROPE_BASE ** (-np.arange(0, half, 2, dtype=np.float32) / np.float32(half))).astype(np.float32)
    row = (pos // GRID_W).astype(np.float32)
    col = (pos % GRID_W).astype(np.float32)
    ang_r = (row[:, None] * inv[None, :]).astype(np.float32)
    ang_c = (col[:, None] * inv[None, :]).astype(np.float32)
    ang = np.concatenate([ang_r, ang_r, ang_c, ang_c], axis=1)
    tab = np.zeros((P, CTX + len(pos)), np.float32)
    tab[0:64, :CTX] = 1.0
    tab[0:64, CTX:] = np.cos(ang).T
    tab[64:128, CTX:] = np.sin(ang).T
    return tab


_PROG_CACHE = {}


def make_in_maps(cfg, inputs, n_batch):
    NL, L = cfg.NL, cfg.DEPTH
    f32 = lambda a: np.ascontiguousarray(np.asarray(a, dtype=np.float32))
    x, c, ctx, c_ctx = (np.asarray(inputs[k]) for k in ("x", "c", "ctx", "c_ctx"))
    shared = {
        "adabT": f32(np.asarray(inputs["ada_b"])[:L].reshape(L, 48, P).transpose(2, 0, 1)),
        "ngT": f32(np.asarray(inputs["norm_g"])[:L].reshape(L, 4, DC, P).transpose(3, 0, 1, 2)),
        "ada_w": f32(np.asarray(inputs["ada_w"])[:L]),
        "mlp_w1": f32(np.asarray(inputs["mlp_w1"])[:L]),
        "mlp_w2": f32(np.asarray(inputs["mlp_w2"])[:L]),
        "consts": make_consts(),
    }
    na = (L + 1) // 2
    nb = max(L // 2, 1)
    for k in ("mla_w_dq", "mla_w_uq", "mla_w_dkv", "mla_w_ukv", "mla_w_o"):
        shared[k] = f32(np.asarray(inputs[k])[:na])
    gq = np.asarray(inputs["mla_g_q"])[:na].reshape(na, 3, P)
    gkv = np.asarray(inputs["mla_g_kv"])[:na].reshape(na, 2, P)
    shared["mla_gT"] = f32(np.concatenate([gq, gkv], axis=1).transpose(2, 0, 1))
    for k in ("gla_w_q", "gla_w_k", "gla_w_v", "gla_w_r", "gla_w_o"):
        shared[k] = f32(np.asarray(inputs[k])[:nb])
    shared["gla_goT"] = f32(np.asarray(inputs["gla_g_o"])[:nb].reshape(nb, 2, P).transpose(2, 0, 1))
    in_maps = []
    for b in range(n_batch):
        for r in range(2):
            if r == 0:
                pos = np.arange(0, NL)
                cx = ctx[b]
                dirs = [0, 1]
            else:
                pos = np.arange(2 * NL - 1, NL - 1, -1)
                cx = ctx[b][::-1]
                dirs = [1, 0]
            xt = np.concatenate([cx, x[b][pos]], axis=0)
            m = dict(shared)
            m["xT"] = f32(xt.T)
            m["cT"] = f32(np.stack([c[b].reshape(DC, P).T, c_ctx.reshape(DC, P).T], -1))
            m["ropeT"] = rope_table(pos)
            sel = np.zeros((P, 2), np.float32)
            sel[:, 1 - r] = 1.0
            m["sel"] = sel
            m["gla_w_gate1"] = f32(np.asarray(inputs["gla_w_gate1"])[:nb][:, dirs])
            m["gla_w_gate2"] = f32(np.asarray(inputs["gla_w_gate2"])[:nb][:, dirs])
            m["gla_b_gate"] = f32(np.asarray(inputs["gla_b_gate"])[:nb][:, dirs])
            in_maps.append(m)
    return in_maps


def assemble(cfg, outs, n_batch):
    NL = cfg.NL
    out = np.zeros((n_batch, 2 * NL, D), np.float32)
    for b in range(n_batch):
        out[b, :NL] = np.asarray(outs[2 * b])[:, CTX:].T
        out[b, NL:] = np.asarray(outs[2 * b + 1])[:, CTX:].T[::-1]
    return out


def run_cores(cfg, inputs, n_batch):
    in_maps = make_in_maps(cfg, inputs, n_batch)
    nc = Prog(cfg).build()
    res = run_bass_kernel_spmd(nc, in_maps, core_ids=list(range(2 * n_batch)))
    return assemble(cfg, [r["outT"] for r in res.results], n_batch)


def kernel(**inputs):
    cfg = Cfg(NL=4096, DEPTH=4)
    return run_cores(cfg, inputs, 4)
```

```python
import contextlib
import numpy as np
import ml_dtypes
import concourse.bass as bass
import concourse.mybir as mybir
from concourse.bass_utils import run_bass_kernel_spmd

F32 = mybir.dt.float32
BF16 = mybir.dt.bfloat16
AF = mybir.ActivationFunctionType
ALU = mybir.AluOpType

NDMA = 20
P = 128
D = 1024
DC = 8
DFF = 4096
CTX = 256
EPS = 1e-6
GRID_W = 64
MLA_H = 8
GLA_H = 4
ROPE_BASE = 10000.0


class Buf:
    __slots__ = ("name", "w", "r")

    def __init__(self, name):
        self.name = name
        self.w = None
        self.r = []


class FW:
    def __init__(self, nc, stack):
        self.nc = nc
        self.eng = {"pe": nc.tensor, "act": nc.scalar, "dve": nc.vector,
                    "pool": nc.gpsimd, "sp": nc.sync}
        self.sem, self.cnt, self.semobj = {}, {}, {}
        for e in self.eng:
            self.sem[e] = stack.enter_context(nc.semaphore("s_" + e))
            self.cnt[e] = 0
            self.semobj["s_" + e] = self.sem[e]
        self.dsem, self.dcnt, self.dnext = {}, {}, {}
        for q in ("sp", "pool"):
            self.dsem[q] = [stack.enter_context(nc.semaphore(f"d_{q}{i}")) for i in range(NDMA)]
            self.dcnt[q] = [0] * NDMA
            self.dnext[q] = 0
            for i, s in enumerate(self.dsem[q]):
                self.semobj[f"d_{q}{i}"] = s
        self.bar = stack.enter_context(nc.semaphore("s_bar"))
        self.semobj["s_bar"] = self.bar
        self.barcnt = 0
        self.cc = stack.enter_context(nc.semaphore("s_cc"))
        self.semobj["s_cc"] = self.cc
        self.cccnt = 0
        self.known = {e: {} for e in self.eng}
        self.nwaits = 0
        self.nins = 0

    def _wait(self, e, tok):
        if tok is None:
            return
        name, val = tok
        if self.known[e].get(name, 0) >= val:
            return
        self.eng[e].wait_ge(self.semobj[name], val)
        self.known[e][name] = val
        self.nwaits += 1

    def _deps(self, e, reads, writes):
        own = "s_" + e
        for b in reads:
            if b.w is not None and not (b.w[0] == own and e == "pe"):
                self._wait(e, b.w)
        for b in writes:
            if b.w is not None and not (b.w[0] == own and e == "pe"):
                self._wait(e, b.w)
            for t in b.r:
                if t[0] != own:
                    self._wait(e, t)

    def _record(self, tok, reads, writes):
        for b in reads:
            b.r.append(tok)
        for b in writes:
            b.w = tok
            b.r = []

    def op(self, e, fn, reads=(), writes=()):
        self._deps(e, reads, writes)
        ins = fn(self.eng[e])
        self.cnt[e] += 1
        ins.then_inc(self.sem[e], 1)
        tok = ("s_" + e, self.cnt[e])
        self._record(tok, reads, writes)
        self.nins += 1
        return tok

    def dma(self, q, out, in_, reads=(), writes=()):
        i = self.dnext[q]
        self.dnext[q] = (i + 1) % NDMA
        name = f"d_{q}{i}"
        if self.dcnt[q][i] > 0:
            self._wait(q, (name, 16 * self.dcnt[q][i]))
        self._deps(q, reads, writes)
        self.dcnt[q][i] += 1
        self.eng[q].dma_start(out=out, in_=in_).then_inc(self.dsem[q][i], 16)
        tok = (name, 16 * self.dcnt[q][i])
        self._record(tok, reads, writes)
        self.nins += 1
        return tok

    def barrier(self):
        for e in self.eng:
            if e != "sp" and self.cnt[e] > 0:
                self._wait("sp", ("s_" + e, self.cnt[e]))
        for q in self.dsem:
            for i in range(NDMA):
                if self.dcnt[q][i] > 0:
                    self._wait("sp", (f"d_{q}{i}", 16 * self.dcnt[q][i]))
        if self.cccnt > 0:
            self._wait("sp", ("s_cc", self.cccnt))
        self.barcnt += 1
        self.eng["sp"].sem_inc(self.bar, 1)
        for e in self.eng:
            if e != "sp":
                self._wait(e, ("s_bar", self.barcnt))
            for e2 in self.eng:
                self.known[e]["s_" + e2] = self.cnt[e2]
            for q in self.dsem:
                for i in range(NDMA):
                    self.known[e][f"d_{q}{i}"] = 16 * self.dcnt[q][i]
            self.known[e]["s_cc"] = self.cccnt

    def allgather_many(self, pairs):
        self.barrier()
        for (in_ap, out_ap) in pairs:
            self.cccnt += 1
            self.eng["pool"].collective_compute(
                "AllGather", ALU.bypass,
                replica_groups=[[0, 1], [2, 3], [4, 5], [6, 7]],
                ins=[in_ap], outs=[out_ap]).then_inc(self.cc, 1)
        self.barrier()


class T:
    def __init__(self, h, name):
        self.h = h
        self.b = Buf(name)

    def __getitem__(self, k):
        return self.h[k]


class Cfg:
    def __init__(self, NL=4096, DEPTH=4):
        self.NL = NL
        self.DEPTH = DEPTH
        self.NT = CTX + NL
        self.NKEY = CTX + 2 * NL


def tok_tiles(cfg, TT):
    out = []
    t = 0
    while t < CTX:
        n = min(TT, CTX - t)
        out.append((t, n, 1))
        t += n
    while t < cfg.NT:
        n = min(TT, cfg.NT - t)
        out.append((t, n, 0))
        t += n
    return out


class Prog:
    def __init__(self, cfg):
        self.cfg = cfg
        self.nc = bass.Bass("TRN2", target_bir_lowering=False)
        self.stack = contextlib.ExitStack()

    def sb(self, st, name, shape, dt):
        self._uid = getattr(self, "_uid", 0) + 1
        name = f"{name}_{self._uid}"
        return T(st.enter_context(self.nc.sbuf_tensor(name, shape, dt)), name)

    def dram_in(self, name, shape, dt=F32):
        return self.nc.dram_tensor(name, list(shape), dt, kind="ExternalInput").ap()

    def dram(self, name, shape, dt):
        return self.nc.dram_tensor(name, list(shape), dt).ap()

    def load_w_bf16(self, dst, src_ap, nsplit):
        fw = self.fw
        a = src_ap.shape[1]
        step = (a + nsplit - 1) // nsplit
        for i in range(0, a, step):
            j = min(a, i + step)
            fw.dma("pool", dst[:, i:j, :], src_ap[:, i:j, :], writes=[dst.b])

    def rstd_from_sq(self, ps, n, F, rstd):
        fw = self.fw
        fw.op("act", lambda e: e.activation(rstd[:, :n], ps[:, :n], AF.Sqrt, bias=EPS, scale=1.0 / F),
              writes=[ps.b, rstd.b])
        fw.op("dve", lambda e: e.reciprocal(rstd[:, :n], rstd[:, :n]), reads=[rstd.b], writes=[rstd.b])

    def prenorm(self, x, n, A, Bv, h, ps, sq, rstd, tmp):
        fw = self.fw
        for c in range(DC):
            s = sq[c % 2]
            fw.op("act", lambda e: e.activation(s[:, :n], x[:, c, :n], AF.Square), reads=[x.b], writes=[s.b])
            fw.op("pe", lambda e: e.matmul(ps[:, :n], self.ones_bf[:, :], s[:, :n], start=(c == 0), stop=(c == DC - 1)),
                  reads=[s.b, self.ones_bf.b], writes=[ps.b])
        self.rstd_from_sq(ps, n, D, rstd)
        for c in range(DC):
            t = tmp[c % 2]
            fw.op("dve", lambda e: e.scalar_tensor_tensor(t[:, :n], x[:, c, :n], A[:, c:c + 1], rstd[:, :n], ALU.mult, ALU.mult),
                  reads=[x.b, rstd.b, self.sc.b], writes=[t.b])
            fw.op("act", lambda e: e.activation(h[:, c, :n], t[:, :n], AF.Identity, bias=Bv[:, c:c + 1], scale=1.0),
                  reads=[t.b, self.sc.b], writes=[h.b])

    def postnorm_residual(self, n, Cg, x, y, ps_stat, sq, rstd, tmp, mm_chunk, pbanks):
        fw = self.fw
        pending = None
        for c in range(DC):
            pb = pbanks[c % 2]
            mm_chunk(c, pb)
            fw.op("dve", lambda e: e.tensor_copy(y[:, c, :n], pb[:, :n]), writes=[pb.b, y.b])
            s = sq[c % 2]
            fw.op("act", lambda e: e.activation(s[:, :n], y[:, c, :n], AF.Square), reads=[y.b], writes=[s.b])
            if pending is not None:
                pc, psq = pending
                fw.op("pe", lambda e: e.matmul(ps_stat[:, :n], self.ones_bf[:, :], psq[:, :n], start=(pc == 0), stop=False),
                      reads=[psq.b], writes=[ps_stat.b])
            pending = (c, s)
        pc, psq = pending
        fw.op("pe", lambda e: e.matmul(ps_stat[:, :n], self.ones_bf[:, :], psq[:, :n], start=False, stop=True),
              reads=[psq.b], writes=[ps_stat.b])
        self.rstd_from_sq(ps_stat, n, D, rstd)
        for c in range(DC):
            t = tmp[c % 2]
            fw.op("dve", lambda e: e.scalar_tensor_tensor(t[:, :n], y[:, c, :n], Cg[:, c:c + 1], rstd[:, :n], ALU.mult, ALU.mult),
                  reads=[y.b, rstd.b, self.sc.b], writes=[t.b])
            fw.op("pool", lambda e: e.tensor_tensor(x[:, c, :n], x[:, c, :n], t[:, :n], ALU.add),
                  reads=[x.b, t.b], writes=[x.b])

    def scv(self, l, slot, s):
        return self.sc[:, l, slot, :, s]

    def phase_ada(self):
        fw, nc, cfg = self.fw, self.nc, self.cfg
        with contextlib.ExitStack() as st:
            csb = self.sb(st, "csb", [P, DC, 2], F32)
            sil = self.sb(st, "sil", [P, DC, 2], F32)
            adab = self.sb(st, "adab", [P, cfg.DEPTH, 48], F32)
            ng = self.sb(st, "ng", [P, cfg.DEPTH, 4, DC], F32)
            mod = self.sb(st, "mod", [P, cfg.DEPTH, 48, 2], F32)
            wbuf = [self.sb(st, f"adaw{i}", [P, DC, 1024], F32) for i in range(2)]
            fw.dma("sp", csb[:, :, :], self.in_cT, writes=[csb.b])
            fw.dma("sp", adab[:, :, :], self.in_adab, writes=[adab.b])
            fw.dma("sp", ng[:, :, :, :], self.in_ng, writes=[ng.b])
            fw.op("act", lambda e: e.activation(sil[:, :, :], csb[:, :, :], AF.Silu), reads=[csb.b], writes=[sil.b])
            ps = self.PS[0]
            it = 0
            for l in range(cfg.DEPTH):
                wv = self.in_adaw[l].rearrange("(c p) m -> p c m", p=P)
                for g6 in range(6):
                    wb = wbuf[it % 2]
                    it += 1
                    for hh in range(2):
                        fw.dma("sp", wb[:, hh * 4:(hh + 1) * 4, :], wv[:, hh * 4:(hh + 1) * 4, g6 * 1024:(g6 + 1) * 1024], writes=[wb.b])
                    for j in range(8):
                        jj = g6 * 8 + j
                        for k in range(DC):
                            last = (k == DC - 1)
                            f = lambda e: e.matmul(ps[:, jj * 2:jj * 2 + 2], wb[:, k, j * 128:(j + 1) * 128], sil[:, k, :],
                                                   start=(k == 0), stop=last)
                            if last:
                                fw.op("pe", f, reads=[wb.b, sil.b], writes=[ps.b])
                            else:
                                fw._deps("pe", [wb.b, sil.b], [ps.b])
                                f(fw.eng["pe"])
                psv = ps[:, 0:96].rearrange("p (j s) -> p j s", s=2)
                for s in range(2):
                    fw.op("dve", lambda e: e.tensor_tensor(mod[:, l, :, s], psv[:, :, s], adab[:, l, :], ALU.add),
                          reads=[adab.b], writes=[ps.b, mod.b])
                for s in range(2):
                    m = lambda j: mod[:, l, j * 8:(j + 1) * 8, s]
                    sc = self.sc
                    fw.op("dve", lambda e: e.scalar_tensor_tensor(sc[:, l, 0, :, s], m(1), 1.0, ng[:, l, 0, :], ALU.add, ALU.mult),
                          reads=[mod.b, ng.b], writes=[sc.b])
                    fw.op("dve", lambda e: e.tensor_copy(sc[:, l, 1, :, s], m(0)), reads=[mod.b], writes=[sc.b])
                    fw.op("dve", lambda e: e.tensor_tensor(sc[:, l, 2, :, s], m(2), ng[:, l, 1, :], ALU.mult),
                          reads=[mod.b, ng.b], writes=[sc.b])
                    fw.op("dve", lambda e: e.scalar_tensor_tensor(sc[:, l, 3, :, s], m(4), 1.0, ng[:, l, 2, :], ALU.add, ALU.mult),
                          reads=[mod.b, ng.b], writes=[sc.b])
                    fw.op("dve", lambda e: e.tensor_copy(sc[:, l, 4, :, s], m(3)), reads=[mod.b], writes=[sc.b])
                    fw.op("dve", lambda e: e.tensor_tensor(sc[:, l, 5, :, s], m(5), ng[:, l, 3, :], ALU.mult),
                          reads=[mod.b, ng.b], writes=[sc.b])
            fw.barrier()

    def phase_mlp(self, l, x_src, x_dst):
        fw, nc, cfg = self.fw, self.nc, self.cfg
        TT = 256
        with contextlib.ExitStack() as st:
            w1 = self.sb(st, "w1", [P, DC, DFF], BF16)
            w2 = self.sb(st, "w2", [P, DFF // P, D], BF16)
            xs = [self.sb(st, f"mx{i}", [P, DC, TT], F32) for i in range(2)]
            h = self.sb(st, "mh", [P, DC, TT], BF16)
            hid = self.sb(st, "mhid", [P, DFF // P, TT], BF16)
            y = self.sb(st, "my", [P, DC, TT], F32)
            rl = [self.sb(st, f"mrl{i}", [P, TT], F32) for i in range(2)]
            sq = [self.sb(st, f"msq{i}", [P, TT], BF16) for i in range(2)]
            tmp = [self.sb(st, f"mtmp{i}", [P, TT], F32) for i in range(2)]
            rstd = self.sb(st, "mrstd", [P, TT], F32)
            self.load_w_bf16(w1, self.in_w1[l].rearrange("(c p) m -> p c m", p=P), 4)
            self.load_w_bf16(w2, self.in_w2[l].rearrange("(c p) m -> p c m", p=P), 8)
            tiles = tok_tiles(cfg, TT)
            xv_src = x_src.rearrange("(c p) t -> p c t", p=P)
            xv_dst = x_dst.rearrange("(c p) t -> p c t", p=P)
            PS = self.PS

            def load(i):
                t0, n, s = tiles[i]
                xb = xs[i % 2]
                fw.dma("sp", xb[:, :, :n], xv_src[:, :, t0:t0 + n], writes=[xb.b])

            load(0)
            for i, (t0, n, s) in enumerate(tiles):
                if i + 1 < len(tiles):
                    load(i + 1)
                x = xs[i % 2]
                self.prenorm(x, n, self.scv(l, 3, s), self.scv(l, 4, s), h, PS[4], sq, rstd, tmp)
                for m in range(DFF // P):
                    pb = PS[m % 2]
                    for k in range(DC):
                        f = lambda e: e.matmul(pb[:, :n], w1[:, k, m * P:(m + 1) * P], h[:, k, :n], start=(k == 0), stop=(k == DC - 1))
                        if k == DC - 1:
                            fw.op("pe", f, reads=[w1.b, h.b], writes=[pb.b])
                        else:
                            fw._deps("pe", [w1.b, h.b], [pb.b])
                            f(fw.eng["pe"])
                    r = rl[m % 2]
                    fw.op("act", lambda e: e.activation(r[:, :n], pb[:, :n], AF.Relu), writes=[pb.b, r.b])
                    fw.op("dve", lambda e: e.tensor_tensor(hid[:, m, :n], r[:, :n], r[:, :n], ALU.mult), reads=[r.b], writes=[hid.b])

                def mm_chunk(c, pb):
                    KC = DFF // P
                    for k in range(KC):
                        f = lambda e: e.matmul(pb[:, :n], w2[:, k, c * P:(c + 1) * P], hid[:, k, :n], start=(k == 0), stop=(k == KC - 1))
                        if k == KC - 1:
                            fw.op("pe", f, reads=[w2.b, hid.b], writes=[pb.b])
                        else:
                            fw._deps("pe", [w2.b, hid.b], [pb.b])
                            f(fw.eng["pe"])

                self.postnorm_residual(n, self.scv(l, 5, s), x, y, PS[5], sq, rstd, tmp, mm_chunk, [PS[2], PS[3]])
                fw.dma("pool", xv_dst[:, :, t0:t0 + n], x[:, :, :n], reads=[x.b])
            fw.barrier()

    def mm(self, pb, out_ap, pairs, reads):
        fw = self.fw
        last = len(pairs) - 1
        for i, (l_, r_) in enumerate(pairs):
            f = lambda e: e.matmul(out_ap, l_, r_, start=(i == 0), stop=(i == last))
            if i == last:
                fw.op("pe", f, reads=reads, writes=[pb.b])
            else:
                fw._deps("pe", reads, [pb.b])
                f(fw.eng["pe"])

    def phase_mla1(self, l, j, x_src):
        fw, nc, cfg = self.fw, self.nc, self.cfg
        TT = 512
        PS = self.PS
        with contextlib.ExitStack() as st:
            wdq = self.sb(st, "wdq", [P, DC, 384], BF16)
            wuqn = self.sb(st, "wuqn", [P, 3, 8, 128], BF16)
            wuqr = self.sb(st, "wuqr", [P, 3, 8, 128], BF16)
            wdkc = self.sb(st, "wdkc", [P, DC, 256], BF16)
            wdkr = self.sb(st, "wdkr", [P, DC, 128], BF16)
            wukk = self.sb(st, "wukk", [P, 2, 8, 128], BF16)
            wukv = self.sb(st, "wukv", [P, 2, 8, 128], BF16)
            foldb = self.sb(st, "foldb", [P, P], BF16)
            gsc = self.sb(st, "gsc", [P, 5], F32)
            xs = [self.sb(st, f"ax{i}", [P, DC, TT], F32) for i in range(2)]
            cs = [self.sb(st, f"acs{i}", [P, TT], F32) for i in range(2)]
            h = self.sb(st, "ah", [P, DC, TT], BF16)
            sq = [self.sb(st, f"asq{i}", [P, TT], BF16) for i in range(2)]
            tmp = [self.sb(st, f"atmp{i}", [P, TT], F32) for i in range(2)]
            rstd = self.sb(st, "arstd", [P, TT], F32)
            cqf = self.sb(st, "acqf", [P, 3, TT], F32)
            cqb = self.sb(st, "acqb", [P, 3, TT], BF16)
            ckf = self.sb(st, "ackf", [P, 2, TT], F32)
            ckb = self.sb(st, "ackb", [P, 2, TT], BF16)
            kru = self.sb(st, "akru", [P, TT], BF16)
            ob = [self.sb(st, f"aob{i}", [P, TT], BF16) for i in range(4)]
            vsb = [self.sb(st, f"avsb{i}", [P, 1024], BF16) for i in range(2)]

            fw.dma("pool", wdq[:, :, :], self.in_wdq[j].rearrange("(c p) m -> p c m", p=P), writes=[wdq.b])
            uqv = self.in_wuq[j].rearrange("(c p) (h e) -> p c h e", p=P, e=192)
            for c in range(3):
                fw.dma("pool", wuqn[:, c, :, :], uqv[:, c, :, 0:128], writes=[wuqn.b])
                fw.dma("pool", wuqr[:, c, :, 0:64], uqv[:, c, :, 128:192], writes=[wuqr.b])
            dkv = self.in_wdkv[j].rearrange("(c p) m -> p c m", p=P)
            fw.dma("pool", wdkc[:, :, :], dkv[:, :, 0:256], writes=[wdkc.b])
            fw.dma("pool", wdkr[:, :, 0:64], dkv[:, :, 256:320], writes=[wdkr.b])
            ukv = self.in_wukv[j].rearrange("(c p) (h e) -> p c h e", p=P, e=256)
            for c in range(2):
                fw.dma("pool", wukk[:, c, :, :], ukv[:, c, :, 0:128], writes=[wukk.b])
                fw.dma("pool", wukv[:, c, :, :], ukv[:, c, :, 128:256], writes=[wukv.b])
            fw.dma("sp", gsc[:, :], self.in_mlag[:, j, :], writes=[gsc.b])
            fw.op("dve", lambda e: e.tensor_copy(foldb[:, :], self.cst[:, 0, :]), reads=[self.cst.b], writes=[foldb.b])
            for (wt, nd) in ((wuqr, 4), (wdkr, 3)):
                for (d0, s0, sg) in ((64, 16, -1.0), (80, 0, 1.0), (96, 48, -1.0), (112, 32, 1.0)):
                    if nd == 4:
                        o_, i_ = wt[:, :, :, d0:d0 + 16], wt[:, :, :, s0:s0 + 16]
                    else:
                        o_, i_ = wt[:, :, d0:d0 + 16], wt[:, :, s0:s0 + 16]
                    fw.op("dve", lambda e: e.tensor_scalar_mul(o_, i_, sg), reads=[wt.b], writes=[wt.b])

            tiles = tok_tiles(cfg, TT)
            xv = x_src.rearrange("(c p) t -> p c t", p=P)

            def load(i):
                t0, n, s = tiles[i]
                fw.dma("sp", xs[i % 2][:, :, :n], xv[:, :, t0:t0 + n], writes=[xs[i % 2].b])
                fw.dma("sp", cs[i % 2][:, :n], self.in_rope[:, t0:t0 + n], writes=[cs[i % 2].b])

            load(0)
            nob = 0
            for i, (t0, n, s) in enumerate(tiles):
                if i + 1 < len(tiles):
                    load(i + 1)
                x, csb = xs[i % 2], cs[i % 2]
                self.prenorm(x, n, self.scv(l, 0, s), self.scv(l, 1, s), h, PS[4], sq, rstd, tmp)
                for m in range(3):
                    pb = PS[m % 2]
                    self.mm(pb, pb[:, :n], [(wdq[:, k, m * P:(m + 1) * P], h[:, k, :n]) for k in range(DC)], [wdq.b, h.b])
                    fw.op("dve", lambda e: e.tensor_copy(cqf[:, m, :n], pb[:, :n]), writes=[pb.b, cqf.b])
                    sqq = sq[m % 2]
                    fw.op("act", lambda e: e.activation(sqq[:, :n], cqf[:, m, :n], AF.Square), reads=[cqf.b], writes=[sqq.b])
                    fw.op("pe", lambda e: e.matmul(PS[5][:, :n], self.ones_bf[:, :], sqq[:, :n], start=(m == 0), stop=(m == 2)),
                          reads=[sqq.b], writes=[PS[5].b])
                self.rstd_from_sq(PS[5], n, 384, rstd)
                for m in range(3):
                    fw.op("dve", lambda e: e.scalar_tensor_tensor(cqb[:, m, :n], cqf[:, m, :n], gsc[:, m:m + 1], rstd[:, :n], ALU.mult, ALU.mult),
                          reads=[cqf.b, gsc.b, rstd.b], writes=[cqb.b])
                for m in range(2):
                    pb = PS[m % 2]
                    self.mm(pb, pb[:, :n], [(wdkc[:, k, m * P:(m + 1) * P], h[:, k, :n]) for k in range(DC)], [wdkc.b, h.b])
                    fw.op("dve", lambda e: e.tensor_copy(ckf[:, m, :n], pb[:, :n]), writes=[pb.b, ckf.b])
                    sqq = sq[m % 2]
                    fw.op("act", lambda e: e.activation(sqq[:, :n], ckf[:, m, :n], AF.Square), reads=[ckf.b], writes=[sqq.b])
                    fw.op("pe", lambda e: e.matmul(PS[5][:, :n], self.ones_bf[:, :], sqq[:, :n], start=(m == 0), stop=(m == 1)),
                          reads=[sqq.b], writes=[PS[5].b])
                self.rstd_from_sq(PS[5], n, 256, rstd)
                for m in range(2):
                    fw.op("dve", lambda e: e.scalar_tensor_tensor(ckb[:, m, :n], ckf[:, m, :n], gsc[:, 3 + m:4 + m], rstd[:, :n], ALU.mult, ALU.mult),
                          reads=[ckf.b, gsc.b, rstd.b], writes=[ckb.b])
                pb = PS[2]
                self.mm(pb, pb[:, :n], [(wdkr[:, k, :], h[:, k, :n]) for k in range(DC)], [wdkr.b, h.b])
                fw.op("dve", lambda e: e.tensor_tensor(kru[:, :n], pb[:, :n], csb[:, :n], ALU.mult), reads=[csb.b], writes=[pb.b, kru.b])
                pb = PS[3]
                self.mm(pb, pb[:, :n], [(foldb[:, :], kru[:, :n])], [foldb.b, kru.b])
                o_ = ob[nob % 4]; nob += 1
                fw.op("act", lambda e: e.activation(o_[:, :n], pb[:, :n], AF.Copy), writes=[pb.b, o_.b])
                fw.dma("pool", self.KrT_own[:, t0:t0 + n], o_[:, :n], reads=[o_.b])
                for hh in range(MLA_H):
                    pb = PS[0]
                    self.mm(pb, pb[:, :n], [(wuqn[:, k, hh, :], cqb[:, k, :n]) for k in range(3)], [wuqn.b, cqb.b])
                    o_ = ob[nob % 4]; nob += 1
                    fw.op("act", lambda e: e.activation(o_[:, :n], pb[:, :n], AF.Copy), writes=[pb.b, o_.b])
                    fw.dma("pool", self.qnT[hh * P:(hh + 1) * P, t0:t0 + n], o_[:, :n], reads=[o_.b])
                    pb = PS[1]
                    self.mm(pb, pb[:, :n], [(wuqr[:, k, hh, :], cqb[:, k, :n]) for k in range(3)], [wuqr.b, cqb.b])
                    o_ = ob[nob % 4]; nob += 1
                    fw.op("dve", lambda e: e.tensor_tensor(o_[:, :n], pb[:, :n], csb[:, :n], ALU.mult), reads=[csb.b], writes=[pb.b, o_.b])
                    fw.dma("pool", self.qrT[hh * P:(hh + 1) * P, t0:t0 + n], o_[:, :n], reads=[o_.b])
                    pb = PS[2]
                    self.mm(pb, pb[:, :n], [(wukk[:, k, hh, :], ckb[:, k, :n]) for k in range(2)], [wukk.b, ckb.b])
                    o_ = ob[nob % 4]; nob += 1
                    fw.op("act", lambda e: e.activation(o_[:, :n], pb[:, :n], AF.Copy), writes=[pb.b, o_.b])
                    fw.dma("pool", self.KnT_own[hh * P:(hh + 1) * P, t0:t0 + n], o_[:, :n], reads=[o_.b])
                for sidx in range(n // P):
                    vs_ = vsb[sidx % 2]
                    for half in range(2):
                        pb = PS[6 + half]
                        self.mm(pb, pb[:, :], [(ckb[:, k, sidx * P:(sidx + 1) * P], wukv[:, k, half * 4:(half + 1) * 4, :]) for k in range(2)],
                                [wukv.b, ckb.b])
                        if half == 0:
                            fw.op("act", lambda e: e.activation(vs_[:, 0:512], pb[:, :], AF.Copy), writes=[pb.b, vs_.b])
                        else:
                            fw.op("dve", lambda e: e.tensor_copy(vs_[:, 512:1024], pb[:, :]), writes=[pb.b, vs_.b])
                    fw.dma("pool", self.V_own[t0 + sidx * P:t0 + (sidx + 1) * P, :], vs_[:, :], reads=[vs_.b])
            fw.barrier()

    def phase_mla2(self):
        fw, nc, cfg = self.fw, self.nc, self.cfg
        PS = self.PS
        NT, NL, NKEY = cfg.NT, cfg.NL, cfg.NKEY
        NKT = NKEY // P
        SCALE = 192.0 ** -0.5
        TQ = 512
        with contextlib.ExitStack() as st:
            Kn = [self.sb(st, f"bKn{i}", [P, NKEY], BF16) for i in range(2)]
            Vh = [self.sb(st, f"bVh{i}", [P, NKT, P], BF16) for i in range(2)]
            Kr = self.sb(st, "bKr", [P, NKEY], BF16)
            qn = [self.sb(st, f"bqn{i}", [P, TQ], BF16) for i in range(2)]
            qr = [self.sb(st, f"bqr{i}", [P, TQ], BF16) for i in range(2)]
            Pt = [self.sb(st, f"bPt{i}", [P, 2, TQ], BF16) for i in range(2)]
            acc = self.sb(st, "bacc", [P, 2, TQ], F32)
            accs = self.sb(st, "baccs", [P, TQ], F32)
            rec = self.sb(st, "brec", [P, TQ], F32)
            obf = [self.sb(st, f"bobf{i}", [P, TQ], BF16) for i in range(2)]
            for half in range(2):
                hp = slice(half * 64, (half + 1) * 64)
                fw.dma("sp", Kr[hp, 0:NT], self.Kr_all[half, 0:64, :], writes=[Kr.b])
                fw.dma("sp", Kr[hp, NT:NKEY], self.Kr_all[half, 64:128, CTX:NT], writes=[Kr.b])

            def loadhead(hh):
                kb, vb = Kn[hh % 2], Vh[hh % 2]
                for half in range(2):
                    hp = slice(half * 64, (half + 1) * 64)
                    c = 2 * hh + half
                    fw.dma("sp", kb[hp, 0:NT], self.Kn_all[c, 0:64, :], writes=[kb.b])
                    fw.dma("sp", kb[hp, NT:NKEY], self.Kn_all[c, 64:128, CTX:NT], writes=[kb.b])
                hc = slice(hh * P, (hh + 1) * P)
                for c in range(NT // 256):
                    v0 = self.V_all[c, 0:256, hc].rearrange("(t p) d -> p t d", p=P)
                    fw.dma("sp", vb[:, 2 * c:2 * c + 2, :], v0, writes=[vb.b])
                    if c >= 1:
                        v1 = self.V_all[c, 256:512, hc].rearrange("(t p) d -> p t d", p=P)
                        k0 = NT // P + 2 * (c - 1)
                        fw.dma("sp", vb[:, k0:k0 + 2, :], v1, writes=[vb.b])

            qtiles = tok_tiles(cfg, TQ)
            loadhead(0)
            it = 0
            for hh in range(MLA_H):
                if hh + 1 < MLA_H:
                    loadhead(hh + 1)
                kb, vb = Kn[hh % 2], Vh[hh % 2]
                for (t0, n, s) in qtiles:
                    qn_, qr_ = qn[it % 2], qr[it % 2]
                    o_ = obf[it % 2]
                    Ob = PS[4 + it % 2]
                    it += 1
                    fw.dma("sp", qn_[:, :n], self.qnT[hh * P:(hh + 1) * P, t0:t0 + n], writes=[qn_.b])
                    fw.dma("sp", qr_[:, :n], self.qrT[hh * P:(hh + 1) * P, t0:t0 + n], writes=[qr_.b])
                    nkt = (CTX // P) if s == 1 else NKT
                    npair = nkt // 2

                    def scores(pi):
                        for jj in range(2):
                            kt = 2 * pi + jj
                            pb = PS[2 * (pi % 2) + jj]
                            self.mm(pb, pb[:, :n], [(kb[:, kt * P:(kt + 1) * P], qn_[:, :n]), (Kr[:, kt * P:(kt + 1) * P], qr_[:, :n])],
                                    [kb.b, Kr.b, qn_.b, qr_.b])

                    scores(0)
                    for pi in range(npair):
                        if pi + 1 < npair:
                            scores(pi + 1)
                        S2 = self.PS2[pi % 2][:, :].rearrange("p (a b) -> p a b", a=2)
                        pt = Pt[pi % 2]
                        fw.op("act", lambda e: e.activation(pt[:, :, :n], S2[:, :, :n], AF.Exp, scale=SCALE),
                              writes=[PS[2 * (pi % 2)].b, PS[2 * (pi % 2) + 1].b, pt.b])
                        if pi == 0:
                            fw.op("dve", lambda e: e.tensor_copy(acc[:, :, :n], pt[:, :, :n]), reads=[pt.b], writes=[acc.b])
                        else:
                            fw.op("dve", lambda e: e.tensor_tensor(acc[:, :, :n], acc[:, :, :n], pt[:, :, :n], ALU.add), reads=[pt.b], writes=[acc.b])
                        for jj in range(2):
                            kt = 2 * pi + jj
                            f = lambda e: e.matmul(Ob[:, :n], vb[:, kt, :], pt[:, jj, :n], start=(kt == 0), stop=(kt == nkt - 1))
                            if jj == 1:
                                fw.op("pe", f, reads=[vb.b, pt.b], writes=[Ob.b])
                            else:
                                fw._deps("pe", [vb.b, pt.b], [Ob.b])
                                f(fw.eng["pe"])
                    fw.op("dve", lambda e: e.tensor_tensor(accs[:, :n], acc[:, 0, :n], acc[:, 1, :n], ALU.add), reads=[acc.b], writes=[accs.b])
                    fw.op("pe", lambda e: e.matmul(PS[6][:, :n], self.ones_f[:, :], accs[:, :n], start=True, stop=True),
                          reads=[accs.b, self.ones_f.b], writes=[PS[6].b])
                    fw.op("dve", lambda e: e.reciprocal(rec[:, :n], PS[6][:, :n]), writes=[PS[6].b, rec.b])
                    fw.op("dve", lambda e: e.tensor_tensor(o_[:, :n], Ob[:, :n], rec[:, :n], ALU.mult), reads=[rec.b], writes=[Ob.b, o_.b])
                    fw.dma("pool", self.oT[hh * P:(hh + 1) * P, t0:t0 + n], o_[:, :n], reads=[o_.b])
            fw.barrier()

    def phase_oproj(self, l, w_src, x_src, x_dst, gla_j=None):
        fw, nc, cfg = self.fw, self.nc, self.cfg
        PS = self.PS
        TT = 512
        with contextlib.ExitStack() as st:
            wo = self.sb(st, "wo", [P, DC, D], BF16)
            xs = [self.sb(st, f"ox{i}", [P, DC, TT], F32) for i in range(2)]
            os_ = [self.sb(st, f"oo{i}", [P, DC, TT], BF16) for i in range(2)]
            y = self.sb(st, "oy", [P, DC, TT], F32)
            sq = [self.sb(st, f"osq{i}", [P, TT], BF16) for i in range(2)]
            tmp = [self.sb(st, f"otmp{i}", [P, TT], F32) for i in range(2)]
            rstd = self.sb(st, "orstd", [P, TT], F32)
            if gla_j is not None:
                of = [self.sb(st, f"oof{i}", [P, DC, TT], F32) for i in range(2)]
                rs = [self.sb(st, f"ors{i}", [P, DC, TT], BF16) for i in range(2)]
                go = self.sb(st, "ogo", [P, 2], F32)
                fw.dma("sp", go[:, :], self.in_glago[:, gla_j, :], writes=[go.b])
            self.load_w_bf16(wo, w_src.rearrange("(c p) m -> p c m", p=P), 2)
            tiles = tok_tiles(cfg, TT)
            xv_src = x_src.rearrange("(c p) t -> p c t", p=P)
            xv_dst = x_dst.rearrange("(c p) t -> p c t", p=P)

            def load(i):
                t0, n, s = tiles[i]
                fw.dma("sp", xs[i % 2][:, :, :n], xv_src[:, :, t0:t0 + n], writes=[xs[i % 2].b])
                if gla_j is None:
                    fw.dma("sp", os_[i % 2][:, :, :n], self.oT.rearrange("(c p) t -> p c t", p=P)[:, :, t0:t0 + n], writes=[os_[i % 2].b])
                else:
                    fw.dma("sp", of[i % 2][:, :, :n], self.goT.rearrange("(c p) t -> p c t", p=P)[:, :, t0:t0 + n], writes=[of[i % 2].b])
                    fw.dma("sp", rs[i % 2][:, :, :n], self.grT.rearrange("(c p) t -> p c t", p=P)[:, :, t0:t0 + n], writes=[rs[i % 2].b])

            load(0)
            for i, (t0, n, s) in enumerate(tiles):
                if i + 1 < len(tiles):
                    load(i + 1)
                x, o_ = xs[i % 2], os_[i % 2]
                if gla_j is not None:
                    of_, rs_ = of[i % 2], rs[i % 2]
                    for hd in range(GLA_H):
                        for half in range(2):
                            c = hd * 2 + half
                            sqq = sq[half]
                            fw.op("act", lambda e: e.activation(sqq[:, :n], of_[:, c, :n], AF.Square), reads=[of_.b], writes=[sqq.b])
                            fw.op("pe", lambda e: e.matmul(PS[4][:, :n], self.ones_bf[:, :], sqq[:, :n], start=(half == 0), stop=(half == 1)),
                                  reads=[sqq.b], writes=[PS[4].b])
                        self.rstd_from_sq(PS[4], n, 256, rstd)
                        for half in range(2):
                            c = hd * 2 + half
                            t = tmp[half]
                            fw.op("dve", lambda e: e.scalar_tensor_tensor(t[:, :n], of_[:, c, :n], go[:, half:half + 1], rstd[:, :n], ALU.mult, ALU.mult),
                                  reads=[of_.b, go.b, rstd.b], writes=[t.b])
                            fw.op("pool", lambda e: e.tensor_tensor(o_[:, c, :n], t[:, :n], rs_[:, c, :n], ALU.mult), reads=[t.b, rs_.b], writes=[o_.b])

                def mm_chunk(c, pb):
                    self.mm(pb, pb[:, :n], [(wo[:, k, c * P:(c + 1) * P], o_[:, k, :n]) for k in range(DC)], [wo.b, o_.b])

                self.postnorm_residual(n, self.scv(l, 2, s), x, y, PS[5], sq, rstd, tmp, mm_chunk, [PS[2], PS[3]])
                fw.dma("pool", xv_dst[:, :, t0:t0 + n], x[:, :, :n], reads=[x.b])
            fw.barrier()

    def phase_gla1(self, l, j, x_src):
        fw, nc, cfg = self.fw, self.nc, self.cfg
        TT = 512
        PS = self.PS
        with contextlib.ExitStack() as st:
            wq = self.sb(st, "gwq", [P, DC, 512], BF16)
            wk = self.sb(st, "gwk", [P, DC, 512], BF16)
            wv = self.sb(st, "gwv", [P, DC, 1024], BF16)
            wr = self.sb(st, "gwr", [P, DC, 1024], BF16)
            wg1 = self.sb(st, "gwg1", [P, DC, 2, 16], BF16)
            w2a = self.sb(st, "gw2a", [32, 2, 512], BF16)
            xs = [self.sb(st, f"gx{i}", [P, DC, TT], F32) for i in range(2)]
            h = self.sb(st, "gh", [P, DC, TT], BF16)
            sq = [self.sb(st, f"gsq{i}", [P, TT], BF16) for i in range(2)]
            tmp = [self.sb(st, f"gtmp{i}", [P, TT], F32) for i in range(2)]
            rstd = self.sb(st, "grstd", [P, TT], F32)
            low = [self.sb(st, f"glow{i}", [32, TT], BF16) for i in range(2)]
            of32 = [self.sb(st, f"gof{i}", [P, 512], F32) for i in range(4)]
            obf = [self.sb(st, f"gob{i}", [P, 1024], BF16) for i in range(2)]
            robf = [self.sb(st, f"grob{i}", [P, DC, TT], BF16) for i in range(2)]
            ex = [self.sb(st, f"gex{i}", [P, 512], F32) for i in range(2)]
            for (w_, src) in ((wq, self.in_gwq), (wk, self.in_gwk), (wv, self.in_gwv), (wr, self.in_gwr)):
                self.load_w_bf16(w_, src[j].rearrange("(c p) m -> p c m", p=P), 2)
            for d_ in range(2):
                fw.dma("pool", wg1[:, :, d_, :], self.in_gw1[j, d_].rearrange("(c p) m -> p c m", p=P), writes=[wg1.b])
                fw.dma("pool", w2a[0:16, d_, :], self.in_gw2[j, d_], writes=[w2a.b])
                fw.dma("pool", w2a[16:17, d_, :], self.in_gbg[j, d_:d_ + 1, :], writes=[w2a.b])
                fw.op("dve", lambda e: e.memset(low[d_][:, :], 1.0), writes=[low[d_].b])
            tiles = tok_tiles(cfg, TT)
            xv = x_src.rearrange("(c p) t -> p c t", p=P)

            def load(i):
                t0, n, s = tiles[i]
                fw.dma("sp", xs[i % 2][:, :, :n], xv[:, :, t0:t0 + n], writes=[xs[i % 2].b])

            load(0)
            nf = 0
            nb = 0
            gout = [self.ggA, self.ggB]
            for i, (t0, n, s) in enumerate(tiles):
                if i + 1 < len(tiles):
                    load(i + 1)
                x = xs[i % 2]
                self.prenorm(x, n, self.scv(l, 0, s), self.scv(l, 1, s), h, PS[4], sq, rstd, tmp)
                for (w_, dst, scl) in ((wq, self.gqT, 128.0 ** -0.5), (wk, self.gkT, 1.0)):
                    for hd in range(GLA_H):
                        pb = PS[hd % 2]
                        self.mm(pb, pb[:, :n], [(w_[:, k, hd * P:(hd + 1) * P], h[:, k, :n]) for k in range(DC)], [w_.b, h.b])
                        o_ = of32[nf % 4]; nf += 1
                        fw.op("act", lambda e: e.activation(o_[:, :n], pb[:, :n], AF.Copy, scale=scl), writes=[pb.b, o_.b])
                        fw.dma("pool", dst[hd * P:(hd + 1) * P, t0:t0 + n], o_[:, :n], reads=[o_.b])
                ob_ = robf[i % 2]
                for c in range(DC):
                    pb = PS[c % 2]
                    self.mm(pb, pb[:, :n], [(wr[:, k, c * P:(c + 1) * P], h[:, k, :n]) for k in range(DC)], [wr.b, h.b])
                    fw.op("act", lambda e: e.activation(ob_[:, c, :n], pb[:, :n], AF.Silu), writes=[pb.b, ob_.b])
                fw.dma("pool", self.grT.rearrange("(c p) t -> p c t", p=P)[:, :, t0:t0 + n], ob_[:, :, :n], reads=[ob_.b])
                for d_ in range(2):
                    pb = PS[2 + d_]
                    self.mm(pb, pb[0:16, :n], [(wg1[:, k, d_, :], h[:, k, :n]) for k in range(DC)], [wg1.b, h.b])
                    fw.op("dve", lambda e: e.tensor_copy(low[d_][0:16, :n], pb[0:16, :n]), writes=[pb.b, low[d_].b])
                for sidx in range(n // P):
                    tsl = slice(sidx * P, (sidx + 1) * P)
                    r0 = t0 + sidx * P
                    pb = PS[6]
                    self.mm(pb, pb[:, :], [(h[:, k, tsl], wk[:, k, :]) for k in range(DC)], [wk.b, h.b])
                    o_ = of32[nf % 4]; nf += 1
                    fw.op("dve", lambda e: e.tensor_copy(o_[:, :], pb[:, :]), writes=[pb.b, o_.b])
                    fw.dma("pool", self.gk[r0:r0 + P, :], o_[:, :], reads=[o_.b])
                    ob_ = obf[nb % 2]; nb += 1
                    for half in range(2):
                        pb = PS[6 + half]
                        self.mm(pb, pb[:, :], [(h[:, k, tsl], wv[:, k, half * 512:(half + 1) * 512]) for k in range(DC)], [wv.b, h.b])
                        if half == 0:
                            fw.op("act", lambda e: e.activation(ob_[:, 0:512], pb[:, :], AF.Copy), writes=[pb.b, ob_.b])
                        else:
                            fw.op("dve", lambda e: e.tensor_copy(ob_[:, 512:1024], pb[:, :]), writes=[pb.b, ob_.b])
                    fw.dma("pool", self.gv[r0:r0 + P, :], ob_[:, :], reads=[ob_.b])
                    for d_ in range(2):
                        pb = PS[d_]
                        self.mm(pb, pb[:, :], [(low[d_][0:17, tsl], w2a[0:17, d_, :])], [low[d_].b, w2a.b])
                        e_ = ex[d_]
                        fw.op("act", lambda e: e.activation(e_[:, :], pb[:, :], AF.Exp, scale=-1.0), writes=[pb.b, e_.b])
                        o_ = of32[nf % 4]; nf += 1
                        fw.op("act", lambda e: e.activation(o_[:, :], e_[:, :], AF.Ln, bias=1.0, scale=1.0), reads=[e_.b], writes=[o_.b])
                        fw.dma("pool", gout[d_][r0:r0 + P, :], o_[:, :], reads=[o_.b])
            fw.barrier()

    def phase_gla_scan(self, which):
        fw, nc, cfg = self.fw, self.nc, self.cfg
        PS = self.PS
        NTL = cfg.NT // P
        NCT = CTX // P
        A = (which == "A")
        cM, cU, cMask = (1, 2, 3) if A else (4, 5, 6)
        gsrc = self.ggA if A else self.ggB
        with contextlib.ExitStack() as st:
            S = [self.sb(st, f"sS{hd}", [P, 256], F32) for hd in range(GLA_H)]
            Sbf = [[self.sb(st, f"sSbf{i}_{hd}", [P, 256], BF16) for hd in range(GLA_H)] for i in range(2)]
            qT = [self.sb(st, f"sq{i}", [P, GLA_H, P], F32) for i in range(2)]
            kT = [self.sb(st, f"sk{i}", [P, GLA_H, P], F32) for i in range(2)]
            kk = [self.sb(st, f"skk{i}", [P, 512], F32) for i in range(2)]
            vv = [self.sb(st, f"sv{i}", [P, 1024], BF16) for i in range(2)]
            gg = [self.sb(st, f"sg{i}", [P, 512], F32) for i in range(2)]
            oa = [self.sb(st, f"soa{i}", [P, DC, P], F32) for i in range(2)]
            eG = [self.sb(st, f"seG{i}", [P, P], F32) for i in range(4)]
            enG = [self.sb(st, f"senG{i}", [P, P], F32) for i in range(4)]
            eE = [self.sb(st, f"seE{i}", [P, P], F32) for i in range(4)]
            qt = [self.sb(st, f"sqt{i}", [P, P], BF16) for i in range(4)]
            kt = [self.sb(st, f"skt{i}", [P, P], BF16) for i in range(4)]
            ke = [self.sb(st, f"ske{i}", [P, P], BF16) for i in range(4)]
            ab = [self.sb(st, f"sab{i}", [P, P], BF16) for i in range(4)]
            osb = [self.sb(st, f"sos{i}", [P, 2, P], F32) for i in range(4)]
            cst = self.cst
            if A:
                for hd in range(GLA_H):
                    fw.op("dve", lambda e: e.memset(S[hd][:, :], 0.0), writes=[S[hd].b])
            else:
                s0 = self.sb(st, "ss0", [P, GLA_H, 256], F32)
                s1 = self.sb(st, "ss1", [P, GLA_H, 256], F32)
                sel = self.sb(st, "ssel", [P, 2], F32)
                fw.dma("sp", sel[:, :], self.in_sel, writes=[sel.b])
                ev = self.exch_all.rearrange("(r h p) v -> r p h v", r=2, p=P)
                fw.dma("sp", s0[:, :, :], ev[0], writes=[s0.b])
                fw.dma("sp", s1[:, :, :], ev[1], writes=[s1.b])
                for hd in range(GLA_H):
                    fw.op("dve", lambda e: e.tensor_scalar_mul(S[hd][:, :], s0[:, hd, :], sel[:, 0:1]), reads=[s0.b, sel.b], writes=[S[hd].b])
                    fw.op("dve", lambda e: e.scalar_tensor_tensor(S[hd][:, :], s1[:, hd, :], sel[:, 1:2], S[hd][:, :], ALU.mult, ALU.add),
                          reads=[s1.b, sel.b, S[hd].b], writes=[S[hd].b])
            for hd in range(GLA_H):
                fw.op("dve", lambda e: e.tensor_copy(Sbf[0][hd][:, :], S[hd][:, :]), reads=[S[hd].b], writes=[Sbf[0][hd].b])
            sp_ = 0
            if A:
                order = list(range(NTL))
            else:
                order = list(range(NTL - 1, NCT - 1, -1)) + list(range(NCT - 1, -1, -1))
            corder = (0, 1) if A else (1, 0)

            def load(ii):
                t = order[ii]
                r0 = t * P
                b_ = ii % 2
                fw.dma("sp", qT[b_][:, :, :], self.gqT.rearrange("(h p) t -> p h t", p=P)[:, :, r0:r0 + P], writes=[qT[b_].b])
                fw.dma("sp", kT[b_][:, :, :], self.gkT.rearrange("(h p) t -> p h t", p=P)[:, :, r0:r0 + P], writes=[kT[b_].b])
                fw.dma("sp", kk[b_][:, :], self.gk[r0:r0 + P, :], writes=[kk[b_].b])
                fw.dma("sp", vv[b_][:, :], self.gv[r0:r0 + P, :], writes=[vv[b_].b])
                fw.dma("sp", gg[b_][:, :], gsrc[r0:r0 + P, :], writes=[gg[b_].b])
                if not A:
                    fw.dma("sp", oa[b_][:, :, :], self.goT.rearrange("(c p) t -> p c t", p=P)[:, :, r0:r0 + P], writes=[oa[b_].b])

            load(0)
            nit = 0
            nos = 0
            for ii, t in enumerate(order):
                if ii + 1 < len(order):
                    load(ii + 1)
                b_ = ii % 2
                r0 = t * P
                if (not A) and t == NCT - 1:
                    sp_ ^= 1
                    for hd in range(GLA_H):
                        fw.op("dve", lambda e: e.memset(S[hd][:, :], 0.0), writes=[S[hd].b])
                        fw.op("dve", lambda e: e.tensor_copy(Sbf[sp_][hd][:, :], S[hd][:, :]), reads=[S[hd].b], writes=[Sbf[sp_][hd].b])
                cur = sp_
                for hd in range(GLA_H):
                    bX, bY = PS[2 * hd], PS[2 * hd + 1]
                    i2 = nit % 4
                    nit += 1
                    hs = slice(hd * P, (hd + 1) * P)
                    g_ = gg[b_]
                    fw.op("pe", lambda e: e.matmul(bX[:, 0:128], g_[:, hs], cst[:, cM, :], start=True, stop=True),
                          reads=[g_.b, cst.b], writes=[bX.b])
                    fw.op("pe", lambda e: e.matmul(bX[:, 128:256], cst[:, cU, :], g_[:, hs], start=True, stop=True),
                          reads=[g_.b, cst.b], writes=[bX.b])
                    fw.op("act", lambda e: e.activation(eG[i2][:, :], bX[:, 0:128], AF.Exp), writes=[bX.b, eG[i2].b])
                    fw.op("act", lambda e: e.activation(enG[i2][:, :], bX[:, 0:128], AF.Exp, scale=-1.0), writes=[bX.b, enG[i2].b])
                    fw.op("act", lambda e: e.activation(eE[i2][:, :], bX[:, 128:256], AF.Exp), writes=[bX.b, eE[i2].b])
                    fw.op("dve", lambda e: e.tensor_tensor(qt[i2][:, :], qT[b_][:, hd, :], eG[i2][:, :], ALU.mult),
                          reads=[qT[b_].b, eG[i2].b], writes=[qt[i2].b])
                    fw.op("dve", lambda e: e.tensor_tensor(kt[i2][:, :], kT[b_][:, hd, :], enG[i2][:, :], ALU.mult),
                          reads=[kT[b_].b, enG[i2].b], writes=[kt[i2].b])
                    fw.op("dve", lambda e: e.tensor_tensor(ke[i2][:, :], kk[b_][:, hs], eE[i2][:, :], ALU.mult),
                          reads=[kk[b_].b, eE[i2].b], writes=[ke[i2].b])
                    fw.op("pe", lambda e: e.matmul(bX[:, 256:384], kt[i2][:, :], qt[i2][:, :], start=True, stop=True),
                          reads=[kt[i2].b, qt[i2].b], writes=[bX.b])
                    fw.op("dve", lambda e: e.tensor_tensor(ab[i2][:, :], bX[:, 256:384], cst[:, cMask, :], ALU.mult),
                          reads=[cst.b], writes=[bX.b, ab[i2].b])
                    c0, c1 = corder
                    v_ = vv[b_]

                    def upd(c, src_par, dst_par):
                        cs_ = slice(c * 64, (c + 1) * 64)
                        fw.op("pe", lambda e: e.matmul(bY[:, 256:512], ke[i2][cs_, :], v_[cs_, hd * 256:(hd + 1) * 256], start=True, stop=True),
                              reads=[ke[i2].b, v_.b], writes=[bY.b])
                        col = c * 64 + (63 if A else 0)
                        fw.op("dve", lambda e: e.scalar_tensor_tensor(S[hd][:, :], S[hd][:, :], eG[i2][:, col:col + 1], bY[:, 256:512], ALU.mult, ALU.add),
                              reads=[eG[i2].b], writes=[S[hd].b, bY.b])
                        fw.op("act", lambda e: e.activation(Sbf[dst_par][hd][:, :], S[hd][:, :], AF.Copy), reads=[S[hd].b], writes=[Sbf[dst_par][hd].b])

                    upd(c0, cur, cur ^ 1)
                    for half in range(2):
                        po = bY[:, half * P:(half + 1) * P]
                        vs = slice(hd * 256 + half * P, hd * 256 + (half + 1) * P)
                        ss = slice(half * P, (half + 1) * P)
                        fw._deps("pe", [v_.b, ab[i2].b, Sbf[0][hd].b, Sbf[1][hd].b, qt[i2].b], [bY.b])
                        fw.eng["pe"].matmul(po, v_[:, vs], ab[i2][:, :], start=True, stop=False)
                        fw.eng["pe"].matmul(po[:, c0 * 64:(c0 + 1) * 64], Sbf[cur][hd][:, ss], qt[i2][:, c0 * 64:(c0 + 1) * 64], start=False, stop=False)
                        fw.op("pe", lambda e: e.matmul(po[:, c1 * 64:(c1 + 1) * 64], Sbf[cur ^ 1][hd][:, ss], qt[i2][:, c1 * 64:(c1 + 1) * 64], start=False, stop=True),
                              reads=[v_.b, ab[i2].b, Sbf[0][hd].b, Sbf[1][hd].b, qt[i2].b], writes=[bY.b])
                    o_ = osb[nos % 4]; nos += 1
                    pov = bY[:, 0:256].rearrange("p (a b) -> p a b", a=2)
                    if A:
                        fw.op("act", lambda e: e.activation(o_[:, :, :], pov, AF.Copy), writes=[bY.b, o_.b])
                    else:
                        fw.op("dve", lambda e: e.tensor_tensor(o_[:, :, :], pov, oa[b_][:, 2 * hd:2 * hd + 2, :], ALU.add),
                              reads=[oa[b_].b], writes=[bY.b, o_.b])
                    fw.dma("pool", self.goT[hd * 256:(hd + 1) * 256, r0:r0 + P].rearrange("(a p) t -> p a t", p=P), o_[:, :, :], reads=[o_.b])
                    upd(c1, cur ^ 1, cur)
            if A:
                for hd in range(GLA_H):
                    fw.dma("pool", self.exch_own[hd * P:(hd + 1) * P, :], S[hd][:, :], reads=[S[hd].b])
            fw.barrier()

    def declare_common(self):
        cfg, nc = self.cfg, self.nc
        L = cfg.DEPTH
        self.in_xT = self.dram_in("xT", [D, cfg.NT])
        self.in_cT = self.dram_in("cT", [P, DC, 2])
        self.in_adab = self.dram_in("adabT", [P, L, 48])
        self.in_ng = self.dram_in("ngT", [P, L, 4, DC])
        self.in_adaw = self.dram_in("ada_w", [L, D, 6 * D])
        self.in_w1 = self.dram_in("mlp_w1", [L, D, DFF])
        self.in_w2 = self.dram_in("mlp_w2", [L, DFF, D])
        self.in_consts = self.dram_in("consts", [P, 8, P])
        self.out_T = nc.dram_tensor("outT", [D, cfg.NT], F32, kind="ExternalOutput").ap()
        self.xA = self.dram("xA", [D, cfg.NT], F32)
        self.xB = self.dram("xB", [D, cfg.NT], F32)

    def declare_all(self):
        cfg, nc = self.cfg, self.nc
        self.declare_common()
        NT = cfg.NT
        na = (cfg.DEPTH + 1) // 2
        nb = max(cfg.DEPTH // 2, 1)
        self.in_rope = self.dram_in("ropeT", [P, NT])
        self.in_sel = self.dram_in("sel", [P, 2])
        self.in_wdq = self.dram_in("mla_w_dq", [na, D, 384])
        self.in_wuq = self.dram_in("mla_w_uq", [na, 384, 1536])
        self.in_wdkv = self.dram_in("mla_w_dkv", [na, D, 320])
        self.in_wukv = self.dram_in("mla_w_ukv", [na, 256, 2048])
        self.in_wo_mla = self.dram_in("mla_w_o", [na, D, D])
        self.in_mlag = self.dram_in("mla_gT", [P, na, 5])
        self.in_gwq = self.dram_in("gla_w_q", [nb, D, 512])
        self.in_gwk = self.dram_in("gla_w_k", [nb, D, 512])
        self.in_gwv = self.dram_in("gla_w_v", [nb, D, D])
        self.in_gwr = self.dram_in("gla_w_r", [nb, D, D])
        self.in_gw1 = self.dram_in("gla_w_gate1", [nb, 2, D, 16])
        self.in_gw2 = self.dram_in("gla_w_gate2", [nb, 2, 16, 512])
        self.in_gbg = self.dram_in("gla_b_gate", [nb, 2, 512])
        self.in_glago = self.dram_in("gla_goT", [P, nb, 2])
        self.in_wo_gla = self.dram_in("gla_w_o", [nb, D, D])
        self.KnT_own = self.dram("KnT_own", [1024, NT], BF16)
        self.KrT_own = self.dram("KrT_own", [P, NT], BF16)
        self.V_own = self.dram("V_own", [NT, 1024], BF16)
        self.Kn_all = self.dram("Kn_all", [16, P, NT], BF16)
        self.Kr_all = self.dram("Kr_all", [2, P, NT], BF16)
        self.V_all = self.dram("V_all", [NT // 256, 512, 1024], BF16)
        self.qnT = self.dram("qnT", [1024, NT], BF16)
        self.qrT = self.dram("qrT", [1024, NT], BF16)
        self.oT = self.dram("oT", [1024, NT], BF16)
        self.gqT = self.dram("gqT", [512, NT], F32)
        self.gkT = self.dram("gkT", [512, NT], F32)
        self.gk = self.dram("gk", [NT, 512], F32)
        self.gv = self.dram("gv", [NT, 1024], BF16)
        self.ggA = self.dram("ggA", [NT, 512], F32)
        self.ggB = self.dram("ggB", [NT, 512], F32)
        self.grT = self.dram("grT", [1024, NT], BF16)
        self.goT = self.dram("goT", [1024, NT], F32)
        self.exch_own = self.dram("exch_own", [512, 256], F32)
        self.exch_all = self.dram("exch_all", [1024, 256], F32)

    def build(self):
        cfg = self.cfg
        self.declare_all()
        self.setup_globals()
        fw = self.fw
        self.phase_ada()
        x_cur = self.in_xT
        for l in range(cfg.DEPTH):
            j = l // 2
            if l % 2 == 0:
                self.phase_mla1(l, j, x_cur)
                pairs = [(self.KnT_own[c * 64:(c + 1) * 64, :], self.Kn_all[c]) for c in range(16)]
                pairs += [(self.KrT_own[c * 64:(c + 1) * 64, :], self.Kr_all[c]) for c in range(2)]
                pairs += [(self.V_own[c * 256:(c + 1) * 256, :], self.V_all[c]) for c in range(cfg.NT // 256)]
                fw.allgather_many(pairs)
                self.phase_mla2()
                self.phase_oproj(l, self.in_wo_mla[j], x_cur, self.xA)
            else:
                self.phase_gla1(l, j, x_cur)
                self.phase_gla_scan("A")
                fw.allgather_many([(self.exch_own, self.exch_all)])
                self.phase_gla_scan("B")
                self.phase_oproj(l, self.in_wo_gla[j], x_cur, self.xA, gla_j=j)
            dst = self.out_T if l == cfg.DEPTH - 1 else self.xB
            self.phase_mlp(l, self.xA, dst)
            x_cur = self.xB
        fw.barrier()
        self.stack.close()
        return self.nc

    def setup_globals(self):
        st, nc, cfg = self.stack, self.nc, self.cfg
        self.fw = FW(nc, st)
        fw = self.fw
        self.PS2 = [st.enter_context(nc.psum_tensor(f"ps{i}", [P, 1024], F32)) for i in range(4)]
        self.PS = [T(self.PS2[i // 2][:, (i % 2) * 512:(i % 2 + 1) * 512], f"psb{i}") for i in range(8)]
        self.sc = self.sb(st, "sc", [P, cfg.DEPTH, 6, DC, 2], F32)
        self.cst = self.sb(st, "cst", [P, 8, P], F32)
        self.ones_bf = self.sb(st, "ones_bf", [P, P], BF16)
        self.ones_f = self.sb(st, "ones_f", [P, P], F32)
        fw.dma("sp", self.cst[:, :, :], self.in_consts, writes=[self.cst.b])
        fw.op("dve", lambda e: e.memset(self.ones_bf[:, :], 1.0), writes=[self.ones_bf.b])
        fw.op("dve", lambda e: e.memset(self.ones_f[:, :], 1.0), writes=[self.ones_f.b])

    def build_mlp_only(self):
        self.declare_common()
        self.setup_globals()
        self.phase_ada()
        self.phase_mlp(0, self.in_xT, self.out_T)
        self.fw.barrier()
        self.stack.close()
        return self.nc


def make_consts():
    c = np.zeros((P, 8, P), np.float32)
    j = np.arange(P)[:, None]
    i = np.arange(P)[None, :]
    same = (j // 64) == (i // 64)
    c[:, 0, :] = ((j % 64) == (i % 64)).astype(np.float32)
    c[:, 1, :] = np.where(same & (j <= i), -1.0 / 16, 0.0)
    c[:, 2, :] = np.where(same & (j > i), -1.0 / 16, 0.0)
    c[:, 3, :] = np.where(same & (j <= i), 1.0, 0.0)
    c[:, 4, :] = np.where(same & (j >= i), -1.0 / 16, 0.0)
    c[:, 5, :] = np.where(same & (j < i), -1.0 / 16, 0.0)
    c[:, 6, :] = np.where(same & (j >= i), 1.0, 0.0)
    return c


def rope_table(pos):
    half = 32
    inv = (ROPE_BASE ** (-np.arange(0, half, 2, dtype=np.float32) / np.float32(half))).astype(np.float32)
    row = (pos // GRID_W).astype(np.float32)
    col = (pos % GRID_W).astype(np.float32)
    ang_r = (row[:, None] * inv[None, :]).astype(np.float32)
    ang_c = (col[:, None] * inv[None, :]).astype(np.float32)
    ang = np.concatenate([ang_r, ang_r, ang_c, ang_c], axis=1)
    tab = np.zeros((P, CTX + len(pos)), np.float32)
    tab[0:64, :CTX] = 1.0
    tab[0:64, CTX:] = np.cos(ang).T
    tab[64:128, CTX:] = np.sin(ang).T
    return tab


_PROG_CACHE = {}


def make_in_maps(cfg, inputs, n_batch):
    NL, L = cfg.NL, cfg.DEPTH
    f32 = lambda a: np.ascontiguousarray(np.asarray(a, dtype=np.float32))
    x, c, ctx, c_ctx = (np.asarray(inputs[k]) for k in ("x", "c", "ctx", "c_ctx"))
    shared = {
        "adabT": f32(np.asarray(inputs["ada_b"])[:L].reshape(L, 48, P).transpose(2, 0, 1)),
        "ngT": f32(np.asarray(inputs["norm_g"])[:L].reshape(L, 4, DC, P).transpose(3, 0, 1, 2)),
        "ada_w": f32(np.asarray(inputs["ada_w"])[:L]),
        "mlp_w1": f32(np.asarray(inputs["mlp_w1"])[:L]),
        "mlp_w2": f32(np.asarray(inputs["mlp_w2"])[:L]),
        "consts": make_consts(),
    }
    na = (L + 1) // 2
    nb = max(L // 2, 1)
    for k in ("mla_w_dq", "mla_w_uq", "mla_w_dkv", "mla_w_ukv", "mla_w_o"):
        shared[k] = f32(np.asarray(inputs[k])[:na])
    gq = np.asarray(inputs["mla_g_q"])[:na].reshape(na, 3, P)
    gkv = np.asarray(inputs["mla_g_kv"])[:na].reshape(na, 2, P)
    shared["mla_gT"] = f32(np.concatenate([gq, gkv], axis=1).transpose(2, 0, 1))
    for k in ("gla_w_q", "gla_w_k", "gla_w_v", "gla_w_r", "gla_w_o"):
        shared[k] = f32(np.asarray(inputs[k])[:nb])
    shared["gla_goT"] = f32(np.asarray(inputs["gla_g_o"])[:nb].reshape(nb, 2, P).transpose(2, 0, 1))
    in_maps = []
    for b in range(n_batch):
        for r in range(2):
            if r == 0:
                pos = np.arange(0, NL)
                cx = ctx[b]
                dirs = [0, 1]
            else:
                pos = np.arange(2 * NL - 1, NL - 1, -1)
                cx = ctx[b][::-1]
                dirs = [1, 0]
            xt = np.concatenate([cx, x[b][pos]], axis=0)
            m = dict(shared)
            m["xT"] = f32(xt.T)
            m["cT"] = f32(np.stack([c[b].reshape(DC, P).T, c_ctx.reshape(DC, P).T], -1))
            m["ropeT"] = rope_table(pos)
            sel = np.zeros((P, 2), np.float32)
            sel[:, 1 - r] = 1.0
            m["sel"] = sel
            m["gla_w_gate1"] = f32(np.asarray(inputs["gla_w_gate1"])[:nb][:, dirs])
            m["gla_w_gate2"] = f32(np.asarray(inputs["gla_w_gate2"])[:nb][:, dirs])
            m["gla_b_gate"] = f32(np.asarray(inputs["gla_b_gate"])[:nb][:, dirs])
            in_maps.append(m)
    return in_maps


def assemble(cfg, outs, n_batch):
    NL = cfg.NL
    out = np.zeros((n_batch, 2 * NL, D), np.float32)
    for b in range(n_batch):
        out[b, :NL] = np.asarray(outs[2 * b])[:, CTX:].T
        out[b, NL:] = np.asarray(outs[2 * b + 1])[:, CTX:].T[::-1]
    return out


def run_cores(cfg, inputs, n_batch):
    in_maps = make_in_maps(cfg, inputs, n_batch)
    nc = Prog(cfg).build()
    res = run_bass_kernel_spmd(nc, in_maps, core_ids=list(range(2 * n_batch)))
    return assemble(cfg, [r["outT"] for r in res.results], n_batch)


def kernel(**inputs):
    cfg = Cfg(NL=4096, DEPTH=4)
    return run_cores(cfg, inputs, 4)
```
